# Optimizing a Trainium2 kernel written in Bass

```python
import jax
import jax.numpy as jnp
from jax import lax
import numpy as np

D_MODEL = 1024
BATCH = 2
SEQ = 16384
DEPTH = 4
DEC_BATCH = 16
DEC_SEQ = 64
PAST_LEN = 2048

CHUNK = 64
N_MIXERS = 4
N_A = (DEPTH + 3) // N_MIXERS
N_B = (DEPTH + 2) // N_MIXERS
N_C = (DEPTH + 1) // N_MIXERS
N_D = DEPTH // N_MIXERS
EPS = 1e-6
HG_DK = 128
HG_HEADS = D_MODEL // HG_DK
HG_DV = D_MODEL // HG_HEADS
HG_KEY = HG_HEADS * HG_DK
RW_N = 64
RW_HEADS = D_MODEL // RW_N
RW_DECAY_LORA = 64
RW_A_LORA = 64
RW_GATE_LORA = 128
RW_DECAY_SCALE = float(np.exp(-0.5))
RW_GN_EPS = 64e-5
LRU_W = D_MODEL
LRU_BLOCKS = 8
LRU_BW = LRU_W // LRU_BLOCKS
LRU_CONV = 4
LRU_C = 8.0
CONF_K = 31
D_FF = ((8 * D_MODEL // 3 + 255) // 256) * 256

kernel_name = 'hybrid_streaming_encoder_step'


def _rmsnorm(x, g):
    xf = x.astype(jnp.float32)
    y = xf * lax.rsqrt(jnp.mean(xf * xf, axis=-1, keepdims=True) + EPS)
    return (y * g.astype(jnp.float32)).astype(x.dtype)


def _layernorm(x, g, b):
    xf = x.astype(jnp.float32)
    mu = jnp.mean(xf, axis=-1, keepdims=True)
    xc = xf - mu
    var = jnp.mean(xc * xc, axis=-1, keepdims=True)
    return (xc * lax.rsqrt(var + EPS) * g.astype(jnp.float32) + b.astype(jnp.float32)).astype(x.dtype)


def _causal_dwconv(x, buf, w, b):
    xp = jnp.concatenate([buf.astype(x.dtype), x], axis=1)
    y = lax.conv_general_dilated(xp, w[:, None, :].astype(x.dtype), window_strides=(1,),
                                 padding='VALID', dimension_numbers=('NWC', 'WIO', 'NWC'),
                                 feature_group_count=x.shape[-1])
    return y + b, xp[:, xp.shape[1] - (w.shape[0] - 1):]


def _gla_chunkwise(q, k, v, logf, s0):
    b_, seq_len, n_h, _ = q.shape
    n_blk = -(-seq_len // CHUNK)
    pad = n_blk * CHUNK - seq_len

    def blocks(t):
        t = jnp.pad(t, ((0, 0), (0, pad), (0, 0), (0, 0)))
        return t.reshape(b_, n_blk, CHUNK, n_h, t.shape[-1]).transpose(1, 0, 3, 2, 4)

    mask = jnp.tril(jnp.ones((CHUNK, CHUNK), bool))[:, :, None]

    def step(S, blk):
        qc, kc, vc, gc = blk
        cum = jnp.cumsum(gc, axis=2)
        rel = cum[:, :, :, None, :] - cum[:, :, None, :, :]
        dec = jnp.exp(jnp.where(mask, rel, -jnp.inf))
        scores = jnp.einsum('bhtd,bhtsd,bhsd->bhts', qc, dec, kc)
        o = (jnp.einsum('bhts,bhsv->bhtv', scores, vc)
             + jnp.einsum('bhtd,bhdv->bhtv', qc * jnp.exp(cum), S))
        last = cum[:, :, -1:, :]
        S = (jnp.exp(last[:, :, 0, :, None]) * S
             + jnp.einsum('bhsd,bhsv->bhdv', kc * jnp.exp(last - cum), vc))
        return S, o

    S, o = lax.scan(step, s0, (blocks(q), blocks(k), blocks(v), blocks(logf)))
    o = o.transpose(1, 0, 3, 2, 4).reshape(b_, n_blk * CHUNK, n_h, v.shape[-1])[:, :seq_len]
    return o, S


def _hgrn2(x, s0, lb, wq, wf, wi, wg, gn, wo):
    bsz, seq_len, _ = x.shape
    q = jax.nn.silu(x @ wq).astype(jnp.float32).reshape(bsz, seq_len, HG_HEADS, HG_DK) * (HG_DK ** -0.5)
    fl = (x @ wf).astype(jnp.float32)
    f = lb + (1.0 - lb) * jax.nn.sigmoid(fl)
    k = ((1.0 - lb) * jax.nn.sigmoid(-fl)).reshape(bsz, seq_len, HG_HEADS, HG_DK)
    logf = jnp.log(f).reshape(bsz, seq_len, HG_HEADS, HG_DK)
    v = (x @ wi).astype(jnp.float32).reshape(bsz, seq_len, HG_HEADS, HG_DV)
    o, S = _gla_chunkwise(q, k, v, logf, s0.astype(jnp.float32))
    o = o * lax.rsqrt(jnp.mean(o * o, axis=-1, keepdims=True) + EPS) * gn.astype(jnp.float32)
    o = o.reshape(bsz, seq_len, D_MODEL).astype(x.dtype) * jax.nn.silu(x @ wg)
    return o @ wo, S.astype(x.dtype)


def _rwkv7(x, s0, prev, mu, wr, wk, wv, w0, w1, w2, a0, a1, a2, g1, g2, k_k, k_a, r_k, ln_g, ln_b, wo):
    bsz, seq_len, _ = x.shape
    xprev = jnp.concatenate([prev[:, None].astype(x.dtype), x[:, :-1]], axis=1)
    xx = xprev - x
    xr, xw, xk, xv, xa, xg = [x + xx * mu[n] for n in range(6)]
    r = xr @ wr
    k = xk @ wk
    v = xv @ wv
    logw = -RW_DECAY_SCALE * jax.nn.sigmoid((w0 + jnp.tanh(xw @ w1) @ w2).astype(jnp.float32))
    a = jax.nn.sigmoid((a0 + (xa @ a1) @ a2).astype(jnp.float32))
    g = jax.nn.sigmoid(xg @ g1) @ g2

    def heads(t):
        return t.astype(jnp.float32).reshape(bsz, seq_len, RW_HEADS, RW_N)

    kk = heads(k * k_k)
    kk = kk / jnp.maximum(jnp.sqrt(jnp.sum(kk * kk, axis=-1, keepdims=True)), 1e-12)
    kh = heads(k.astype(jnp.float32) * (1.0 + (a - 1.0) * k_a))
    rh, vh, ah, wh = heads(r), heads(v), heads(a), jnp.exp(heads(logw))

    def step(S, inp):
        r_t, w_t, k_t, v_t, kk_t, a_t = inp
        sk = jnp.einsum('bhvk,bhk->bhv', S, kk_t)
        S = (S * w_t[:, :, None, :] - sk[..., None] * (kk_t * a_t)[:, :, None, :]
             + v_t[..., None] * k_t[:, :, None, :])
        return S, jnp.einsum('bhvk,bhk->bhv', S, r_t)

    seq = lambda t: jnp.moveaxis(t, 1, 0)
    S, o = lax.scan(step, s0.astype(jnp.float32), (seq(rh), seq(wh), seq(kh), seq(vh), seq(kk), seq(ah)))
    o = jnp.moveaxis(o, 0, 1)
    mean = jnp.mean(o, axis=-1, keepdims=True)
    oc = o - mean
    o = oc * lax.rsqrt(jnp.mean(oc * oc, axis=-1, keepdims=True) + RW_GN_EPS)
    o = o.reshape(bsz, seq_len, D_MODEL) * ln_g + ln_b
    bonus = jnp.sum(rh * kh * r_k, axis=-1, keepdims=True) * vh
    o = (o + bonus.reshape(bsz, seq_len, D_MODEL)).astype(x.dtype) * g
    return o @ wo, S.astype(x.dtype), x[:, -1]


def _lru_combine(left, right):
    return left[0] * right[0], right[0] * left[1] + right[1]


def _rglru(x, h0, buf, wy, wx, conv_w, conv_b, ga_w, ga_b, gx_w, gx_b, lam, wo):
    bsz, seq_len, _ = x.shape
    y = jax.nn.gelu(x @ wy)
    u, new_buf = _causal_dwconv(x @ wx, buf, conv_w, conv_b)
    ub = u.reshape(bsz, seq_len, LRU_BLOCKS, LRU_BW)

    def gate(w, b):
        z = jnp.einsum('blhi,hij->blhj', ub, w).reshape(bsz, seq_len, LRU_W) + b
        return jax.nn.sigmoid(z.astype(jnp.float32))

    r = gate(ga_w, ga_b)
    i = gate(gx_w, gx_b)
    log_a = -LRU_C * r * jax.nn.softplus(-lam.astype(jnp.float32))
    a = jnp.exp(log_a)
    bterm = jnp.sqrt(-jnp.expm1(2.0 * log_a)) * (i * u.astype(jnp.float32))
    bterm = bterm.at[:, 0].add(a[:, 0] * h0.astype(jnp.float32))
    _, h = lax.associative_scan(_lru_combine, (a, bterm), axis=1)
    out = (h.astype(x.dtype) * y) @ wo
    return out, h[:, -1].astype(x.dtype), new_buf


def _conformer_conv(x, buf, w1, b1, dw_w, dw_b, ln_g, ln_b, w2, b2):
    h = x @ w1 + b1
    u = h[..., :D_MODEL] * jax.nn.sigmoid(h[..., D_MODEL:])
    c, new_buf = _causal_dwconv(u, buf, dw_w, dw_b)
    c = jax.nn.silu(_layernorm(c, ln_g, ln_b))
    return c @ w2 + b2, new_buf


def _swiglu(x, w1, w3, w2):
    return (jax.nn.silu(x @ w1) * (x @ w3)) @ w2


def setup_inputs(seed: int = 0) -> dict:
    key = jax.random.key(seed)
    ks = iter(jax.random.split(key, 96))
    f32 = jnp.float32
    D = D_MODEL

    def nrm(shape, scale=1.0):
        return scale * jax.random.normal(next(ks), shape, f32)

    def unif(shape, lo, hi):
        return jax.random.uniform(next(ks), shape, f32, lo, hi)

    def gain(shape):
        return 1.0 + nrm(shape, 0.05)

    a_base = unif((N_C, LRU_W), 0.9, 0.999) ** (1.0 / LRU_C)
    return {
        'x_prompt': nrm((BATCH, SEQ, D)),
        'x_sample': nrm((DEC_BATCH, DEC_SEQ, D)),
        'state_hgrn': nrm((N_A, DEC_BATCH, HG_HEADS, HG_DK, HG_DV), 0.5),
        'state_rwkv': nrm((N_B, DEC_BATCH, RW_HEADS, RW_N, RW_N), 0.3),
        'state_rwkv_shift': nrm((N_B, DEC_BATCH, D)),
        'state_lru': nrm((N_C, DEC_BATCH, LRU_W), 0.5),
        'state_lru_conv': nrm((N_C, DEC_BATCH, LRU_CONV - 1, LRU_W)),
        'state_conf_conv': nrm((N_D, DEC_BATCH, CONF_K - 1, D), 0.5),
        'norm_mix': gain((DEPTH, D)),
        'norm_ffn': gain((DEPTH, D)),
        'norm_final': gain((D,)),
        'hg_wq': nrm((N_A, D, HG_KEY), D ** -0.5),
        'hg_wf': nrm((N_A, D, HG_KEY), D ** -0.5),
        'hg_wi': nrm((N_A, D, D), D ** -0.5),
        'hg_wg': nrm((N_A, D, D), D ** -0.5),
        'hg_gn': gain((N_A, HG_DV)),
        'hg_wo': nrm((N_A, D, D), D ** -0.5),
        'hg_lb': nrm((N_A + 1, HG_KEY)),
        'rw_mu': unif((N_B, 6, D), 0.0, 1.0),
        'rw_wr': nrm((N_B, D, D), D ** -0.5),
        'rw_wk': nrm((N_B, D, D), D ** -0.5),
        'rw_wv': nrm((N_B, D, D), D ** -0.5),
        'rw_w0': unif((N_B, D), -5.0, 1.0),
        'rw_w1': nrm((N_B, D, RW_DECAY_LORA), D ** -0.5),
        'rw_w2': nrm((N_B, RW_DECAY_LORA, D), 0.5 * RW_DECAY_LORA ** -0.5),
        'rw_a0': nrm((N_B, D), 0.5),
        'rw_a1': nrm((N_B, D, RW_A_LORA), D ** -0.5),
        'rw_a2': nrm((N_B, RW_A_LORA, D), 0.5 * RW_A_LORA ** -0.5),
        'rw_g1': nrm((N_B, D, RW_GATE_LORA), D ** -0.5),
        'rw_g2': nrm((N_B, RW_GATE_LORA, D), RW_GATE_LORA ** -0.5),
        'rw_kk': 0.85 + nrm((N_B, D), 0.05),
        'rw_ka': gain((N_B, D)),
        'rw_rk': nrm((N_B, RW_HEADS, RW_N), 0.1),
        'rw_ln_g': gain((N_B, D)),
        'rw_ln_b': nrm((N_B, D), 0.01),
        'rw_wo': nrm((N_B, D, D), D ** -0.5),
        'lru_wy': nrm((N_C, D, LRU_W), D ** -0.5),
        'lru_wx': nrm((N_C, D, LRU_W), D ** -0.5),
        'lru_conv_w': nrm((N_C, LRU_CONV, LRU_W), LRU_CONV ** -0.5),
        'lru_conv_b': nrm((N_C, LRU_W), 0.01),
        'lru_ga_w': nrm((N_C, LRU_BLOCKS, LRU_BW, LRU_BW), LRU_BW ** -0.5),
        'lru_ga_b': nrm((N_C, LRU_W), 0.01),
        'lru_gx_w': nrm((N_C, LRU_BLOCKS, LRU_BW, LRU_BW), LRU_BW ** -0.5),
        'lru_gx_b': nrm((N_C, LRU_W), 0.01),
        'lru_lam': jnp.log(a_base) - jnp.log1p(-a_base),
        'lru_wo': nrm((N_C, LRU_W, D), LRU_W ** -0.5),
        'cf_w1': nrm((N_D, D, 2 * D), D ** -0.5),
        'cf_b1': nrm((N_D, 2 * D), 0.01),
        'cf_dw_w': nrm((N_D, CONF_K, D), CONF_K ** -0.5),
        'cf_dw_b': nrm((N_D, D), 0.01),
        'cf_ln_g': gain((N_D, D)),
        'cf_ln_b': nrm((N_D, D), 0.01),
        'cf_w2': nrm((N_D, D, D), D ** -0.5),
        'cf_b2': nrm((N_D, D), 0.01),
        'ffn_w1': nrm((DEPTH, D, D_FF), D ** -0.5),
        'ffn_w3': nrm((DEPTH, D, D_FF), D ** -0.5),
        'ffn_w2': nrm((DEPTH, D_FF, D), D_FF ** -0.5),
    }


def reference(x_prompt, x_sample, state_hgrn, state_rwkv, state_rwkv_shift, state_lru, state_lru_conv,
              state_conf_conv, norm_mix, norm_ffn, norm_final, hg_wq, hg_wf, hg_wi, hg_wg, hg_gn, hg_wo,
              hg_lb, rw_mu, rw_wr, rw_wk, rw_wv, rw_w0, rw_w1, rw_w2, rw_a0, rw_a1, rw_a2, rw_g1, rw_g2,
              rw_kk, rw_ka, rw_rk, rw_ln_g, rw_ln_b, rw_wo, lru_wy, lru_wx, lru_conv_w, lru_conv_b,
              lru_ga_w, lru_ga_b, lru_gx_w, lru_gx_b, lru_lam, lru_wo, cf_w1, cf_b1, cf_dw_w, cf_dw_b,
              cf_ln_g, cf_ln_b, cf_w2, cf_b2, ffn_w1, ffn_w3, ffn_w2):
    lb_all = jnp.cumsum(jax.nn.softmax(hg_lb.astype(jnp.float32), axis=0), axis=0)

    def stack(x, s_hg, s_rw, s_sh, s_lh, s_lc, s_cf):
        o_hg, o_rw, o_sh, o_lh, o_lc, o_cf = [], [], [], [], [], []
        for i in range(DEPTH):
            m, j = i % N_MIXERS, i // N_MIXERS
            h = _rmsnorm(x, norm_mix[i])
            if m == 0:
                out, s = _hgrn2(h, s_hg[j], lb_all[j], hg_wq[j], hg_wf[j], hg_wi[j], hg_wg[j], hg_gn[j], hg_wo[j])
                o_hg.append(s)
            elif m == 1:
                out, s, sh = _rwkv7(h, s_rw[j], s_sh[j], rw_mu[j], rw_wr[j], rw_wk[j], rw_wv[j], rw_w0[j],
                                    rw_w1[j], rw_w2[j], rw_a0[j], rw_a1[j], rw_a2[j], rw_g1[j], rw_g2[j],
                                    rw_kk[j], rw_ka[j], rw_rk[j], rw_ln_g[j], rw_ln_b[j], rw_wo[j])
                o_rw.append(s)
                o_sh.append(sh)
            elif m == 2:
                out, hl, cb = _rglru(h, s_lh[j], s_lc[j], lru_wy[j], lru_wx[j], lru_conv_w[j], lru_conv_b[j],
                                     lru_ga_w[j], lru_ga_b[j], lru_gx_w[j], lru_gx_b[j], lru_lam[j], lru_wo[j])
                o_lh.append(hl)
                o_lc.append(cb)
            else:
                out, cb = _conformer_conv(h, s_cf[j], cf_w1[j], cf_b1[j], cf_dw_w[j], cf_dw_b[j],
                                          cf_ln_g[j], cf_ln_b[j], cf_w2[j], cf_b2[j])
                o_cf.append(cb)
            x = x + out
            x = x + _swiglu(_rmsnorm(x, norm_ffn[i]), ffn_w1[i], ffn_w3[i], ffn_w2[i])
        return (_rmsnorm(x, norm_final), jnp.stack(o_hg), jnp.stack(o_rw), jnp.stack(o_sh),
                jnp.stack(o_lh), jnp.stack(o_lc), jnp.stack(o_cf))

    bp, dt = x_prompt.shape[0], x_prompt.dtype
    zeros_like_state = lambda s: jnp.zeros((s.shape[0], bp) + s.shape[2:], dt)
    y_prompt, p_hg, p_rw, p_sh, p_lh, p_lc, p_cf = stack(
        x_prompt, zeros_like_state(state_hgrn), zeros_like_state(state_rwkv),
        zeros_like_state(state_rwkv_shift), zeros_like_state(state_lru),
        zeros_like_state(state_lru_conv), zeros_like_state(state_conf_conv))
    y_sample, s_hg, s_rw, s_sh, s_lh, s_lc, s_cf = stack(
        x_sample, state_hgrn, state_rwkv, state_rwkv_shift, state_lru, state_lru_conv, state_conf_conv)
    return (y_prompt, y_sample, p_hg, p_rw, p_sh, p_lh, p_lc, p_cf, s_hg, s_rw, s_sh, s_lh, s_lc, s_cf)
```

```python
import numpy as np
import concourse.bass as bass
import concourse.mybir as mybir
from concourse.bass_utils import run_bass_kernel_spmd

F32 = mybir.dt.float32
BF16 = mybir.dt.bfloat16
AF = mybir.ActivationFunctionType
ALU = mybir.AluOpType
AX = mybir.AxisListType

D = 1024
NC8 = 8
DFF = 2816
NFF = DFF // 128
EPS = 1e-6
SEQ = 16384
DEC_SEQ = 64
N_CORES = 8
INV_BF16 = True


class Op:
    __slots__ = ("eng", "fn", "reads", "writes", "dma", "ekey", "pos", "deps", "need_inc", "semval", "clock", "rows")

    def __init__(self, eng, fn, reads, writes, dma):
        self.eng = eng
        self.fn = fn
        self.reads = reads
        self.writes = writes
        self.dma = dma
        self.need_inc = False
        self.deps = ()
        self.rows = (0, 128)


def _overlap(a, b):
    for x, y in zip(a, b):
        if x == y:
            continue
        if isinstance(x, str) or isinstance(y, str):
            return True
        return False
    return True


class Prog:
    ENGS = ("pe", "act", "dve", "pool", "sp")
    NSLOT = 8

    def __init__(self):
        self.ops = []
        self.dma_rr = {}

    def op(self, eng, fn, reads=(), writes=(), dma=False, rows=None):
        o = Op(eng, fn, tuple(reads), tuple(writes), dma)
        if rows is not None:
            o.rows = rows
        self.ops.append(o)
        return o

    def analyze(self):
        bufs = {}
        pos = {}
        known = {e: {} for e in self.ENGS}
        last_on_slot = {}
        for o in self.ops:
            if o.dma:
                rr = self.dma_rr.get(o.eng, 0)
                self.dma_rr[o.eng] = rr + 1
                o.ekey = (o.eng, rr % self.NSLOT)
            else:
                o.ekey = o.eng
            pos[o.ekey] = pos.get(o.ekey, 0) + 1
            o.pos = pos[o.ekey]
            deps = {}

            def add(d):
                if d is None:
                    return
                if (not o.dma) and d.ekey == o.ekey and o.eng == "pe":
                    if not (d.rows[1] <= o.rows[0] or o.rows[1] <= d.rows[0]):
                        return
                if deps.get(d.ekey, (0, None))[0] < d.pos:
                    deps[d.ekey] = (d.pos, d)

            if o.dma:
                add(last_on_slot.get(o.ekey))
                last_on_slot[o.ekey] = o
            for r in o.reads:
                for ent in bufs.get(r[0], ()):
                    if _overlap(ent[0], r):
                        add(ent[1])
            for w in o.writes:
                for ent in bufs.get(w[0], ()):
                    if _overlap(ent[0], w):
                        add(ent[1])
                        for rd in ent[2].values():
                            if rd is not o:
                                add(rd)
            for r in o.reads:
                lst = bufs.setdefault(r[0], [])
                for ent in lst:
                    if ent[0] == r:
                        ent[2][o.ekey] = o
                        break
                else:
                    lst.append([r, None, {o.ekey: o}])
            for w in o.writes:
                lst = bufs.setdefault(w[0], [])
                lst[:] = [e for e in lst if not (len(e[0]) >= len(w) and e[0][:len(w)] == w)]
                lst.append([w, o, {}])
            kn = known[o.eng]
            final = []
            for ek, (p, d) in deps.items():
                if kn.get(ek, 0) >= p:
                    continue
                final.append(d)
            for d in final:
                d.need_inc = True
                for ek, p in d.clock.items():
                    if kn.get(ek, 0) < p:
                        kn[ek] = p
            o.deps = final
            ck = dict(kn)
            ck[o.ekey] = o.pos
            o.clock = ck
            if o.dma:
                o.need_inc = True
            else:
                kn_self = kn
        cnt = {}
        for o in self.ops:
            if o.need_inc:
                cnt[o.ekey] = cnt.get(o.ekey, 0) + (16 if o.dma else 1)
                o.semval = cnt[o.ekey]
        return cnt

    def emit(self, nc):
        cnt = self.analyze()
        ekeys = sorted(cnt.keys(), key=str)
        import contextlib
        with contextlib.ExitStack() as es:
            sems = {}
            for ek in ekeys:
                nm = "s_" + (ek if isinstance(ek, str) else f"{ek[0]}{ek[1]}")
                sems[ek] = es.enter_context(nc.semaphore(nm))
            block = es.enter_context(nc.Block())
            per_eng = {e: [o for o in self.ops if o.eng == e] for e in self.ENGS}

            def run(engobj, ename):
                lst = per_eng[ename]
                for o in lst:
                    for d in o.deps:
                        engobj.wait_ge(sems[d.ekey], d.semval)
                    ins = o.fn(engobj)
                    if o.need_inc:
                        ins.then_inc(sems[o.ekey], 16 if o.dma else 1)
                for ek in ekeys:
                    if not isinstance(ek, str) and ek[0] == ename:
                        engobj.wait_ge(sems[ek], cnt[ek])


            @block.tensor
            def _(e):
                run(e, "pe")

            @block.scalar
            def _(e):
                run(e, "act")

            @block.vector
            def _(e):
                run(e, "dve")

            @block.gpsimd
            def _(e):
                run(e, "pool")

            @block.sync
            def _(e):
                run(e, "sp")


def _cols(v):
    v = np.asarray(v, np.float32).reshape(-1, 128)
    return np.ascontiguousarray(v.T)


PVEC_SPEC = [
    ("norm_mix", 32), ("norm_ffn", 32), ("norm_final", 8),
    ("hg_lb", 16), ("hg_gn", 1),
    ("lru_conv_w", 32), ("lru_conv_b", 8), ("lru_ga_b", 8), ("lru_gx_b", 8), ("lru_lam", 8),
    ("rw_mu", 48), ("rw_w0", 8), ("rw_a0", 8), ("rw_kk", 8), ("rw_ka", 8), ("rw_rk", 8), ("rw_ln_g", 8), ("rw_ln_b", 8),
    ("cf_b1", 16), ("cf_dw_w", 248), ("cf_dw_b", 8), ("cf_ln_g", 8), ("cf_ln_b", 8), ("cf_b2", 8),
]


def _rows_to_pcr(a):
    a = np.asarray(a, np.float32)
    r = a.shape[0]
    return np.ascontiguousarray(a.reshape(r, 8, 128).transpose(2, 1, 0))


def _pcr_to_rows(a):
    return np.ascontiguousarray(np.asarray(a).transpose(2, 1, 0).reshape(a.shape[2], 1024))


def _pc_to_vec(a):
    return np.ascontiguousarray(np.asarray(a).T.reshape(1024))


def _rw_in(S):
    S = np.asarray(S, np.float32).reshape(8, 2, 64, 64)
    return np.ascontiguousarray(S.transpose(1, 3, 0, 2).reshape(128, 8, 64))


def _rw_out(Hs):
    Hs = np.asarray(Hs).reshape(2, 64, 8, 64)
    return np.ascontiguousarray(Hs.transpose(2, 0, 3, 1).reshape(16, 64, 64))


def pack_pvec(inp):
    cols = []
    for name, n in PVEC_SPEC:
        a = _cols(np.asarray(inp[name]).reshape(-1))
        assert a.shape[1] == n, (name, a.shape)
        cols.append(a)
    return np.ascontiguousarray(np.concatenate(cols, axis=1))


def pvec_offsets():
    offs = {}
    o = 0
    for name, n in PVEC_SPEC:
        offs[name] = o
        o += n
    return offs, o


CST_SPEC = [("ident", 128), ("m64", 64), ("maska", 256), ("maskb", 256), ("idp", 64), ("bdm", 128)]


def make_cst():
    ident = np.eye(128, dtype=np.float32)
    u64 = np.zeros((128, 64), np.float32)
    m64 = np.zeros((128, 64), np.float32)
    for s in range(64):
        u64[s, :s] = 1.0
        m64[s, s:] = 1.0
    idx = np.arange(128)
    same = (idx[:, None] // 64) == (idx[None, :] // 64)
    msu = (same & (idx[:, None] < idx[None, :])).astype(np.float32)
    msl = msu.T.copy()
    mui = (same & (idx[:, None] <= idx[None, :])).astype(np.float32)
    idp = np.concatenate([np.eye(64, dtype=np.float32)] * 2, axis=0)
    bdm = same.astype(np.float32) / 64.0
    return np.ascontiguousarray(np.concatenate([ident, m64, -msu, -msl, mui, msl, idp, bdm], axis=1))


def cst_offsets():
    offs = {}
    o = 0
    for name, n in CST_SPEC:
        offs[name] = o
        o += n
    return offs, o


WEIGHTS = [
    ("ffn_w1", D, DFF, 4), ("ffn_w3", D, DFF, 4), ("ffn_w2", DFF, D, 4),
    ("hg_wq", D, D, 1), ("hg_wf", D, D, 1), ("hg_wi", D, D, 1), ("hg_wg", D, D, 1), ("hg_wo", D, D, 1),
    ("lru_wy", D, D, 1), ("lru_wx", D, D, 1), ("lru_wo", D, D, 1), ("lru_ga_w", D, 128, 1), ("lru_gx_w", D, 128, 1),
    ("cf_w1", D, 2 * D, 1), ("cf_w2", D, D, 1),
    ("rw_wr", D, D, 1), ("rw_wk", D, D, 1), ("rw_wv", D, D, 1), ("rw_wo", D, D, 1),
    ("rw_w1", D, 64, 1), ("rw_a1", D, 64, 1), ("rw_g1", D, 128, 1),
    ("rw_w2", 64, D, 1), ("rw_a2", 64, D, 1), ("rw_g2", 128, D, 1),
]

NU = 7


class Builder:
    def __init__(self, Lp, Ls=DEC_SEQ, nsamp=2, mixers=(True, True, True, True), depth=4):
        self.Lp, self.Ls, self.nsamp = Lp, Ls, nsamp
        self.mixers = mixers
        self.depth = depth
        self.nc = bass.Bass("TRN2", target_bir_lowering=False)
        self.P = Prog()
        self.poffs, self.npv = pvec_offsets()
        self.coffs, self.ncst = cst_offsets()
        self._bank_rr = 0
        self._wslot_rr = 0
        self._uid = 0

    def dram(self, name, shape, dt, kind):
        return self.nc.dram_tensor(name, list(shape), dt, kind=kind).ap()

    def sb(self, name, shape, dt):
        return self._es.enter_context(self.nc.sbuf_tensor(name, list(shape), dt))

    def build(self):
        import contextlib
        nc = self.nc
        Lp, Ls, ns = self.Lp, self.Ls, self.nsamp
        with contextlib.ExitStack() as es:
            self._es = es
            self.xp = self.dram("xp", [Lp, D], F32, "ExternalInput")
            self.xs = self.dram("xs", [ns * Ls, D], F32, "ExternalInput")
            self.yp = self.dram("yp", [Lp, D], F32, "ExternalOutput")
            self.ys = self.dram("ys", [ns * Ls, D], F32, "ExternalOutput")
            self.pvec_d = self.dram("pvec", [128, self.npv], F32, "ExternalInput")
            self.cst_d = self.dram("cst", [128, self.ncst], F32, "ExternalInput")
            self.st_hg = self.dram("st_hg", [ns, 8, 128, 128], F32, "ExternalInput")
            self.o_hg_p = self.dram("o_hg_p", [1, 8, 128, 128], F32, "ExternalOutput")
            self.o_hg_s = self.dram("o_hg_s", [ns, 8, 128, 128], F32, "ExternalOutput")
            self.st_lh = self.dram("st_lh", [ns, 128, 8], F32, "ExternalInput")
            self.st_lc = self.dram("st_lc", [ns, 128, 8, 3], F32, "ExternalInput")
            self.st_cf = self.dram("st_cf", [ns, 128, 8, 30], F32, "ExternalInput")
            self.o_lh_p = self.dram("o_lh_p", [1, 128, 8], F32, "ExternalOutput")
            self.o_lh_s = self.dram("o_lh_s", [ns, 128, 8], F32, "ExternalOutput")
            self.o_lc_p = self.dram("o_lc_p", [1, 128, 8, 3], F32, "ExternalOutput")
            self.o_lc_s = self.dram("o_lc_s", [ns, 128, 8, 3], F32, "ExternalOutput")
            self.o_cf_p = self.dram("o_cf_p", [1, 128, 8, 30], F32, "ExternalOutput")
            self.o_cf_s = self.dram("o_cf_s", [ns, 128, 8, 30], F32, "ExternalOutput")
            self.st_rw = self.dram("st_rw", [ns, 128, 8, 64], F32, "ExternalInput")
            self.st_sh = self.dram("st_sh", [ns, 128, 8], F32, "ExternalInput")
            self.o_rw_p = self.dram("o_rw_p", [1, 128, 8, 64], F32, "ExternalOutput")
            self.o_rw_s = self.dram("o_rw_s", [ns, 128, 8, 64], F32, "ExternalOutput")
            self.o_sh_p = self.dram("o_sh_p", [1, 128, 8], F32, "ExternalOutput")
            self.o_sh_s = self.dram("o_sh_s", [ns, 128, 8], F32, "ExternalOutput")
            self.dg_d = self.dram("cf_dg_bf", [8, 128, 31 * 128], BF16, "Internal")
            self.w_in = {}
            self.w_bf = {}
            for name, K, N, cnt in WEIGHTS:
                self.w_in[name] = self.dram(name, [cnt, K, N], F32, "ExternalInput")
                self.w_bf[name] = self.dram(name + "_bf", [cnt, K, N], BF16, "Internal")
            self.pvec = self.sb("pvec_sb", [128, self.npv], F32)
            self.cst = self.sb("cst_sb", [128, self.ncst], F32)
            self.ident = self.cst[:, self.coffs["ident"]:self.coffs["ident"] + 128]
            self.ones_bf = self.sb("ones_bf", [128, 128], BF16)
            self.ones128 = self.sb("ones128", [128, 128], BF16)
            self.eps_t = self.sb("eps_t", [128, 1], F32)
            self.one_t = self.sb("one_t", [128, 1], F32)
            self.smask = self.sb("smask", [128, 4096], BF16)
            self.X = self.sb("X", [128, 4096], F32)
            self.RSTD = self.sb("RSTD", [128, 512], F32)
            self.UALL = self.sb("UALL", [128, NU * 4096], F32)
            self.TMP = [self.sb(f"TMP{i}", [128, 512], F32) for i in range(2)]
            self.WS = [self.sb(f"WS{i}", [128, 8192], BF16) for i in range(3)]
            self.LB = self.sb("LB", [128, 8], F32)
            self.OML = self.sb("OML", [128, 8], F32)
            self.SH = self.sb("SH", [128, 8, 128], F32)
            self.GAM = self.sb("GAM", [128, 8, 8], F32)
            self.LH = self.sb("LH", [128, 8], F32)
            self.LC = self.sb("LC", [128, 8, 3], F32)
            self.CH = self.sb("CH", [128, 8, 30], F32)
            self.NSP = self.sb("NSP", [128, 8], F32)
            self.NSP2 = self.sb("NSP2", [128, 8], F32)
            self.SPT = self.sb("SPT", [128, 4, 8], F32)
            self.HST = self.sb("HST", [128, 8, 64], F32)
            self.SHIFT = self.sb("SHIFT", [128, 8], F32)
            self.LOR = self.sb("LOR", [128, 512], BF16)
            self.IDB = self.sb("IDB", [128, 128], BF16)
            self.BD1 = self.sb("BD1", [128, 128], BF16)
            self.gn_eps = self.sb("gn_eps", [128, 1], F32)
            self.banks = [es.enter_context(nc.psum_tensor(f"bank{i}", [128, 512], F32)) for i in range(8)]
            self.program()
            self.P.emit(nc)
        return nc

    def bank(self):
        i = self._bank_rr % 8
        self._bank_rr += 1
        return i

    def tmp(self):
        self._uid += 1
        i = self._uid % 2
        return self.TMP[i], ("TMP", i)

    def pv(self, name, c0, n=1):
        o = self.poffs[name] + c0
        return self.pvec[:, o:o + n]

    def cs(self, name, rows=128):
        o = self.coffs[name]
        n = dict(CST_SPEC)[name]
        return self.cst[0:rows, o:o + n]

    def Uf(self, i):
        return self.UALL[:, i * 4096:(i + 1) * 4096]

    def f3(self, i, T, n=8):
        return self.Uf(i)[:, 0:n * T].rearrange("p (c t) -> p c t", c=n)

    def ff(self, i, T, n=8):
        return self.Uf(i)[:, 0:n * T]

    def Ub(self, i, half):
        return self.UALL[:, i * 4096 + half * 2048:i * 4096 + (half + 1) * 2048].bitcast(BF16)

    def b3(self, i, half, T, n=8):
        return self.Ub(i, half)[:, 0:n * T].rearrange("p (c t) -> p c t", c=n)

    def bf(self, i, half, T, n=8):
        return self.Ub(i, half)[:, 0:n * T]

    def Ubw(self, i):
        return self.UALL[:, i * 4096:(i + 1) * 4096].bitcast(BF16)

    @staticmethod
    def kf(i, c, T):
        return ("U", i, (c * T) // 2048, f"f{T}", c)

    @staticmethod
    def kb(i, half, c, T):
        return ("U", i, half, f"b{T}", c)

    def x3(self, T):
        return self.X[:, 0:8 * T].rearrange("p (c t) -> p c t", c=8)

    @staticmethod
    def kx(c, T):
        return ("X", f"t{T}", c)

    def program(self):
        P = self.P
        P.op("sp", lambda e: e.dma_start(out=self.pvec[:], in_=self.pvec_d[:, :]), writes=[("pvec",)], dma=True)
        P.op("sp", lambda e: e.dma_start(out=self.cst[:], in_=self.cst_d[:, :]), writes=[("cst",)], dma=True)
        P.op("pool", lambda e: e.memset(self.ones_bf[:], 1.0 / D), writes=[("ones",)])
        P.op("pool", lambda e: e.memset(self.ones128[:], 1.0 / 128), writes=[("ones128",)])
        P.op("pool", lambda e: e.memset(self.eps_t[:], EPS), writes=[("eps",)])
        P.op("pool", lambda e: e.memset(self.one_t[:], 1.0), writes=[("eps",)])
        P.op("pool", lambda e: e.memset(self.smask[:], 1.0), writes=[("smask",)])
        sm3 = self.smask[:].rearrange("p (a b) -> p a b", b=64)
        P.op("pool", lambda e: e.memset(sm3[:, :, 0:1], 0.0), reads=[("smask",)], writes=[("smask",)])
        P.op("dve", lambda e: e.tensor_tensor(out=self.LB[:], in0=self.pv("hg_lb", 0, 8), in1=self.pv("hg_lb", 8, 8), op=ALU.subtract),
             reads=[("pvec",)], writes=[("LB",)])
        P.op("act", lambda e: e.activation(out=self.LB[:], in_=self.LB[:], func=AF.Sigmoid), reads=[("LB",)], writes=[("LB",)])
        P.op("dve", lambda e: e.tensor_scalar(out=self.OML[:], in0=self.LB[:], scalar1=-1.0, scalar2=1.0, op0=ALU.mult, op1=ALU.add),
             reads=[("LB",)], writes=[("OML",)])
        P.op("dve", lambda e: e.tensor_copy(out=self.IDB[:], in_=self.ident), reads=[("cst",)], writes=[("IDB",)])
        P.op("dve", lambda e: e.tensor_scalar(out=self.BD1[:], in0=self.cs("bdm"), scalar1=64.0, scalar2=None, op0=ALU.mult), reads=[("cst",)], writes=[("BD1",)])
        P.op("pool", lambda e: e.memset(self.gn_eps[:], 64e-5), writes=[("eps",)])
        lam = self.pv("lru_lam", 0, 8)
        Z, W, W2, ACC = (self.SPT[:, i, :] for i in range(4))
        kS = ("SPT",)
        P.op("dve", lambda e: e.tensor_scalar(out=Z, in0=lam, scalar1=-1.0, scalar2=None, op0=ALU.mult), reads=[("pvec",)], writes=[kS])
        P.op("dve", lambda e: e.tensor_tensor(out=Z, in0=Z, in1=lam, op=ALU.max), reads=[("pvec",), kS], writes=[kS])
        P.op("act", lambda e: e.activation(out=Z, in_=Z, func=AF.Exp, scale=-1.0), reads=[kS], writes=[kS])
        P.op("dve", lambda e: e.tensor_scalar(out=W, in0=Z, scalar1=2.0, scalar2=None, op0=ALU.add), reads=[kS], writes=[kS])
        P.op("dve", lambda e: e.reciprocal(out=W, in_=W), reads=[kS], writes=[kS])
        P.op("dve", lambda e: e.tensor_tensor(out=W, in0=W, in1=Z, op=ALU.mult), reads=[kS], writes=[kS])
        P.op("dve", lambda e: e.tensor_tensor(out=W2, in0=W, in1=W, op=ALU.mult), reads=[kS], writes=[kS])
        P.op("dve", lambda e: e.tensor_scalar(out=ACC, in0=W2, scalar1=1.0 / 11, scalar2=1.0 / 9, op0=ALU.mult, op1=ALU.add), reads=[kS], writes=[kS])
        for cf in (1.0 / 7, 1.0 / 5, 1.0 / 3, 1.0):
            P.op("dve", lambda e: e.tensor_tensor(out=ACC, in0=ACC, in1=W2, op=ALU.mult), reads=[kS], writes=[kS])
            P.op("dve", (lambda e, cf=cf: e.tensor_scalar(out=ACC, in0=ACC, scalar1=float(cf), scalar2=None, op0=ALU.add)), reads=[kS], writes=[kS])
        P.op("dve", lambda e: e.tensor_tensor(out=ACC, in0=ACC, in1=W, op=ALU.mult), reads=[kS], writes=[kS])
        P.op("dve", lambda e: e.tensor_scalar(out=Z, in0=lam, scalar1=-1.0, scalar2=0.0, op0=ALU.mult, op1=ALU.max), reads=[("pvec",), kS], writes=[kS])
        P.op("dve", lambda e: e.scalar_tensor_tensor(out=ACC, in0=ACC, scalar=2.0, in1=Z, op0=ALU.mult, op1=ALU.add), reads=[kS], writes=[kS])
        P.op("dve", lambda e: e.tensor_scalar(out=self.NSP[:], in0=ACC, scalar1=-8.0, scalar2=None, op0=ALU.mult), reads=[kS], writes=[("NSP",)])
        P.op("dve", lambda e: e.tensor_scalar(out=self.NSP2[:], in0=ACC, scalar1=-16.0, scalar2=None, op0=ALU.mult), reads=[kS], writes=[("NSP",)])
        for c in range(8):
            stg = self.Ubw(c % 2)[:, 0:31 * 128].rearrange("p (j m) -> p j m", j=31)
            kst = ("U", c % 2)
            for jj in range(31):
                P.op("dve" if jj % 2 == 0 else "pool", (lambda e, stg=stg, jj=jj, c=c: e.tensor_scalar(out=stg[:, jj, :], in0=self.ident, scalar1=self.pv("cf_dw_w", jj * 8 + c),
                                                                  scalar2=None, op0=ALU.mult)), reads=[("cst",), ("pvec",)], writes=[kst + (0, "dg", jj)])
            P.op("sp", (lambda e, stg=stg, c=c: e.dma_start(out=self.dg_d[c], in_=stg.rearrange("p j m -> p (j m)"))), reads=[kst], writes=[("dg", c)], dma=True)
        for name, K, N, cnt in WEIGHTS:
            for i in range(cnt):
                for r0 in range(0, K, 128):
                    r1 = min(K, r0 + 128)
                    src = self.w_in[name][i, r0:r1, :]
                    dst = self.w_bf[name][i, r0:r1, :]
                    P.op("pool", (lambda e, s=src, d=dst: e.dma_start(out=d, in_=s)),
                         writes=[("wbf", name, i, r0 // 128)], dma=True)
        seqs = [("p", 0, self.Lp)] + [("s", s, self.Ls) for s in range(self.nsamp)]
        for (kind, sid, L) in seqs:
            ntile = (L + 511) // 512
            for ti in range(ntile):
                t0 = ti * 512
                T = min(512, L - t0)
                self.do_tile(kind, sid, t0, T, ti == 0, ti == ntile - 1)

    def load_weight(self, name, idx, k0, nk, n0, ncols):
        slot = self._wslot_rr % 3
        self._wslot_rr += 1
        ws = self.WS[slot]
        view = ws[:, 0:nk * ncols].rearrange("p (k n) -> p k n", k=nk)
        src = self.w_bf[name][idx].rearrange("(k p) n -> p k n", p=128)[:, k0:k0 + nk, n0:n0 + ncols]
        reads = [("wbf", name, idx, k) for k in range(k0, k0 + nk)]
        self.P.op("sp", (lambda e, v=view, s=src: e.dma_start(out=v, in_=s)),
                  reads=reads, writes=[("WS", slot)], dma=True)
        return slot, view

    def mm(self, bank_i, lhsT, rhs, start, stop, reads, out, rows=None):
        self.P.op("pe", (lambda e, o=out, l=lhsT, r=rhs, st=start, sp=stop: e.matmul(o, lhsT=l, rhs=r, start=st, stop=sp)),
                  reads=list(reads), writes=[("bank", bank_i)], rows=rows)

    def proj_fm(self, wname, widx, rhs3, rhs_keyf, T, evac, n0=0, ncols=D, nk=8, k0=0):
        slot, v = self.load_weight(wname, widx, k0, nk, n0, ncols)
        for jj in range(ncols // 128):
            b = self.bank()
            for k in range(nk):
                self.mm(b, v[:, k, jj * 128:(jj + 1) * 128], rhs3[:, k, :], k == 0, k == nk - 1,
                        [("WS", slot), rhs_keyf(k)], self.banks[b][:, 0:T])
            evac(n0 // 128 + jj, b)
        return slot, v

    def rmsnorm(self, gname, gidx, T, dst3, dst_keyf, sq_i=0, sq_half=1):
        P = self.P
        X3 = self.x3(T)
        SQf = self.bf(sq_i, sq_half, T)
        SQ3 = self.b3(sq_i, sq_half, T)
        sqk = ("U", sq_i, sq_half)
        P.op("act", lambda e: e.activation(out=SQf, in_=self.X[:, 0:8 * T], func=AF.Square), reads=[("X",)], writes=[sqk])
        b = self.bank()
        for c in range(NC8):
            self.mm(b, self.ones_bf[:], SQ3[:, c, :], c == 0, c == NC8 - 1, [("ones",), sqk], self.banks[b][:, 0:T])
        t, tk = self.tmp()
        P.op("act", lambda e: e.activation(out=t[:, 0:T], in_=self.banks[b][:, 0:T], func=AF.Sqrt, bias=self.eps_t[:, 0:1], scale=1.0),
             reads=[("bank", b), ("eps",)], writes=[tk])
        P.op("dve", lambda e: e.reciprocal(out=self.RSTD[:, 0:T], in_=t[:, 0:T]), reads=[tk], writes=[("RSTD",)])
        for c in range(NC8):
            P.op("dve", (lambda e, c=c: e.scalar_tensor_tensor(out=dst3[:, c, :], in0=X3[:, c, :], scalar=self.pv(gname, gidx * 8 + c),
                                                               in1=self.RSTD[:, 0:T], op0=ALU.mult, op1=ALU.mult)),
                 reads=[self.kx(c, T), ("RSTD",), ("pvec",)], writes=[dst_keyf(c)])

    def add_to_x(self, T):
        X3 = self.x3(T)

        def evac(j, b):
            self.P.op("dve", (lambda e: e.tensor_tensor(out=X3[:, j, :], in0=X3[:, j, :], in1=self.banks[b][:, 0:T], op=ALU.add)),
                      reads=[("bank", b), self.kx(j, T)], writes=[self.kx(j, T)])
        return evac

    def ffn(self, li, T):
        P = self.P
        H3 = self.b3(0, 0, T)
        hk = lambda c: self.kb(0, 0, c, T)
        self.rmsnorm("norm_ffn", li, T, H3, hk)
        abase = self.UALL[:, 4096:4096 + 2 * 4096].bitcast(BF16)
        A3 = abase[:, 0:NFF * T].rearrange("p (c t) -> p c t", c=NFF)

        def ka(j):
            f0 = j * T // 2
            return ("U", 1 + f0 // 4096, (f0 % 4096) // 2048, f"a{T}", j)
        for (n0, ncols) in [(0, 1024), (1024, 1024), (2048, 768)]:
            s1, v1 = self.load_weight("ffn_w1", li, 0, 8, n0, ncols)
            s3, v3 = self.load_weight("ffn_w3", li, 0, 8, n0, ncols)
            for jj in range(ncols // 128):
                j = n0 // 128 + jj
                b1, b3 = self.bank(), self.bank()
                for k in range(8):
                    self.mm(b1, v1[:, k, jj * 128:(jj + 1) * 128], H3[:, k, :], k == 0, k == 7, [("WS", s1), hk(k)], self.banks[b1][:, 0:T])
                for k in range(8):
                    self.mm(b3, v3[:, k, jj * 128:(jj + 1) * 128], H3[:, k, :], k == 0, k == 7, [("WS", s3), hk(k)], self.banks[b3][:, 0:T])
                t, tk = self.tmp()
                P.op("act", (lambda e, t=t, b1=b1: e.activation(out=t[:, 0:T], in_=self.banks[b1][:, 0:T], func=AF.Silu)),
                     reads=[("bank", b1)], writes=[tk])
                P.op("dve", (lambda e, t=t, b3=b3, j=j: e.tensor_tensor(out=A3[:, j, :], in0=t[:, 0:T], in1=self.banks[b3][:, 0:T], op=ALU.mult)),
                     reads=[tk, ("bank", b3)], writes=[ka(j)])
        ev = self.add_to_x(T)
        for q in range(4):
            self.proj_fm("ffn_w2", li, A3, ka, T, ev, n0=q * 256, ncols=256, nk=NFF)

    def load_x(self, src_rows, T):
        P = self.P
        nb = (T + 127) // 128
        bs = min(128, T)
        XIN = self.Uf(6).rearrange("p (a b) -> p a b", a=4)
        kxin = lambda tb: ("U", 6, tb // 2, "x", tb)
        X3 = self.x3(T)
        for tb in range(nb):
            P.op("sp", (lambda e, tb=tb: e.dma_start(out=XIN[0:bs, tb, :], in_=src_rows[tb * bs:(tb + 1) * bs, :])),
                 writes=[kxin(tb)], dma=True)
        for c in range(NC8):
            b = self.bank()
            for tb in range(nb):
                P.op("pe", (lambda e, b=b, tb=tb, c=c: e.transpose(self.banks[b][:, tb * bs:(tb + 1) * bs],
                                                                  XIN[0:bs, tb, c * 128:(c + 1) * 128], self.ident[0:bs, 0:bs])),
                     reads=[kxin(tb), ("cst",)], writes=[("bank", b)])
            P.op("act", (lambda e, b=b, c=c: e.copy(out=X3[:, c, :], in_=self.banks[b][:, 0:T])),
                 reads=[("bank", b)], writes=[self.kx(c, T)])

    def store_y(self, dst_rows, T):
        P = self.P
        nb = (T + 127) // 128
        bs = min(128, T)
        Y3 = self.f3(5, T)
        ky = lambda c: self.kf(5, c, T)
        self.rmsnorm("norm_final", 0, T, Y3, ky)
        XIN = self.Uf(6).rearrange("p (a b) -> p a b", a=4)
        kxin = lambda tb: ("U", 6, tb // 2, "x", tb)
        for tb in range(nb):
            for half in range(2):
                b = self.bank()
                for cc in range(4):
                    c = half * 4 + cc
                    P.op("pe", (lambda e, b=b, tb=tb, c=c, cc=cc: e.transpose(self.banks[b][0:bs, cc * 128:(cc + 1) * 128],
                                                                             Y3[:, c, tb * bs:(tb + 1) * bs], self.ident[:, :])),
                         reads=[ky(c), ("cst",)], writes=[("bank", b)])
                P.op("act", (lambda e, b=b, tb=tb, half=half: e.copy(out=XIN[0:bs, tb, half * 512:(half + 1) * 512], in_=self.banks[b][0:bs, :])),
                     reads=[("bank", b)], writes=[kxin(tb)])
            P.op("sp", (lambda e, tb=tb: e.dma_start(out=dst_rows[tb * bs:(tb + 1) * bs, :], in_=XIN[0:bs, tb, :])),
                 reads=[kxin(tb)], writes=[("yout", str(dst_rows.name if hasattr(dst_rows, 'name') else 0), tb)], dma=True)

    def do_tile(self, kind, sid, t0, T, first, last):
        if kind == "p":
            src = self.xp[t0:t0 + T, :]
            dst = self.yp[t0:t0 + T, :]
        else:
            src = self.xs[sid * self.Ls:(sid + 1) * self.Ls, :]
            dst = self.ys[sid * self.Ls:(sid + 1) * self.Ls, :]
        self.load_x(src, T)
        for li in range(self.depth):
            m = li % 4
            if self.mixers[li]:
                if m == 0:
                    self.hgrn2(li // 4, li, T, kind, sid, first, last)
                elif m == 1:
                    self.rwkv7(li // 4, li, T, kind, sid, first, last)
                elif m == 2:
                    self.rglru(li // 4, li, T, kind, sid, first, last)
                elif m == 3:
                    self.conformer(li // 4, li, T, kind, sid, first, last)
            self.ffn(li, T)
        self.store_y(dst, T)

    def hgrn2(self, j, li, T, kind, sid, first, last):
        P = self.P
        nch = T // 64
        H3 = self.b3(0, 0, T)
        hk = lambda c: self.kb(0, 0, c, T)
        self.rmsnorm("norm_mix", li, T, H3, hk)
        if first:
            if kind == "p":
                P.op("pool", lambda e: e.memset(self.SH[:], 0.0), writes=[("SH",)])
            else:
                src = self.st_hg[sid].rearrange("h d v -> d h v")
                P.op("sp", lambda e: e.dma_start(out=self.SH[:], in_=src), writes=[("SH",)], dma=True)
        Q3 = self.b3(0, 1, T)
        kq = lambda c: self.kb(0, 1, c, T)
        self.proj_fm("hg_wq", j, H3, hk, T,
                     lambda jc, b: P.op("act", (lambda e: e.activation(out=Q3[:, jc, :], in_=self.banks[b][:, 0:T], func=AF.Silu)),
                                        reads=[("bank", b)], writes=[kq(jc)]))
        F3 = self.f3(1, T)
        kfF = lambda c: self.kf(1, c, T)

        def evac_f(jc, b):
            P.op("act", (lambda e: e.activation(out=F3[:, jc, :], in_=self.banks[b][:, 0:T], func=AF.Sigmoid)),
                 reads=[("bank", b)], writes=[kfF(jc)])
            P.op("dve", (lambda e: e.tensor_scalar(out=F3[:, jc, :], in0=F3[:, jc, :], scalar1=self.OML[:, jc:jc + 1], scalar2=self.LB[:, jc:jc + 1],
                                                   op0=ALU.mult, op1=ALU.add)),
                 reads=[kfF(jc), ("OML",), ("LB",)], writes=[kfF(jc)])
        self.proj_fm("hg_wf", j, H3, hk, T, evac_f)
        G3 = self.b3(2, 0, T)
        kg = lambda c: self.kb(2, 0, c, T)
        self.proj_fm("hg_wg", j, H3, hk, T,
                     lambda jc, b: P.op("act", (lambda e: e.activation(out=G3[:, jc, :], in_=self.banks[b][:, 0:T], func=AF.Silu)),
                                        reads=[("bank", b)], writes=[kg(jc)]))
        Ff = self.ff(1, T)
        K1f = self.bf(2, 1, T)
        Bf = self.ff(3, T)
        B3 = self.f3(3, T)
        Ef = self.ff(4, T)
        Qf = self.bf(0, 1, T)
        QTf = self.bf(5, 0, T)
        QT3 = self.b3(5, 0, T)
        KTf = self.bf(5, 1, T)
        KT3 = self.b3(5, 1, T)
        P.op("dve", lambda e: e.tensor_scalar(out=K1f, in0=Ff, scalar1=-1.0, scalar2=1.0, op0=ALU.mult, op1=ALU.add),
             reads=[("U", 1)], writes=[("U", 2, 1)])
        P.op("act", lambda e: e.activation(out=Ff, in_=Ff, func=AF.Ln), reads=[("U", 1)], writes=[("U", 1)])
        P.op("dve", lambda e: e.tensor_tensor_scan(out=Bf, data0=self.smask[:, 0:8 * T], data1=Ff, initial=0.0, op0=ALU.mult, op1=ALU.add),
             reads=[("U", 1), ("smask",)], writes=[("U", 3)])
        P.op("act", lambda e: e.activation(out=Ef, in_=Bf, func=AF.Exp), reads=[("U", 3)], writes=[("U", 4)])
        P.op("dve", lambda e: e.scalar_tensor_tensor(out=QTf, in0=Qf, scalar=float(128 ** -0.5), in1=Ef, op0=ALU.mult, op1=ALU.mult),
             reads=[("U", 0, 1), ("U", 4)], writes=[("U", 5, 0)])
        B4 = self.Uf(3)[:, 0:8 * T].rearrange("p (c n t) -> p c n t", c=8, t=64)
        P.op("act", lambda e: e.activation(out=self.GAM[:, :, 0:nch], in_=B4[:, :, :, 63], func=AF.Exp), reads=[("U", 3)], writes=[("GAM",)])
        P.op("dve", lambda e: e.tensor_scalar(out=Ef, in0=Bf, scalar1=-1.0, scalar2=80.0, op0=ALU.mult, op1=ALU.min),
             reads=[("U", 3)], writes=[("U", 4)])
        P.op("act", lambda e: e.activation(out=Ef, in_=Ef, func=AF.Exp), reads=[("U", 4)], writes=[("U", 4)])
        P.op("dve", lambda e: e.tensor_tensor(out=Ef, in0=K1f, in1=Ef, op=ALU.mult), reads=[("U", 2, 1), ("U", 4)], writes=[("U", 4)])
        P.op("act", lambda e: e.copy(out=KTf, in_=Ef), reads=[("U", 4)], writes=[("U", 5, 1)])
        E4 = self.Uf(4)[:, 0:8 * T].rearrange("p (c n t) -> p c n t", c=8, t=64)
        gb4 = self.GAM[:, :, 0:nch].unsqueeze(3).to_broadcast([128, 8, nch, 64])
        P.op("dve", lambda e: e.tensor_tensor(out=E4, in0=E4, in1=gb4, op=ALU.mult), reads=[("U", 4), ("GAM",)], writes=[("U", 4)])
        E3 = self.f3(4, T)
        s_i, v_i = self.load_weight("hg_wi", j, 0, 8, 0, D)
        VALL = self.Ubw(1)[0:64, 0:nch * D].rearrange("p (c n) -> p c n", c=nch)
        KHALL = self.Ubw(6)[0:64, 0:nch * D].rearrange("p (c n) -> p c n", c=nch)
        kv = lambda c: ("U", 1, c // 4, "tm", c)
        kkh = lambda c: ("U", 6, c // 4, "tm", c)
        for c in range(nch):
            for half in range(2):
                b = self.bank()
                for k in range(8):
                    self.mm(b, H3[:, k, c * 64:(c + 1) * 64], v_i[:, k, half * 512:(half + 1) * 512], k == 0, k == 7,
                            [("WS", s_i), hk(k)], self.banks[b][0:64, :])
                P.op("act", (lambda e, b=b, c=c, half=half: e.copy(out=VALL[:, c, half * 512:(half + 1) * 512], in_=self.banks[b][0:64, :])),
                     reads=[("bank", b)], writes=[kv(c)])
            for hh in range(2):
                b = self.bank()
                for h4 in range(4):
                    h = hh * 4 + h4
                    P.op("pe", (lambda e, b=b, h=h, h4=h4, c=c: e.transpose(self.banks[b][0:64, h4 * 128:(h4 + 1) * 128],
                                                                           E3[:, h, c * 64:(c + 1) * 64], self.ident[:, :])),
                         reads=[("U", 4), ("cst",)], writes=[("bank", b)])
                P.op("act", (lambda e, b=b, c=c, hh=hh: e.copy(out=KHALL[:, c, hh * 512:(hh + 1) * 512], in_=self.banks[b][0:64, :])),
                     reads=[("bank", b)], writes=[kkh(c)])
        SBF = self.Ubw(3)[:, 0:nch * 1024].rearrange("p (c h v) -> p c h v", c=nch, h=8)
        ksb = lambda c: ("U", 3, c // 4, "sb", c)
        for c in range(nch):
            P.op("act", (lambda e, c=c: e.copy(out=SBF[:, c, :, :], in_=self.SH[:])), reads=[("SH",)], writes=[ksb(c)])
            bb = []
            for hh in range(2):
                b = self.bank()
                bb.append(b)
                for h4 in range(4):
                    h = hh * 4 + h4
                    self.mm(b, KHALL[:, c, h * 128:(h + 1) * 128], VALL[:, c, h * 128:(h + 1) * 128], True, True,
                            [kkh(c), kv(c)], self.banks[b][:, h4 * 128:(h4 + 1) * 128])
            gam_bc = self.GAM[:, :, c:c + 1].to_broadcast([128, 8, 128])
            P.op("dve", (lambda e, g=gam_bc: e.tensor_tensor(out=self.SH[:], in0=self.SH[:], in1=g, op=ALU.mult)),
                 reads=[("SH",), ("GAM",)], writes=[("SH",)])
            for hh in range(2):
                b = bb[hh]
                shv = self.SH[:, hh * 4:(hh + 1) * 4, :]
                P.op("dve", (lambda e, b=b, shv=shv: e.tensor_tensor(out=shv, in0=shv, in1=self.banks[b][:, :].rearrange("p (h v) -> p h v", h=4), op=ALU.add)),
                     reads=[("SH",), ("bank", b)], writes=[("SH",)])
        if last:
            dsto = (self.o_hg_p[0] if kind == "p" else self.o_hg_s[sid]).rearrange("h d v -> d h v")
            P.op("sp", lambda e, dsto=dsto: e.dma_start(out=dsto, in_=self.SH[:]), reads=[("SH",)], writes=[("o_hg", kind, sid)], dma=True)
        AM = self.Ub(2, 1)[0:64, 0:nch * 512].rearrange("p (c h t) -> p c h t", c=nch, h=8)
        kam = lambda c: ("U", 2, 1, "am", c)
        m64 = self.cs("m64", 64)
        O3 = self.f3(6, T)
        O4 = self.Uf(6)[:, 0:8 * T].rearrange("p (h c t) -> p h c t", h=8, t=64)
        for c in range(nch):
            bs_ = self.bank()
            for h in range(8):
                self.mm(bs_, KT3[:, h, c * 64:(c + 1) * 64], QT3[:, h, c * 64:(c + 1) * 64], True, True,
                        [("U", 5, 1), ("U", 5, 0)], self.banks[bs_][0:64, h * 64:(h + 1) * 64])
            mb = m64.unsqueeze(1).to_broadcast([64, 8, 64])
            P.op("dve", (lambda e, c=c, b=bs_, mb=mb: e.tensor_tensor(out=AM[:, c, :, :], in0=self.banks[b][0:64, :].rearrange("p (h t) -> p h t", h=8),
                                                                     in1=mb, op=ALU.mult)),
                 reads=[("bank", bs_), ("cst",)], writes=[kam(c)])
            bo = self.bank()
            for h in range(8):
                o_ap = self.banks[bo][:, h * 64:(h + 1) * 64]
                self.mm(bo, VALL[:, c, h * 128:(h + 1) * 128], AM[:, c, h, :], True, False, [kv(c), kam(c)], o_ap)
                self.mm(bo, SBF[:, c, h, :], QT3[:, h, c * 64:(c + 1) * 64], False, True, [ksb(c), ("U", 5, 0)], o_ap)
            P.op("act", (lambda e, c=c, bo=bo: e.copy(out=O4[:, :, c, :], in_=self.banks[bo][:, :].rearrange("p (h t) -> p h t", h=8))),
                 reads=[("bank", bo)], writes=[("U", 6)])
        Of = self.ff(6, T)
        SQf = self.bf(0, 1, T)
        SQ3 = self.b3(0, 1, T)
        RS3 = self.f3(1, T)
        RSf = self.ff(1, T)
        P.op("act", lambda e: e.activation(out=SQf, in_=Of, func=AF.Square), reads=[("U", 6)], writes=[("U", 0, 1)])
        for h in range(8):
            b = self.bank()
            self.mm(b, self.ones128[:], SQ3[:, h, :], True, True, [("ones128",), ("U", 0, 1)], self.banks[b][:, 0:T])
            P.op("act", (lambda e, b=b, h=h: e.activation(out=RS3[:, h, :], in_=self.banks[b][:, 0:T], func=AF.Sqrt, bias=self.eps_t[:, 0:1], scale=1.0)),
                 reads=[("bank", b), ("eps",)], writes=[self.kf(1, h, T)])
        P.op("dve", lambda e: e.reciprocal(out=RSf, in_=RSf), reads=[("U", 1)], writes=[("U", 1)])
        P.op("dve", lambda e: e.scalar_tensor_tensor(out=Of, in0=Of, scalar=self.pv("hg_gn", j), in1=RSf, op0=ALU.mult, op1=ALU.mult),
             reads=[("U", 6), ("U", 1), ("pvec",)], writes=[("U", 6)])
        ONf = self.bf(5, 0, T)
        ON3 = self.b3(5, 0, T)
        P.op("dve", lambda e: e.tensor_tensor(out=ONf, in0=Of, in1=self.bf(2, 0, T), op=ALU.mult),
             reads=[("U", 6), ("U", 2, 0)], writes=[("U", 5, 0)])
        self.proj_fm("hg_wo", j, ON3, lambda k: ("U", 5, 0), T, self.add_to_x(T))


    def load_weight_rows(self, name, idx, rows, ncols):
        slot = self._wslot_rr % 3
        self._wslot_rr += 1
        view = self.WS[slot][0:rows, 0:ncols]
        src = self.w_bf[name][idx][0:rows, 0:ncols]
        self.P.op("sp", (lambda e, v=view, s=src: e.dma_start(out=v, in_=s)), reads=[("wbf", name, idx, 0)], writes=[("WS", slot)], dma=True)
        return slot, view

    def rwkv7(self, j, li, T, kind, sid, first, last):
        P = self.P
        nch = T // 64
        bs = min(128, T)
        nblk = T // bs
        nchb = bs // 64
        H3 = self.b3(0, 0, T)
        hk = lambda c: self.kb(0, 0, c, T)
        self.rmsnorm("norm_mix", li, T, H3, hk)
        if first:
            if kind == "p":
                P.op("pool", lambda e: e.memset(self.HST[:], 0.0), writes=[("HST",)])
                P.op("pool", lambda e: e.memset(self.SHIFT[:], 0.0), writes=[("SHIFT",)])
            else:
                P.op("sp", lambda e: e.dma_start(out=self.HST[:], in_=self.st_rw[sid]), writes=[("HST",)], dma=True)
                P.op("sp", lambda e: e.dma_start(out=self.SHIFT[:], in_=self.st_sh[sid]), writes=[("SHIFT",)], dma=True)
        XX3 = self.b3(0, 1, T)
        kxx = ("U", 0, 1)
        P.op("dve", lambda e: e.tensor_tensor(out=XX3[:, :, 1:T], in0=H3[:, :, 0:T - 1], in1=H3[:, :, 1:T], op=ALU.subtract),
             reads=[("U", 0, 0)], writes=[kxx])
        P.op("dve", lambda e: e.tensor_tensor(out=XX3[:, :, 0:1], in0=self.SHIFT[:].unsqueeze(2), in1=H3[:, :, 0:1], op=ALU.subtract),
             reads=[("U", 0, 0), ("SHIFT",)], writes=[kxx])
        P.op("act", lambda e: e.copy(out=self.SHIFT[:].unsqueeze(2), in_=H3[:, :, T - 1:T]), reads=[("U", 0, 0)], writes=[("SHIFT",)])
        if last:
            dsto = self.o_sh_p[0] if kind == "p" else self.o_sh_s[sid]
            P.op("sp", lambda e, dsto=dsto: e.dma_start(out=dsto, in_=self.SHIFT[:]), reads=[("SHIFT",)], writes=[("o_sh", kind, sid)], dma=True)
        self._var_rr = 0

        def variant(n):
            half = self._var_rr % 2
            self._var_rr += 1
            V3 = self.b3(1, half, T)
            for c in range(8):
                P.op("dve", (lambda e, c=c: e.scalar_tensor_tensor(out=V3[:, c, :], in0=XX3[:, c, :], scalar=self.pv("rw_mu", n * 8 + c), in1=H3[:, c, :],
                                                                 op0=ALU.mult, op1=ALU.add)),
                     reads=[kxx, hk(c), ("pvec",)], writes=[self.kb(1, half, c, T)])
            return V3, (lambda c, half=half: self.kb(1, half, c, T))

        def evac_copy(dst3, keyf):
            return lambda jc, b: P.op("act", (lambda e: e.copy(out=dst3[:, jc, :], in_=self.banks[b][:, 0:T])), reads=[("bank", b)], writes=[keyf(jc)])
        R3, K3, V3_, A3, G3 = self.b3(2, 0, T), self.b3(2, 1, T), self.b3(3, 0, T), self.b3(3, 1, T), self.b3(5, 0, T)
        LW3 = self.f3(4, T)
        xv, kxv = variant(0)
        self.proj_fm("rw_wr", j, xv, kxv, T, evac_copy(R3, lambda c: self.kb(2, 0, c, T)))
        xv, kxv = variant(2)
        self.proj_fm("rw_wk", j, xv, kxv, T, evac_copy(K3, lambda c: self.kb(2, 1, c, T)))
        xv, kxv = variant(3)
        self.proj_fm("rw_wv", j, xv, kxv, T, evac_copy(V3_, lambda c: self.kb(3, 0, c, T)))

        def lora(xn, w1name, w2name, hid, hid_func, evac2):
            xv, kxv = variant(xn)
            s1, v1 = self.load_weight(w1name, j, 0, 8, 0, hid)
            b = self.bank()
            for k in range(8):
                self.mm(b, v1[:, k, :], xv[:, k, :], k == 0, k == 7, [("WS", s1), kxv(k)], self.banks[b][0:hid, 0:T])
            P.op("act", (lambda e, b=b: e.activation(out=self.LOR[0:hid, 0:T], in_=self.banks[b][0:hid, 0:T], func=hid_func)),
                 reads=[("bank", b)], writes=[("LOR",)])
            s2, v2 = self.load_weight_rows(w2name, j, hid, D)
            for jc in range(8):
                b2 = self.bank()
                self.mm(b2, v2[:, jc * 128:(jc + 1) * 128], self.LOR[0:hid, 0:T], True, True, [("WS", s2), ("LOR",)], self.banks[b2][:, 0:T])
                evac2(jc, b2)
        lora(1, "rw_w1", "rw_w2", 64, AF.Tanh,
             lambda jc, b: P.op("act", (lambda e: e.activation(out=LW3[:, jc, :], in_=self.banks[b][:, 0:T], func=AF.Sigmoid, bias=self.pv("rw_w0", jc), scale=1.0)),
                                reads=[("bank", b), ("pvec",)], writes=[self.kf(4, jc, T)]))
        lora(4, "rw_a1", "rw_a2", 64, AF.Identity,
             lambda jc, b: P.op("act", (lambda e: e.activation(out=A3[:, jc, :], in_=self.banks[b][:, 0:T], func=AF.Sigmoid, bias=self.pv("rw_a0", jc), scale=1.0)),
                                reads=[("bank", b), ("pvec",)], writes=[self.kb(3, 1, jc, T)]))
        lora(5, "rw_g1", "rw_g2", 128, AF.Sigmoid, evac_copy(G3, lambda c: self.kb(5, 0, c, T)))
        LWf, Bf, E0f, N6f = self.ff(4, T), self.ff(6, T), self.ff(0, T), self.ff(6, T)
        N63 = self.f3(6, T)
        KK3, KKf = self.b3(5, 1, T), self.bf(5, 1, T)
        SQ3, SQf = self.b3(1, 0, T), self.bf(1, 0, T)
        KAPf = self.bf(1, 1, T)
        Rf, Kf, Af = self.bf(2, 0, T), self.bf(2, 1, T), self.bf(3, 1, T)
        P.op("dve", lambda e: e.tensor_scalar(out=LWf, in0=LWf, scalar1=-0.6065306597126334, scalar2=None, op0=ALU.mult), reads=[("U", 4)], writes=[("U", 4)])
        for c in range(8):
            P.op("dve", (lambda e, c=c: e.tensor_scalar(out=KK3[:, c, :], in0=K3[:, c, :], scalar1=self.pv("rw_kk", c), scalar2=None, op0=ALU.mult)),
                 reads=[self.kb(2, 1, c, T), ("pvec",)], writes=[self.kb(5, 1, c, T)])
        P.op("act", lambda e: e.activation(out=SQf, in_=KKf, func=AF.Square), reads=[("U", 5, 1)], writes=[("U", 1, 0)])
        for c in range(8):
            b = self.bank()
            self.mm(b, self.BD1[:], SQ3[:, c, :], True, True, [("BD1",), ("U", 1, 0)], self.banks[b][:, 0:T])
            P.op("act", (lambda e, c=c, b=b: e.activation(out=N63[:, c, :], in_=self.banks[b][:, 0:T], func=AF.Sqrt)), reads=[("bank", b)], writes=[self.kf(6, c, T)])
        P.op("dve", lambda e: e.tensor_scalar(out=N6f, in0=N6f, scalar1=1e-12, scalar2=None, op0=ALU.max), reads=[("U", 6)], writes=[("U", 6)])
        P.op("dve", lambda e: e.reciprocal(out=N6f, in_=N6f), reads=[("U", 6)], writes=[("U", 6)])
        P.op("dve", lambda e: e.tensor_tensor(out=KAPf, in0=KKf, in1=N6f, op=ALU.mult), reads=[("U", 5, 1), ("U", 6)], writes=[("U", 1, 1)])
        for c in range(8):
            P.op("dve", (lambda e, c=c: e.tensor_scalar(out=N63[:, c, :], in0=A3[:, c, :], scalar1=-1.0, scalar2=self.pv("rw_ka", c), op0=ALU.add, op1=ALU.mult)),
                 reads=[self.kb(3, 1, c, T), ("pvec",), ("U", 1, 1)], writes=[self.kf(6, c, T)])
        P.op("dve", lambda e: e.scalar_tensor_tensor(out=Kf, in0=N6f, scalar=1.0, in1=Kf, op0=ALU.add, op1=ALU.mult), reads=[("U", 6), ("U", 2, 1)], writes=[("U", 2, 1)])
        P.op("dve", lambda e: e.tensor_tensor(out=Af, in0=KAPf, in1=Af, op=ALU.mult), reads=[("U", 1, 1), ("U", 3, 1), ("U", 6)], writes=[("U", 3, 1)])
        BON3 = self.b3(5, 1, T)
        for c in range(8):
            P.op("dve", (lambda e, c=c: e.scalar_tensor_tensor(out=BON3[:, c, :], in0=R3[:, c, :], scalar=self.pv("rw_rk", c), in1=K3[:, c, :], op0=ALU.mult, op1=ALU.mult)),
                 reads=[self.kb(2, 0, c, T), ("U", 2, 1), ("pvec",), ("U", 1, 1)], writes=[self.kb(5, 1, c, T)])
            b = self.bank()
            self.mm(b, self.BD1[:], BON3[:, c, :], True, True, [("BD1",), self.kb(5, 1, c, T)], self.banks[b][:, 0:T])
            P.op("dve", (lambda e, c=c, b=b: e.tensor_tensor(out=BON3[:, c, :], in0=self.banks[b][:, 0:T], in1=V3_[:, c, :], op=ALU.mult)),
                 reads=[("bank", b), self.kb(3, 0, c, T)], writes=[self.kb(5, 1, c, T)])
        P.op("dve", lambda e: e.tensor_tensor_scan(out=Bf, data0=self.smask[:, 0:8 * T], data1=LWf, initial=0.0, op0=ALU.mult, op1=ALU.add),
             reads=[("U", 4), ("smask",)], writes=[("U", 6)])
        P.op("dve", lambda e: e.tensor_tensor(out=LWf, in0=Bf, in1=LWf, op=ALU.subtract), reads=[("U", 6), ("U", 4)], writes=[("U", 4)])
        B4 = self.Uf(6)[:, 0:8 * T].rearrange("p (c n t) -> p c n t", c=8, t=64)
        P.op("act", lambda e: e.activation(out=self.GAM[:, :, 0:nch], in_=B4[:, :, :, 63], func=AF.Exp), reads=[("U", 6)], writes=[("GAM",)])
        PAIR1 = self.Ubw(1).rearrange("p (a x) -> p a x", a=2)[:, :, 0:8 * T].rearrange("p a (c t) -> p a c t", c=8)
        PAIR2 = self.Ubw(2).rearrange("p (a x) -> p a x", a=2)[:, :, 0:8 * T].rearrange("p a (c t) -> p a c t", c=8)
        P.op("act", lambda e: e.activation(out=LWf, in_=LWf, func=AF.Exp), reads=[("U", 4)], writes=[("U", 4)])
        P.op("dve", lambda e: e.tensor_tensor(out=self.bf(1, 0, T), in0=KAPf, in1=LWf, op=ALU.mult), reads=[("U", 1, 1), ("U", 4)], writes=[("U", 1, 0)])
        P.op("act", lambda e: e.activation(out=E0f, in_=Bf, func=AF.Exp), reads=[("U", 6)], writes=[("U", 0)])
        P.op("dve", lambda e: e.tensor_tensor(out=self.bf(1, 1, T), in0=Rf, in1=E0f, op=ALU.mult), reads=[("U", 2, 0), ("U", 0), ("U", 1, 0), ("U", 5, 1)], writes=[("U", 1, 1)])
        P.op("act", lambda e: e.activation(out=E0f, in_=Bf, func=AF.Exp, scale=-1.0), reads=[("U", 6), ("U", 1, 1)], writes=[("U", 0)])
        P.op("dve", lambda e: e.tensor_tensor(out=self.bf(2, 0, T), in0=Af, in1=E0f, op=ALU.mult), reads=[("U", 3, 1), ("U", 0), ("U", 1, 1)], writes=[("U", 2, 0)])
        P.op("dve", lambda e: e.tensor_tensor(out=Kf, in0=Kf, in1=E0f, op=ALU.mult), reads=[("U", 2, 1), ("U", 0)], writes=[("U", 2, 1)])
        O3 = self.f3(6, T)
        maska = self.cs("maska")[0:bs, :].rearrange("p (a t) -> p a t", a=2)[:, :, 0:bs]
        maskb = self.cs("maskb")[0:bs, :].rearrange("p (a t) -> p a t", a=2)[:, :, 0:bs]
        mui = self.cs("maskb")[0:bs, 0:bs]
        idp = self.cs("idp")
        identb = self.ident[0:bs, 0:bs]
        TMall = self.Ubw(4)

        def gset(g):
            if g == 0:
                bb_ = self.UALL[:, 0:4096].bitcast(BF16)
                bf_ = self.UALL[:, 0:4096]
                kp = ("U", 0)
            else:
                bb_ = self.WS[g - 1][:, :]
                bf_ = self.WS[g - 1][:, :].bitcast(F32)
                kp = ("WS", g - 1)
            S = {}
            kk_ = lambda i: kp + ("rw", i)
            if INV_BF16:
                S["N2"] = [bb_[0:bs, 0:1024].rearrange("p (u a t) -> p u a t", u=4, a=2)[:, :, :, 0:bs],
                           bb_[0:bs, 2048:3072].rearrange("p (u a t) -> p u a t", u=4, a=2)[:, :, :, 0:bs]]
                S["PT"] = bb_[0:bs, 4096:4608].rearrange("p (u t) -> p u t", u=4)[:, :, 0:bs]
                S["PTb"] = S["PT"]
                S["kPTb"] = kk_(2)
            else:
                S["N2"] = [bf_[0:bs, 0:1024].rearrange("p (u a t) -> p u a t", u=4, a=2)[:, :, :, 0:bs],
                           bf_[0:bs, 1024:2048].rearrange("p (u a t) -> p u a t", u=4, a=2)[:, :, :, 0:bs]]
                S["PT"] = bf_[0:bs, 2048:2560].rearrange("p (u t) -> p u t", u=4)[:, :, 0:bs]
                S["PTb"] = bb_[0:bs, 5120:5632].rearrange("p (u t) -> p u t", u=4)[:, :, 0:bs]
                S["kPTb"] = kk_(3)
            S["kN2"] = [kk_(0), kk_(1)]
            S["kPT"] = kk_(2)
            S["DB"] = bb_[0:bs, 5632:6656].rearrange("p (u a t) -> p u a t", u=4, a=2)[:, :, :, 0:bs]
            S["kDB"] = kk_(4)
            S["CMT"] = bb_[0:bs, 6656:7168].rearrange("p (u t) -> p u t", u=4)[:, :, 0:bs]
            S["kCMT"] = kk_(5)
            S["X1b"] = bb_[0:bs, 7168:7680].rearrange("p (u t) -> p u t", u=4)[:, :, 0:bs]
            S["kX1"] = kk_(6)
            S["X2b"] = bb_[0:bs, 7680:7936].rearrange("p (u k) -> p u k", u=4)
            S["kX2"] = kk_(7)
            S["WT"] = bb_[0:bs, 0:512].rearrange("p (u t) -> p u t", u=4)[:, :, 0:bs]
            S["KS"] = bb_[0:bs, 512:768].rearrange("p (u k) -> p u k", u=4)
            S["MT0"] = bb_[:, 768:1280].rearrange("p (a k) -> p a k", k=64)
            S["HSB"] = bb_[:, 1280:1536].rearrange("p (n c v) -> p n c v", n=2, c=2)
            S["kAL"] = kk_(0)
            return S
        sets = [gset(g) for g in range(4)]
        for bl in range(nblk):
            tb = bl * bs
            par = bl % 2
            TM = [TMall[0:bs, par * 4096 + a * 1024:par * 4096 + (a + 1) * 1024].rearrange("p (c n) -> p c n", c=8) for a in range(4)]
            kTM = [("U", 4, par, "rw", a) for a in range(4)]
            srcs = [(PAIR1[:, 0], ("U", 1, 0)), (PAIR2[:, 0], ("U", 2, 0)), (PAIR2[:, 1], ("U", 2, 1)), (V3_, ("U", 3, 0))]
            for a in range(4):
                b = self.bank()
                pb = self.banks[b][0:bs, :].bitcast(BF16)
                for c in range(8):
                    P.op("pe", (lambda e, pb=pb, c=c, sv=srcs[a][0][:, c, tb:tb + bs]: e.transpose(pb[:, c * 128:(c + 1) * 128], sv, self.IDB[:, :])),
                         reads=[srcs[a][1], ("IDB",)], writes=[("bank", b)])
                P.op("act", (lambda e, pb=pb, tmv=TM[a].rearrange("p c n -> p (c n)"): e.copy(out=tmv, in_=pb[:, 0:1024])), reads=[("bank", b)], writes=[kTM[a]])
            KBT, ABT, KKT, VT = TM
            U4 = lambda bk: self.banks[bk][0:bs, 0:4 * bs].rearrange("p (u t) -> p u t", u=4)
            units_of = lambda g: [(2 * g + cl, e_, cl * 2 + e_) for cl in range(2) for e_ in range(2)]
            st = [dict() for _ in range(4)]

            def s_scores(g):
                S = sets[g]
                bC = [self.bank(), self.bank()]
                for (c, e_, i4) in units_of(g):
                    cl = c - 2 * g
                    ps = slice(64 * e_, 64 * e_ + 64)
                    rw_ = (64 * e_, 64 * e_ + 64)
                    b = self.bank()
                    self.mm(b, PAIR2[ps, 0, c, tb:tb + bs], PAIR1[ps, :, c, tb:tb + bs], True, True, [("U", 2, 0), ("U", 1)],
                            self.banks[b][0:bs, 0:2 * bs].rearrange("p (a t) -> p a t", a=2), rows=rw_)
                    self.mm(b, PAIR1[ps, 0, c, tb:tb + bs], PAIR2[ps, :, c, tb:tb + bs], True, True, [("U", 1, 0), ("U", 2)],
                            self.banks[b][0:bs, 2 * bs:4 * bs].rearrange("p (a t) -> p a t", a=2), rows=rw_)
                    self.mm(bC[e_], PAIR2[ps, 1, c, tb:tb + bs], PAIR1[ps, 1, c, tb:tb + bs], True, True, [("U", 2, 1), ("U", 1, 1)],
                            self.banks[bC[e_]][0:bs, cl * bs:(cl + 1) * bs], rows=rw_)
                    bk4 = self.banks[b][0:bs, 0:4 * bs].rearrange("p (x y t) -> p x y t", x=2, y=2)
                    P.op("dve", (lambda e, bk4=bk4, o_=S["N2"][0][:, i4]: e.tensor_tensor(out=o_, in0=bk4[:, :, 0, :], in1=maska, op=ALU.mult)),
                         reads=[("bank", b), ("cst",)], writes=[S["kN2"][0]])
                    P.op("dve", (lambda e, bk4=bk4, o_=S["DB"][:, i4]: e.tensor_tensor(out=o_, in0=bk4[:, :, 1, :], in1=maskb, op=ALU.mult)),
                         reads=[("bank", b), ("cst",)], writes=[S["kDB"]])
                cm4 = S["CMT"].rearrange("p (c e) t -> p c e t", e=2)
                for e_ in range(2):
                    P.op("dve", (lambda e, bk=bC[e_], o_=cm4[:, :, e_, :]: e.tensor_tensor(out=o_, in0=self.banks[bk][0:bs, 0:2 * bs].rearrange("p (c t) -> p c t", c=2),
                                                                                        in1=mui.unsqueeze(1).to_broadcast([bs, 2, bs]), op=ALU.mult)),
                         reads=[("bank", bC[e_]), ("cst",)], writes=[S["kCMT"]])
                P.op("dve", (lambda e, S=S: e.tensor_tensor(out=S["PT"], in0=S["N2"][0][:, :, 0, :], in1=identb.unsqueeze(1).to_broadcast([bs, 4, bs]), op=ALU.add)),
                     reads=[S["kN2"][0], ("cst",)], writes=[S["kPT"]])
                st[g]["cur"] = 0

            def s_level(g, lvl):
                S = sets[g]
                N2, kN2 = S["N2"], S["kN2"]
                cur = st[g]["cur"]
                nxt = 1 - cur
                bL = self.bank()
                bLT = self.bank() if lvl < 5 else None
                for (c, e_, i4) in units_of(g):
                    self.mm(bL, N2[cur][:, i4, 0, :], N2[cur][:, i4, 1, :], True, True, [kN2[cur]], self.banks[bL][0:bs, i4 * bs:(i4 + 1) * bs], rows=(0, bs))
                    if lvl < 5:
                        self.mm(bLT, N2[cur][:, i4, 1, :], N2[cur][:, i4, 0, :], True, True, [kN2[cur]], self.banks[bLT][0:bs, i4 * bs:(i4 + 1) * bs], rows=(0, bs))
                P.op("act", (lambda e, bL=bL, o_=N2[nxt][:, :, 1, :]: e.copy(out=o_, in_=U4(bL))), reads=[("bank", bL)], writes=[kN2[nxt]])
                if lvl < 5:
                    P.op("act", (lambda e, bLT=bLT, o_=N2[nxt][:, :, 0, :]: e.copy(out=o_, in_=U4(bLT))), reads=[("bank", bLT)], writes=[kN2[nxt]])
                bP = self.bank()
                for (c, e_, i4) in units_of(g):
                    self.mm(bP, N2[nxt][:, i4, 1, :], S["PT"][:, i4, :], True, True, [kN2[nxt], S["kPT"]], self.banks[bP][0:bs, i4 * bs:(i4 + 1) * bs], rows=(0, bs))
                P.op("dve", (lambda e, bP=bP, S=S: e.tensor_tensor(out=S["PT"], in0=S["PT"], in1=U4(bP), op=ALU.add)), reads=[("bank", bP), S["kPT"]], writes=[S["kPT"]])
                st[g]["cur"] = nxt

            def s_x12(g):
                S = sets[g]
                if not INV_BF16:
                    P.op("act", (lambda e, S=S: e.copy(out=S["PTb"], in_=S["PT"])), reads=[S["kPT"]], writes=[S["kPTb"]])
                b1, b2 = self.bank(), self.bank()
                for (c, e_, i4) in units_of(g):
                    self.mm(b1, S["PTb"][:, i4, :], S["DB"][:, i4, 1, :], True, True, [S["kPTb"], S["kDB"]], self.banks[b1][0:bs, i4 * bs:(i4 + 1) * bs], rows=(0, bs))
                    self.mm(b2, S["PTb"][:, i4, :], KBT[:, c, 64 * e_:64 * e_ + 64], True, True, [S["kPTb"], kTM[0]], self.banks[b2][0:bs, i4 * 64:(i4 + 1) * 64], rows=(0, bs))
                P.op("act", (lambda e, b1=b1, S=S: e.copy(out=S["X1b"], in_=U4(b1))), reads=[("bank", b1)], writes=[S["kX1"]])
                P.op("act", (lambda e, b2=b2, S=S: e.copy(out=S["X2b"], in_=self.banks[b2][0:bs, 0:256].rearrange("p (u k) -> p u k", u=4))), reads=[("bank", b2)], writes=[S["kX2"]])

            def s_wkr(g):
                S = sets[g]
                bW, bK, bR = self.bank(), self.bank(), self.bank()
                for (c, e_, i4) in units_of(g):
                    cl = c - 2 * g
                    self.mm(bW, S["X1b"][:, i4, :], S["DB"][:, i4, 0, :], True, True, [S["kX1"], S["kDB"]], self.banks[bW][0:bs, i4 * bs:(i4 + 1) * bs], rows=(0, bs))
                    self.mm(bK, S["X1b"][:, i4, :], ABT[:, c, 64 * e_:64 * e_ + 64], True, True, [S["kX1"], kTM[1]], self.banks[bK][0:bs, i4 * 64:(i4 + 1) * 64], rows=(0, bs))
                    self.mm(bR, S["X2b"][:, i4, :], S["DB"][:, i4, 0, :], True, True, [S["kX2"], S["kDB"]], self.banks[bR][64 * e_:64 * e_ + 64, cl * bs:(cl + 1) * bs], rows=(0, bs))
                P.op("dve", (lambda e, bW=bW, S=S: e.tensor_tensor(out=S["WT"], in0=S["CMT"], in1=U4(bW), op=ALU.subtract)),
                     reads=[("bank", bW), S["kCMT"], S["kN2"][1]], writes=[S["kAL"]])
                P.op("dve", (lambda e, bK=bK, S=S, kkv=KKT[:, 2 * g:2 * g + 2, :].rearrange("p c (e k) -> p (c e) k", e=2): e.tensor_tensor(
                    out=S["KS"], in0=kkv, in1=self.banks[bK][0:bs, 0:256].rearrange("p (u k) -> p u k", u=4), op=ALU.subtract)),
                    reads=[("bank", bK), kTM[2]], writes=[S["kAL"]])
                rv = PAIR1[:, 1, 2 * g:2 * g + 2, tb:tb + bs]
                P.op("dve", (lambda e, bR=bR, rv=rv: e.tensor_tensor(out=rv, in0=rv, in1=self.banks[bR][:, 0:2 * bs].rearrange("p (c t) -> p c t", c=2), op=ALU.subtract)),
                     reads=[("bank", bR), ("U", 1, 1)], writes=[("U", 1, 1)])

            def s_mt0(g):
                S = sets[g]
                nsl = nchb * 2
                for jj in range(nchb):
                    bM = self.bank()
                    tp = slice(64 * jj, 64 * jj + 64)
                    for (c, e_, i4) in units_of(g):
                        cl = c - 2 * g
                        self.mm(bM, S["X2b"][tp, i4, :], ABT[tp, c, 64 * e_:64 * e_ + 64], True, True, [S["kX2"], kTM[1]],
                                self.banks[bM][64 * e_:64 * e_ + 64, cl * 64:(cl + 1) * 64], rows=(64 * jj, 64 * jj + 64))
                    P.op("dve", (lambda e, bM=bM, o_=S["MT0"][:, jj * 2:jj * 2 + 2, :]: e.tensor_tensor(out=o_, in0=idp.unsqueeze(1).to_broadcast([128, 2, 64]),
                                                                                                   in1=self.banks[bM][:, 0:128].rearrange("p (a k) -> p a k", k=64), op=ALU.subtract)),
                         reads=[("bank", bM), ("cst",)], writes=[S["kAL"]])

            def s_chain(g, jj):
                S = sets[g]
                ch = bl * nchb + jj
                tp = slice(64 * jj, 64 * jj + 64)
                P.op("act", (lambda e, o_=S["HSB"][:, jj, :, :]: e.copy(out=o_, in_=self.HST[:, 2 * g:2 * g + 2, :])), reads=[("HST", g)], writes=[S["kAL"]])
                bG = [self.bank(), self.bank()]
                for (c, e_, i4) in units_of(g):
                    cl = c - 2 * g
                    ps = slice(64 * e_, 64 * e_ + 64)
                    oap = self.banks[bG[e_]][ps, cl * 64:(cl + 1) * 64]
                    self.mm(bG[e_], S["KS"][tp, i4, :], VT[tp, c, 64 * e_:64 * e_ + 64], True, False, [S["kAL"], kTM[3]], oap, rows=(64 * jj, 64 * jj + 64))
                    self.mm(bG[e_], S["MT0"][ps, jj * 2 + cl, :], S["HSB"][ps, jj, cl, :], False, True, [S["kAL"]], oap, rows=(64 * e_, 64 * e_ + 64))
                for e_ in range(2):
                    ps = slice(64 * e_, 64 * e_ + 64)
                    gm = self.GAM[ps, 2 * g:2 * g + 2, ch:ch + 1].to_broadcast([64, 2, 64])
                    P.op("dve", (lambda e, bk=bG[e_], ps=ps, gm=gm: e.tensor_tensor(out=self.HST[ps, 2 * g:2 * g + 2, :], in0=self.banks[bk][ps, 0:128].rearrange("p (c v) -> p c v", c=2),
                                                                                  in1=gm, op=ALU.mult)),
                         reads=[("bank", bG[e_]), ("GAM",)], writes=[("HST", g, e_)])

            def s_out(g):
                S = sets[g]
                bO = [self.bank(), self.bank()]
                for (c, e_, i4) in units_of(g):
                    cl = c - 2 * g
                    ps = slice(64 * e_, 64 * e_ + 64)
                    self.mm(bO[e_], VT[:, c, 64 * e_:64 * e_ + 64], S["WT"][:, i4, :], True, False, [kTM[3], S["kAL"]], self.banks[bO[e_]][ps, cl * bs:(cl + 1) * bs], rows=(0, bs))
                    for jj in range(nchb):
                        self.mm(bO[e_], S["HSB"][ps, jj, cl, :], PAIR1[ps, 1, c, tb + 64 * jj:tb + 64 * jj + 64], False, jj == nchb - 1, [S["kAL"], ("U", 1, 1)],
                                self.banks[bO[e_]][ps, cl * bs + 64 * jj:cl * bs + 64 * jj + 64], rows=(64 * e_, 64 * e_ + 64))
                for e_ in range(2):
                    ps = slice(64 * e_, 64 * e_ + 64)
                    P.op("act", (lambda e, bk=bO[e_], ps=ps, ov=O3[ps, 2 * g:2 * g + 2, tb:tb + bs]: e.copy(out=ov, in_=self.banks[bk][ps, 0:2 * bs].rearrange("p (c t) -> p c t", c=2))),
                         reads=[("bank", bO[e_])], writes=[("U", 6, "o", g, e_)])
            stages = [s_scores] + [(lambda g, l=l: s_level(g, l)) for l in range(1, 6)] + [s_x12, s_wkr, s_mt0] + \
                     [(lambda g, jj=jj: s_chain(g, jj)) for jj in range(nchb)] + [s_out]
            for sf in stages:
                for g in range(4):
                    sf(g)
        if last:
            dsto = self.o_rw_p[0] if kind == "p" else self.o_rw_s[sid]
            P.op("sp", lambda e, dsto=dsto: e.dma_start(out=dsto, in_=self.HST[:]), reads=[("HST",)], writes=[("o_rw", kind, sid)], dma=True)
        bdm = self.cs("bdm")
        OUT3 = self.b3(3, 1, T)
        for c in range(8):
            b = self.bank()
            self.mm(b, bdm, O3[:, c, :], True, True, [("cst",), ("U", 6)], self.banks[b][:, 0:T])
            P.op("dve", (lambda e, c=c, b=b: e.tensor_tensor(out=O3[:, c, :], in0=O3[:, c, :], in1=self.banks[b][:, 0:T], op=ALU.subtract)),
                 reads=[("bank", b), ("U", 6)], writes=[self.kf(6, c, T)])
            t, tk = self.tmp()
            P.op("act", (lambda e, c=c, t=t: e.activation(out=t[:, 0:T], in_=O3[:, c, :], func=AF.Square)), reads=[self.kf(6, c, T)], writes=[tk])
            b2 = self.bank()
            self.mm(b2, bdm, t[:, 0:T], True, True, [("cst",), tk], self.banks[b2][:, 0:T])
            P.op("act", (lambda e, b2=b2, t=t: e.activation(out=t[:, 0:T], in_=self.banks[b2][:, 0:T], func=AF.Sqrt, bias=self.gn_eps[:, 0:1], scale=1.0)),
                 reads=[("bank", b2), ("eps",)], writes=[tk])
            P.op("dve", (lambda e, t=t: e.reciprocal(out=t[:, 0:T], in_=t[:, 0:T])), reads=[tk], writes=[tk])
            P.op("dve", (lambda e, c=c, t=t: e.scalar_tensor_tensor(out=O3[:, c, :], in0=O3[:, c, :], scalar=self.pv("rw_ln_g", c), in1=t[:, 0:T], op0=ALU.mult, op1=ALU.mult)),
                 reads=[tk, self.kf(6, c, T), ("pvec",)], writes=[self.kf(6, c, T)])
            P.op("dve", (lambda e, c=c: e.scalar_tensor_tensor(out=O3[:, c, :], in0=O3[:, c, :], scalar=self.pv("rw_ln_b", c), in1=BON3[:, c, :], op0=ALU.add, op1=ALU.add)),
                 reads=[self.kf(6, c, T), self.kb(5, 1, c, T), ("pvec",)], writes=[self.kf(6, c, T)])
            P.op("dve", (lambda e, c=c: e.tensor_tensor(out=OUT3[:, c, :], in0=O3[:, c, :], in1=G3[:, c, :], op=ALU.mult)),
                 reads=[self.kf(6, c, T), self.kb(5, 0, c, T)], writes=[self.kb(3, 1, c, T)])
        self.proj_fm("rw_wo", j, OUT3, lambda k: self.kb(3, 1, k, T), T, self.add_to_x(T))

    def halo_view(self, halo, T):
        n = 8 * (halo + T)
        return self.UALL[:, 4096:4096 + n].rearrange("p (c t) -> p c t", c=8)

    def rglru(self, j, li, T, kind, sid, first, last):
        P = self.P
        H3 = self.b3(0, 0, T)
        hk = lambda c: self.kb(0, 0, c, T)
        self.rmsnorm("norm_mix", li, T, H3, hk)
        if first:
            if kind == "p":
                P.op("pool", lambda e: e.memset(self.LH[:], 0.0), writes=[("LH",)])
                P.op("pool", lambda e: e.memset(self.LC[:], 0.0), writes=[("LC",)])
            else:
                P.op("sp", lambda e: e.dma_start(out=self.LH[:], in_=self.st_lh[sid]), writes=[("LH",)], dma=True)
                P.op("sp", lambda e: e.dma_start(out=self.LC[:], in_=self.st_lc[sid]), writes=[("LC",)], dma=True)
        XWH = self.halo_view(3, T)
        kxw = ("U", 1)
        kxw2 = ("U", 2, 0)
        Y3 = self.b3(0, 1, T)
        ky = lambda c: self.kb(0, 1, c, T)
        U3 = self.f3(3, T)
        UB3 = self.b3(2, 1, T)
        R3 = self.f3(4, T)
        I3 = self.f3(5, T)
        M3 = self.f3(6, T)
        P.op("pool", lambda e: e.tensor_copy(out=XWH[:, :, 0:3], in_=self.LC[:]), reads=[("LC",)], writes=[kxw, kxw2])
        def evac_y(jc, b):
            ps = self.banks[b][:, 0:T]
            t, tk = self.tmp()
            P.op("act", (lambda e: e.activation(out=t[:, 0:T], in_=ps, func=AF.Square)), reads=[("bank", b)], writes=[tk])
            P.op("dve", (lambda e: e.tensor_scalar(out=t[:, 0:T], in0=t[:, 0:T], scalar1=0.044715, scalar2=1.0, op0=ALU.mult, op1=ALU.add)), reads=[tk], writes=[tk])
            P.op("dve", (lambda e: e.tensor_tensor(out=t[:, 0:T], in0=t[:, 0:T], in1=ps, op=ALU.mult)), reads=[tk, ("bank", b)], writes=[tk])
            P.op("act", (lambda e: e.activation(out=t[:, 0:T], in_=t[:, 0:T], func=AF.Sigmoid, scale=1.5957691216057308)), reads=[tk], writes=[tk])
            P.op("dve", (lambda e: e.tensor_tensor(out=Y3[:, jc, :], in0=t[:, 0:T], in1=ps, op=ALU.mult)), reads=[tk, ("bank", b)], writes=[ky(jc)])
        self.proj_fm("lru_wy", j, H3, hk, T, evac_y)
        self.proj_fm("lru_wx", j, H3, hk, T,
                     lambda jc, b: P.op("act", (lambda e: e.copy(out=XWH[:, jc, 3:3 + T], in_=self.banks[b][:, 0:T])),
                                        reads=[("bank", b)], writes=[kxw, kxw2]))
        P.op("pool", lambda e: e.tensor_copy(out=self.LC[:], in_=XWH[:, :, T:T + 3]), reads=[kxw, kxw2], writes=[("LC",)])
        if last:
            dsto = self.o_lc_p[0] if kind == "p" else self.o_lc_s[sid]
            P.op("sp", lambda e, dsto=dsto: e.dma_start(out=dsto, in_=self.LC[:]), reads=[("LC",)], writes=[("o_lc", kind, sid)], dma=True)
        for c in range(8):
            cw = lambda jj, c=c: self.pv("lru_conv_w", jj * 8 + c)
            P.op("dve", (lambda e, c=c, cw=cw: e.tensor_scalar(out=U3[:, c, :], in0=XWH[:, c, 3:3 + T], scalar1=cw(3), scalar2=self.pv("lru_conv_b", c),
                                                             op0=ALU.mult, op1=ALU.add)), reads=[kxw, kxw2, ("pvec",)], writes=[self.kf(3, c, T)])
            for jj in range(3):
                P.op("dve", (lambda e, c=c, cw=cw, jj=jj: e.scalar_tensor_tensor(out=U3[:, c, :], in0=XWH[:, c, jj:jj + T], scalar=cw(jj), in1=U3[:, c, :],
                                                                                 op0=ALU.mult, op1=ALU.add)),
                     reads=[kxw, kxw2, ("pvec",), self.kf(3, c, T)], writes=[self.kf(3, c, T)])
            P.op("act", (lambda e, c=c: e.copy(out=UB3[:, c, :], in_=U3[:, c, :])), reads=[self.kf(3, c, T)], writes=[self.kb(2, 1, c, T)])
        sga, vga = self.load_weight("lru_ga_w", j, 0, 8, 0, 128)
        sgx, vgx = self.load_weight("lru_gx_w", j, 0, 8, 0, 128)
        for c in range(8):
            b = self.bank()
            self.mm(b, vga[:, c, :], UB3[:, c, :], True, True, [("WS", sga), self.kb(2, 1, c, T)], self.banks[b][:, 0:T])
            P.op("act", (lambda e, c=c, b=b: e.activation(out=R3[:, c, :], in_=self.banks[b][:, 0:T], func=AF.Sigmoid, bias=self.pv("lru_ga_b", c), scale=1.0)),
                 reads=[("bank", b), ("pvec",)], writes=[self.kf(4, c, T)])
            b2 = self.bank()
            self.mm(b2, vgx[:, c, :], UB3[:, c, :], True, True, [("WS", sgx), self.kb(2, 1, c, T)], self.banks[b2][:, 0:T])
            P.op("act", (lambda e, c=c, b2=b2: e.activation(out=I3[:, c, :], in_=self.banks[b2][:, 0:T], func=AF.Sigmoid, bias=self.pv("lru_gx_b", c), scale=1.0)),
                 reads=[("bank", b2), ("pvec",)], writes=[self.kf(5, c, T)])
            P.op("act", (lambda e, c=c: e.activation(out=M3[:, c, :], in_=R3[:, c, :], func=AF.Exp, scale=self.NSP2[:, c:c + 1])),
                 reads=[self.kf(4, c, T), ("NSP",)], writes=[self.kf(6, c, T)])
            P.op("act", (lambda e, c=c: e.activation(out=M3[:, c, :], in_=M3[:, c, :], func=AF.Sqrt, scale=-1.0, bias=self.one_t[:, 0:1])),
                 reads=[self.kf(6, c, T), ("eps",)], writes=[self.kf(6, c, T)])
            P.op("act", (lambda e, c=c: e.activation(out=R3[:, c, :], in_=R3[:, c, :], func=AF.Exp, scale=self.NSP[:, c:c + 1])),
                 reads=[self.kf(4, c, T), ("NSP",)], writes=[self.kf(4, c, T)])
            P.op("dve", (lambda e, c=c: e.tensor_tensor(out=I3[:, c, :], in0=I3[:, c, :], in1=U3[:, c, :], op=ALU.mult)),
                 reads=[self.kf(5, c, T), self.kf(3, c, T)], writes=[self.kf(5, c, T)])
            P.op("dve", (lambda e, c=c: e.tensor_tensor(out=I3[:, c, :], in0=I3[:, c, :], in1=M3[:, c, :], op=ALU.mult)),
                 reads=[self.kf(5, c, T), self.kf(6, c, T)], writes=[self.kf(5, c, T)])
            P.op("dve", (lambda e, c=c: e.tensor_tensor_scan(out=U3[:, c, :], data0=R3[:, c, :], data1=I3[:, c, :], initial=self.LH[:, c:c + 1],
                                                            op0=ALU.mult, op1=ALU.add)),
                 reads=[self.kf(4, c, T), self.kf(5, c, T), ("LH",)], writes=[self.kf(3, c, T)])
        P.op("pool", lambda e: e.tensor_copy(out=self.LH[:], in_=U3[:, :, T - 1]), reads=[("U", 3)], writes=[("LH",)])
        if last:
            dsto = self.o_lh_p[0] if kind == "p" else self.o_lh_s[sid]
            P.op("sp", lambda e, dsto=dsto: e.dma_start(out=dsto, in_=self.LH[:]), reads=[("LH",)], writes=[("o_lh", kind, sid)], dma=True)
        OUTf = self.bf(2, 1, T)
        OUT3 = self.b3(2, 1, T)
        P.op("dve", lambda e: e.tensor_tensor(out=OUTf, in0=self.ff(3, T), in1=self.bf(0, 1, T), op=ALU.mult),
             reads=[("U", 3), ("U", 0, 1)], writes=[("U", 2, 1)])
        self.proj_fm("lru_wo", j, OUT3, lambda k: ("U", 2, 1), T, self.add_to_x(T))

    def conformer(self, j, li, T, kind, sid, first, last):
        P = self.P
        H3 = self.b3(0, 0, T)
        hk = lambda c: self.kb(0, 0, c, T)
        self.rmsnorm("norm_mix", li, T, H3, hk)
        if first:
            if kind == "p":
                P.op("pool", lambda e: e.memset(self.CH[:], 0.0), writes=[("CH",)])
            else:
                P.op("sp", lambda e: e.dma_start(out=self.CH[:], in_=self.st_cf[sid]), writes=[("CH",)], dma=True)
        UH = self.Ubw(1)[:, 0:8 * (30 + T)].rearrange("p (c t) -> p c t", c=8)
        kuh = ("U", 1)
        P.op("pool", lambda e: e.tensor_copy(out=UH[:, :, 0:30], in_=self.CH[:]), reads=[("CH",)], writes=[kuh])
        sa, va = self.load_weight("cf_w1", j, 0, 8, 0, D)
        sg, vg = self.load_weight("cf_w1", j, 0, 8, D, D)
        for c in range(8):
            ba, bg = self.bank(), self.bank()
            for k in range(8):
                self.mm(ba, va[:, k, c * 128:(c + 1) * 128], H3[:, k, :], k == 0, k == 7, [("WS", sa), hk(k)], self.banks[ba][:, 0:T])
            for k in range(8):
                self.mm(bg, vg[:, k, c * 128:(c + 1) * 128], H3[:, k, :], k == 0, k == 7, [("WS", sg), hk(k)], self.banks[bg][:, 0:T])
            t, tk = self.tmp()
            P.op("act", (lambda e, t=t, bg=bg, c=c: e.activation(out=t[:, 0:T], in_=self.banks[bg][:, 0:T], func=AF.Sigmoid, bias=self.pv("cf_b1", 8 + c), scale=1.0)),
                 reads=[("bank", bg), ("pvec",)], writes=[tk])
            P.op("dve", (lambda e, t=t, ba=ba, c=c: e.scalar_tensor_tensor(out=UH[:, c, 30:30 + T], in0=self.banks[ba][:, 0:T], scalar=self.pv("cf_b1", c),
                                                                          in1=t[:, 0:T], op0=ALU.add, op1=ALU.mult)),
                 reads=[("bank", ba), tk, ("pvec",)], writes=[kuh + (0, "uh", c)])
        P.op("pool", lambda e: e.tensor_copy(out=self.CH[:], in_=UH[:, :, T:T + 30]), reads=[kuh], writes=[("CH",)])
        if last:
            dsto = self.o_cf_p[0] if kind == "p" else self.o_cf_s[sid]
            P.op("sp", lambda e, dsto=dsto: e.dma_start(out=dsto, in_=self.CH[:]), reads=[("CH",)], writes=[("o_cf", kind, sid)], dma=True)
        C3 = self.f3(3, T)
        for c in range(8):
            slot = self._wslot_rr % 3
            self._wslot_rr += 1
            dgv = self.WS[slot][:, 0:31 * 128].rearrange("p (j m) -> p j m", j=31)
            P.op("sp", (lambda e, dgv=dgv, c=c: e.dma_start(out=dgv.rearrange("p j m -> p (j m)"), in_=self.dg_d[c])), reads=[("dg", c)], writes=[("WS", slot)], dma=True)
            b = self.bank()
            for jj in range(31):
                self.mm(b, dgv[:, jj, :], UH[:, c, jj:jj + T], jj == 0, jj == 30, [("WS", slot), kuh + (0, "uh", c)], self.banks[b][:, 0:T])
            P.op("act", (lambda e, b=b, c=c: e.activation(out=C3[:, c, :], in_=self.banks[b][:, 0:T], func=AF.Identity, bias=self.pv("cf_dw_b", c), scale=1.0)),
                 reads=[("bank", b), ("pvec",)], writes=[self.kf(3, c, T)])
        Cf = self.ff(3, T)
        CBf = self.bf(0, 1, T)
        CB3 = self.b3(0, 1, T)
        XC3 = self.f3(4, T)
        XCf = self.ff(4, T)
        P.op("act", lambda e: e.copy(out=CBf, in_=Cf), reads=[("U", 3)], writes=[("U", 0, 1)])
        bm = self.bank()
        for c in range(8):
            self.mm(bm, self.ones_bf[:], CB3[:, c, :], c == 0, c == 7, [("ones",), ("U", 0, 1)], self.banks[bm][:, 0:T])
        mb = self.banks[bm][:, 0:T].unsqueeze(1).to_broadcast([128, 8, T])
        P.op("dve", lambda e: e.tensor_tensor(out=XC3, in0=C3, in1=mb, op=ALU.subtract), reads=[("U", 3), ("bank", bm)], writes=[("U", 4)])
        P.op("act", lambda e: e.activation(out=CBf, in_=XCf, func=AF.Square), reads=[("U", 4)], writes=[("U", 0, 1)])
        bv = self.bank()
        for c in range(8):
            self.mm(bv, self.ones_bf[:], CB3[:, c, :], c == 0, c == 7, [("ones",), ("U", 0, 1)], self.banks[bv][:, 0:T])
        t, tk = self.tmp()
        P.op("act", lambda e: e.activation(out=t[:, 0:T], in_=self.banks[bv][:, 0:T], func=AF.Sqrt, bias=self.eps_t[:, 0:1], scale=1.0),
             reads=[("bank", bv), ("eps",)], writes=[tk])
        P.op("dve", lambda e: e.reciprocal(out=self.RSTD[:, 0:T], in_=t[:, 0:T]), reads=[tk], writes=[("RSTD",)])
        CS3 = self.b3(2, 1, T)
        for c in range(8):
            P.op("dve", (lambda e, c=c: e.scalar_tensor_tensor(out=XC3[:, c, :], in0=XC3[:, c, :], scalar=self.pv("cf_ln_g", c), in1=self.RSTD[:, 0:T],
                                                             op0=ALU.mult, op1=ALU.mult)),
                 reads=[self.kf(4, c, T), ("RSTD",), ("pvec",)], writes=[self.kf(4, c, T)])
            P.op("act", (lambda e, c=c: e.activation(out=CS3[:, c, :], in_=XC3[:, c, :], func=AF.Silu, bias=self.pv("cf_ln_b", c), scale=1.0)),
                 reads=[self.kf(4, c, T), ("pvec",)], writes=[self.kb(2, 1, c, T)])
        X3 = self.x3(T)

        def evac_o(jc, b):
            P.op("dve", (lambda e: e.scalar_tensor_tensor(out=X3[:, jc, :], in0=self.banks[b][:, 0:T], scalar=self.pv("cf_b2", jc), in1=X3[:, jc, :],
                                                          op0=ALU.add, op1=ALU.add)),
                 reads=[("bank", b), self.kx(jc, T), ("pvec",)], writes=[self.kx(jc, T)])
        self.proj_fm("cf_w2", j, CS3, lambda k: self.kb(2, 1, k, T), T, evac_o)


def run(inputs, Lp=SEQ, mixers=(True, True, True, True), depth=4, trace=False):
    bld = Builder(Lp, mixers=mixers, depth=depth)
    nc = bld.build()
    f = lambda k: np.asarray(inputs[k], np.float32)
    pvec = pack_pvec(inputs)
    cst = make_cst()
    xp, xs = f("x_prompt"), f("x_sample")
    nb = xp.shape[0]
    zeros_p = np.zeros((Lp, D), np.float32)
    in_maps = []
    for c in range(N_CORES):
        m = {"pvec": pvec, "cst": cst}
        m["xp"] = np.ascontiguousarray(xp[c, :Lp]) if c < nb else zeros_p
        m["xs"] = np.ascontiguousarray(xs[2 * c:2 * c + 2].reshape(2 * DEC_SEQ, D))
        m["st_hg"] = np.ascontiguousarray(f("state_hgrn")[0, 2 * c:2 * c + 2])
        m["st_rw"] = np.stack([_rw_in(f("state_rwkv")[0, q]) for q in (2 * c, 2 * c + 1)])
        m["st_sh"] = np.stack([_cols(f("state_rwkv_shift")[0, q]) for q in (2 * c, 2 * c + 1)])
        m["st_lh"] = np.stack([_cols(f("state_lru")[0, q]) for q in (2 * c, 2 * c + 1)])
        m["st_lc"] = np.stack([_rows_to_pcr(f("state_lru_conv")[0, q]) for q in (2 * c, 2 * c + 1)])
        m["st_cf"] = np.stack([_rows_to_pcr(f("state_conf_conv")[0, q]) for q in (2 * c, 2 * c + 1)])
        for name, K, N, cnt in WEIGHTS:
            m[name] = np.ascontiguousarray(f(name)[:cnt]).reshape(cnt, K, N)
        in_maps.append(m)
    res = run_bass_kernel_spmd(nc, in_maps, core_ids=list(range(N_CORES)), **({"trace": True} if trace else {}))
    rs = res.results
    y_prompt = np.stack([rs[c]["yp"] for c in range(nb)], axis=0)
    y_sample = np.concatenate([rs[c]["ys"].reshape(2, DEC_SEQ, D) for c in range(N_CORES)], axis=0)
    p_hg = np.stack([rs[c]["o_hg_p"][0] for c in range(nb)], axis=0)[None]
    s_hg = np.concatenate([rs[c]["o_hg_s"] for c in range(N_CORES)], axis=0)[None]
    p_lh = np.stack([_pc_to_vec(rs[c]["o_lh_p"][0]) for c in range(nb)], axis=0)[None]
    s_lh = np.stack([_pc_to_vec(rs[c]["o_lh_s"][q]) for c in range(N_CORES) for q in range(2)], axis=0)[None]
    p_lc = np.stack([_pcr_to_rows(rs[c]["o_lc_p"][0]) for c in range(nb)], axis=0)[None]
    s_lc = np.stack([_pcr_to_rows(rs[c]["o_lc_s"][q]) for c in range(N_CORES) for q in range(2)], axis=0)[None]
    p_cf = np.stack([_pcr_to_rows(rs[c]["o_cf_p"][0]) for c in range(nb)], axis=0)[None]
    s_cf = np.stack([_pcr_to_rows(rs[c]["o_cf_s"][q]) for c in range(N_CORES) for q in range(2)], axis=0)[None]
    p_rw = np.stack([_rw_out(rs[c]["o_rw_p"][0]) for c in range(nb)], axis=0)[None]
    s_rw = np.stack([_rw_out(rs[c]["o_rw_s"][q]) for c in range(N_CORES) for q in range(2)], axis=0)[None]
    p_sh = np.stack([_pc_to_vec(rs[c]["o_sh_p"][0]) for c in range(nb)], axis=0)[None]
    s_sh = np.stack([_pc_to_vec(rs[c]["o_sh_s"][q]) for c in range(N_CORES) for q in range(2)], axis=0)[None]
    outs = [y_prompt, y_sample, p_hg, p_rw, p_sh, p_lh, p_lc, p_cf, s_hg, s_rw, s_sh, s_lh, s_lc, s_cf]
    return tuple(outs), res


def kernel(**inputs):
    outs, _ = run(inputs)
    return outs
```

```python
import numpy as np
import concourse.bass as bass
import concourse.mybir as mybir
from concourse.bass_utils import run_bass_kernel_spmd

F32 = mybir.dt.float32
BF16 = mybir.dt.bfloat16
AF = mybir.ActivationFunctionType
ALU = mybir.AluOpType
AX = mybir.AxisListType

D = 1024
NC8 = 8
DFF = 2816
NFF = DFF // 128
EPS = 1e-6
SEQ = 16384
DEC_SEQ = 64
N_CORES = 8
INV_BF16 = True


class Op:
    __slots__ = ("eng", "fn", "reads", "writes", "dma", "ekey", "pos", "deps", "need_inc", "semval", "clock", "rows")

    def __init__(self, eng, fn, reads, writes, dma):
        self.eng = eng
        self.fn = fn
        self.reads = reads
        self.writes = writes
        self.dma = dma
        self.need_inc = False
        self.deps = ()
        self.rows = (0, 128)


def _overlap(a, b):
    for x, y in zip(a, b):
        if x == y:
            continue
        if isinstance(x, str) or isinstance(y, str):
            return True
        return False
    return True


class Prog:
    ENGS = ("pe", "act", "dve", "pool", "sp")
    NSLOT = 8

    def __init__(self):
        self.ops = []
        self.dma_rr = {}

    def op(self, eng, fn, reads=(), writes=(), dma=False, rows=None):
        o = Op(eng, fn, tuple(reads), tuple(writes), dma)
        if rows is not None:
            o.rows = rows
        self.ops.append(o)
        return o

    def analyze(self):
        bufs = {}
        pos = {}
        known = {e: {} for e in self.ENGS}
        last_on_slot = {}
        for o in self.ops:
            if o.dma:
                rr = self.dma_rr.get(o.eng, 0)
                self.dma_rr[o.eng] = rr + 1
                o.ekey = (o.eng, rr % self.NSLOT)
            else:
                o.ekey = o.eng
            pos[o.ekey] = pos.get(o.ekey, 0) + 1
            o.pos = pos[o.ekey]
            deps = {}

            def add(d):
                if d is None:
                    return
                if (not o.dma) and d.ekey == o.ekey and o.eng == "pe":
                    if not (d.rows[1] <= o.rows[0] or o.rows[1] <= d.rows[0]):
                        return
                if deps.get(d.ekey, (0, None))[0] < d.pos:
                    deps[d.ekey] = (d.pos, d)

            if o.dma:
                add(last_on_slot.get(o.ekey))
                last_on_slot[o.ekey] = o
            for r in o.reads:
                for ent in bufs.get(r[0], ()):
                    if _overlap(ent[0], r):
                        add(ent[1])
            for w in o.writes:
                for ent in bufs.get(w[0], ()):
                    if _overlap(ent[0], w):
                        add(ent[1])
                        for rd in ent[2].values():
                            if rd is not o:
                                add(rd)
            for r in o.reads:
                lst = bufs.setdefault(r[0], [])
                for ent in lst:
                    if ent[0] == r:
                        ent[2][o.ekey] = o
                        break
                else:
                    lst.append([r, None, {o.ekey: o}])
            for w in o.writes:
                lst = bufs.setdefault(w[0], [])
                lst[:] = [e for e in lst if not (len(e[0]) >= len(w) and e[0][:len(w)] == w)]
                lst.append([w, o, {}])
            kn = known[o.eng]
            final = []
            for ek, (p, d) in deps.items():
                if kn.get(ek, 0) >= p:
                    continue
                final.append(d)
            for d in final:
                d.need_inc = True
                for ek, p in d.clock.items():
                    if kn.get(ek, 0) < p:
                        kn[ek] = p
            o.deps = final
            ck = dict(kn)
            ck[o.ekey] = o.pos
            o.clock = ck
            if o.dma:
                o.need_inc = True
            else:
                kn_self = kn
        cnt = {}
        for o in self.ops:
            if o.need_inc:
                cnt[o.ekey] = cnt.get(o.ekey, 0) + (16 if o.dma else 1)
                o.semval = cnt[o.ekey]
        return cnt

    def emit(self, nc):
        cnt = self.analyze()
        ekeys = sorted(cnt.keys(), key=str)
        import contextlib
        with contextlib.ExitStack() as es:
            sems = {}
            for ek in ekeys:
                nm = "s_" + (ek if isinstance(ek, str) else f"{ek[0]}{ek[1]}")
                sems[ek] = es.enter_context(nc.semaphore(nm))
            block = es.enter_context(nc.Block())
            per_eng = {e: [o for o in self.ops if o.eng == e] for e in self.ENGS}

            def run(engobj, ename):
                lst = per_eng[ename]
                for o in lst:
                    for d in o.deps:
                        engobj.wait_ge(sems[d.ekey], d.semval)
                    ins = o.fn(engobj)
                    if o.need_inc:
                        ins.then_inc(sems[o.ekey], 16 if o.dma else 1)
                for ek in ekeys:
                    if not isinstance(ek, str) and ek[0] == ename:
                        engobj.wait_ge(sems[ek], cnt[ek])


            @block.tensor
            def _(e):
                run(e, "pe")

            @block.scalar
            def _(e):
                run(e, "act")

            @block.vector
            def _(e):
                run(e, "dve")

            @block.gpsimd
            def _(e):
                run(e, "pool")

            @block.sync
            def _(e):
                run(e, "sp")


def _cols(v):
    v = np.asarray(v, np.float32).reshape(-1, 128)
    return np.ascontiguousarray(v.T)


PVEC_SPEC = [
    ("norm_mix", 32), ("norm_ffn", 32), ("norm_final", 8),
    ("hg_lb", 16), ("hg_gn", 1),
    ("lru_conv_w", 32), ("lru_conv_b", 8), ("lru_ga_b", 8), ("lru_gx_b", 8), ("lru_lam", 8),
    ("rw_mu", 48), ("rw_w0", 8), ("rw_a0", 8), ("rw_kk", 8), ("rw_ka", 8), ("rw_rk", 8), ("rw_ln_g", 8), ("rw_ln_b", 8),
    ("cf_b1", 16), ("cf_dw_w", 248), ("cf_dw_b", 8), ("cf_ln_g", 8), ("cf_ln_b", 8), ("cf_b2", 8),
]


def _rows_to_pcr(a):
    a = np.asarray(a, np.float32)
    r = a.shape[0]
    return np.ascontiguousarray(a.reshape(r, 8, 128).transpose(2, 1, 0))


def _pcr_to_rows(a):
    return np.ascontiguousarray(np.asarray(a).transpose(2, 1, 0).reshape(a.shape[2], 1024))


def _pc_to_vec(a):
    return np.ascontiguousarray(np.asarray(a).T.reshape(1024))


def _rw_in(S):
    S = np.asarray(S, np.float32).reshape(8, 2, 64, 64)
    return np.ascontiguousarray(S.transpose(1, 3, 0, 2).reshape(128, 8, 64))


def _rw_out(Hs):
    Hs = np.asarray(Hs).reshape(2, 64, 8, 64)
    return np.ascontiguousarray(Hs.transpose(2, 0, 3, 1).reshape(16, 64, 64))


def pack_pvec(inp):
    cols = []
    for name, n in PVEC_SPEC:
        a = _cols(np.asarray(inp[name]).reshape(-1))
        assert a.shape[1] == n, (name, a.shape)
        cols.append(a)
    return np.ascontiguousarray(np.concatenate(cols, axis=1))


def pvec_offsets():
    offs = {}
    o = 0
    for name, n in PVEC_SPEC:
        offs[name] = o
        o += n
    return offs, o


CST_SPEC = [("ident", 128), ("m64", 64), ("maska", 256), ("maskb", 256), ("idp", 64), ("bdm", 128)]


def make_cst():
    ident = np.eye(128, dtype=np.float32)
    u64 = np.zeros((128, 64), np.float32)
    m64 = np.zeros((128, 64), np.float32)
    for s in range(64):
        u64[s, :s] = 1.0
        m64[s, s:] = 1.0
    idx = np.arange(128)
    same = (idx[:, None] // 64) == (idx[None, :] // 64)
    msu = (same & (idx[:, None] < idx[None, :])).astype(np.float32)
    msl = msu.T.copy()
    mui = (same & (idx[:, None] <= idx[None, :])).astype(np.float32)
    idp = np.concatenate([np.eye(64, dtype=np.float32)] * 2, axis=0)
    bdm = same.astype(np.float32) / 64.0
    return np.ascontiguousarray(np.concatenate([ident, m64, -msu, -msl, mui, msl, idp, bdm], axis=1))


def cst_offsets():
    offs = {}
    o = 0
    for name, n in CST_SPEC:
        offs[name] = o
        o += n
    return offs, o


WEIGHTS = [
    ("ffn_w1", D, DFF, 4), ("ffn_w3", D, DFF, 4), ("ffn_w2", DFF, D, 4),
    ("hg_wq", D, D, 1), ("hg_wf", D, D, 1), ("hg_wi", D, D, 1), ("hg_wg", D, D, 1), ("hg_wo", D, D, 1),
    ("lru_wy", D, D, 1), ("lru_wx", D, D, 1), ("lru_wo", D, D, 1), ("lru_ga_w", D, 128, 1), ("lru_gx_w", D, 128, 1),
    ("cf_w1", D, 2 * D, 1), ("cf_w2", D, D, 1),
    ("rw_wr", D, D, 1), ("rw_wk", D, D, 1), ("rw_wv", D, D, 1), ("rw_wo", D, D, 1),
    ("rw_w1", D, 64, 1), ("rw_a1", D, 64, 1), ("rw_g1", D, 128, 1),
    ("rw_w2", 64, D, 1), ("rw_a2", 64, D, 1), ("rw_g2", 128, D, 1),
]

NU = 7


class Builder:
    def __init__(self, Lp, Ls=DEC_SEQ, nsamp=2, mixers=(True, True, True, True), depth=4):
        self.Lp, self.Ls, self.nsamp = Lp, Ls, nsamp
        self.mixers = mixers
        self.depth = depth
        self.nc = bass.Bass("TRN2", target_bir_lowering=False)
        self.P = Prog()
        self.poffs, self.npv = pvec_offsets()
        self.coffs, self.ncst = cst_offsets()
        self._bank_rr = 0
        self._wslot_rr = 0
        self._uid = 0

    def dram(self, name, shape, dt, kind):
        return self.nc.dram_tensor(name, list(shape), dt, kind=kind).ap()

    def sb(self, name, shape, dt):
        return self._es.enter_context(self.nc.sbuf_tensor(name, list(shape), dt))

    def build(self):
        import contextlib
        nc = self.nc
        Lp, Ls, ns = self.Lp, self.Ls, self.nsamp
        with contextlib.ExitStack() as es:
            self._es = es
            self.xp = self.dram("xp", [Lp, D], F32, "ExternalInput")
            self.xs = self.dram("xs", [ns * Ls, D], F32, "ExternalInput")
            self.yp = self.dram("yp", [Lp, D], F32, "ExternalOutput")
            self.ys = self.dram("ys", [ns * Ls, D], F32, "ExternalOutput")
            self.pvec_d = self.dram("pvec", [128, self.npv], F32, "ExternalInput")
            self.cst_d = self.dram("cst", [128, self.ncst], F32, "ExternalInput")
            self.st_hg = self.dram("st_hg", [ns, 8, 128, 128], F32, "ExternalInput")
            self.o_hg_p = self.dram("o_hg_p", [1, 8, 128, 128], F32, "ExternalOutput")
            self.o_hg_s = self.dram("o_hg_s", [ns, 8, 128, 128], F32, "ExternalOutput")
            self.st_lh = self.dram("st_lh", [ns, 128, 8], F32, "ExternalInput")
            self.st_lc = self.dram("st_lc", [ns, 128, 8, 3], F32, "ExternalInput")
            self.st_cf = self.dram("st_cf", [ns, 128, 8, 30], F32, "ExternalInput")
            self.o_lh_p = self.dram("o_lh_p", [1, 128, 8], F32, "ExternalOutput")
            self.o_lh_s = self.dram("o_lh_s", [ns, 128, 8], F32, "ExternalOutput")
            self.o_lc_p = self.dram("o_lc_p", [1, 128, 8, 3], F32, "ExternalOutput")
            self.o_lc_s = self.dram("o_lc_s", [ns, 128, 8, 3], F32, "ExternalOutput")
            self.o_cf_p = self.dram("o_cf_p", [1, 128, 8, 30], F32, "ExternalOutput")
            self.o_cf_s = self.dram("o_cf_s", [ns, 128, 8, 30], F32, "ExternalOutput")
            self.st_rw = self.dram("st_rw", [ns, 128, 8, 64], F32, "ExternalInput")
            self.st_sh = self.dram("st_sh", [ns, 128, 8], F32, "ExternalInput")
            self.o_rw_p = self.dram("o_rw_p", [1, 128, 8, 64], F32, "ExternalOutput")
            self.o_rw_s = self.dram("o_rw_s", [ns, 128, 8, 64], F32, "ExternalOutput")
            self.o_sh_p = self.dram("o_sh_p", [1, 128, 8], F32, "ExternalOutput")
            self.o_sh_s = self.dram("o_sh_s", [ns, 128, 8], F32, "ExternalOutput")
            self.dg_d = self.dram("cf_dg_bf", [8, 128, 31 * 128], BF16, "Internal")
            self.w_in = {}
            self.w_bf = {}
            for name, K, N, cnt in WEIGHTS:
                self.w_in[name] = self.dram(name, [cnt, K, N], F32, "ExternalInput")
                self.w_bf[name] = self.dram(name + "_bf", [cnt, K, N], BF16, "Internal")
            self.pvec = self.sb("pvec_sb", [128, self.npv], F32)
            self.cst = self.sb("cst_sb", [128, self.ncst], F32)
            self.ident = self.cst[:, self.coffs["ident"]:self.coffs["ident"] + 128]
            self.ones_bf = self.sb("ones_bf", [128, 128], BF16)
            self.ones128 = self.sb("ones128", [128, 128], BF16)
            self.eps_t = self.sb("eps_t", [128, 1], F32)
            self.one_t = self.sb("one_t", [128, 1], F32)
            self.smask = self.sb("smask", [128, 4096], BF16)
            self.X = self.sb("X", [128, 4096], F32)
            self.RSTD = self.sb("RSTD", [128, 512], F32)
            self.UALL = self.sb("UALL", [128, NU * 4096], F32)
            self.TMP = [self.sb(f"TMP{i}", [128, 512], F32) for i in range(2)]
            self.WS = [self.sb(f"WS{i}", [128, 8192], BF16) for i in range(3)]
            self.LB = self.sb("LB", [128, 8], F32)
            self.OML = self.sb("OML", [128, 8], F32)
            self.SH = self.sb("SH", [128, 8, 128], F32)
            self.GAM = self.sb("GAM", [128, 8, 8], F32)
            self.LH = self.sb("LH", [128, 8], F32)
            self.LC = self.sb("LC", [128, 8, 3], F32)
            self.CH = self.sb("CH", [128, 8, 30], F32)
            self.NSP = self.sb("NSP", [128, 8], F32)
            self.NSP2 = self.sb("NSP2", [128, 8], F32)
            self.SPT = self.sb("SPT", [128, 4, 8], F32)
            self.HST = self.sb("HST", [128, 8, 64], F32)
            self.SHIFT = self.sb("SHIFT", [128, 8], F32)
            self.LOR = self.sb("LOR", [128, 512], BF16)
            self.IDB = self.sb("IDB", [128, 128], BF16)
            self.BD1 = self.sb("BD1", [128, 128], BF16)
            self.gn_eps = self.sb("gn_eps", [128, 1], F32)
            self.banks = [es.enter_context(nc.psum_tensor(f"bank{i}", [128, 512], F32)) for i in range(8)]
            self.program()
            self.P.emit(nc)
        return nc

    def bank(self):
        i = self._bank_rr % 8
        self._bank_rr += 1
        return i

    def tmp(self):
        self._uid += 1
        i = self._uid % 2
        return self.TMP[i], ("TMP", i)

    def pv(self, name, c0, n=1):
        o = self.poffs[name] + c0
        return self.pvec[:, o:o + n]

    def cs(self, name, rows=128):
        o = self.coffs[name]
        n = dict(CST_SPEC)[name]
        return self.cst[0:rows, o:o + n]

    def Uf(self, i):
        return self.UALL[:, i * 4096:(i + 1) * 4096]

    def f3(self, i, T, n=8):
        return self.Uf(i)[:, 0:n * T].rearrange("p (c t) -> p c t", c=n)

    def ff(self, i, T, n=8):
        return self.Uf(i)[:, 0:n * T]

    def Ub(self, i, half):
        return self.UALL[:, i * 4096 + half * 2048:i * 4096 + (half + 1) * 2048].bitcast(BF16)

    def b3(self, i, half, T, n=8):
        return self.Ub(i, half)[:, 0:n * T].rearrange("p (c t) -> p c t", c=n)

    def bf(self, i, half, T, n=8):
        return self.Ub(i, half)[:, 0:n * T]

    def Ubw(self, i):
        return self.UALL[:, i * 4096:(i + 1) * 4096].bitcast(BF16)

    @staticmethod
    def kf(i, c, T):
        return ("U", i, (c * T) // 2048, f"f{T}", c)

    @staticmethod
    def kb(i, half, c, T):
        return ("U", i, half, f"b{T}", c)

    def x3(self, T):
        return self.X[:, 0:8 * T].rearrange("p (c t) -> p c t", c=8)

    @staticmethod
    def kx(c, T):
        return ("X", f"t{T}", c)

    def program(self):
        P = self.P
        P.op("sp", lambda e: e.dma_start(out=self.pvec[:], in_=self.pvec_d[:, :]), writes=[("pvec",)], dma=True)
        P.op("sp", lambda e: e.dma_start(out=self.cst[:], in_=self.cst_d[:, :]), writes=[("cst",)], dma=True)
        P.op("pool", lambda e: e.memset(self.ones_bf[:], 1.0 / D), writes=[("ones",)])
        P.op("pool", lambda e: e.memset(self.ones128[:], 1.0 / 128), writes=[("ones128",)])
        P.op("pool", lambda e: e.memset(self.eps_t[:], EPS), writes=[("eps",)])
        P.op("pool", lambda e: e.memset(self.one_t[:], 1.0), writes=[("eps",)])
        P.op("pool", lambda e: e.memset(self.smask[:], 1.0), writes=[("smask",)])
        sm3 = self.smask[:].rearrange("p (a b) -> p a b", b=64)
        P.op("pool", lambda e: e.memset(sm3[:, :, 0:1], 0.0), reads=[("smask",)], writes=[("smask",)])
        P.op("dve", lambda e: e.tensor_tensor(out=self.LB[:], in0=self.pv("hg_lb", 0, 8), in1=self.pv("hg_lb", 8, 8), op=ALU.subtract),
             reads=[("pvec",)], writes=[("LB",)])
        P.op("act", lambda e: e.activation(out=self.LB[:], in_=self.LB[:], func=AF.Sigmoid), reads=[("LB",)], writes=[("LB",)])
        P.op("dve", lambda e: e.tensor_scalar(out=self.OML[:], in0=self.LB[:], scalar1=-1.0, scalar2=1.0, op0=ALU.mult, op1=ALU.add),
             reads=[("LB",)], writes=[("OML",)])
        P.op("dve", lambda e: e.tensor_copy(out=self.IDB[:], in_=self.ident), reads=[("cst",)], writes=[("IDB",)])
        P.op("dve", lambda e: e.tensor_scalar(out=self.BD1[:], in0=self.cs("bdm"), scalar1=64.0, scalar2=None, op0=ALU.mult), reads=[("cst",)], writes=[("BD1",)])
        P.op("pool", lambda e: e.memset(self.gn_eps[:], 64e-5), writes=[("eps",)])
        lam = self.pv("lru_lam", 0, 8)
        Z, W, W2, ACC = (self.SPT[:, i, :] for i in range(4))
        kS = ("SPT",)
        P.op("dve", lambda e: e.tensor_scalar(out=Z, in0=lam, scalar1=-1.0, scalar2=None, op0=ALU.mult), reads=[("pvec",)], writes=[kS])
        P.op("dve", lambda e: e.tensor_tensor(out=Z, in0=Z, in1=lam, op=ALU.max), reads=[("pvec",), kS], writes=[kS])
        P.op("act", lambda e: e.activation(out=Z, in_=Z, func=AF.Exp, scale=-1.0), reads=[kS], writes=[kS])
        P.op("dve", lambda e: e.tensor_scalar(out=W, in0=Z, scalar1=2.0, scalar2=None, op0=ALU.add), reads=[kS], writes=[kS])
        P.op("dve", lambda e: e.reciprocal(out=W, in_=W), reads=[kS], writes=[kS])
        P.op("dve", lambda e: e.tensor_tensor(out=W, in0=W, in1=Z, op=ALU.mult), reads=[kS], writes=[kS])
        P.op("dve", lambda e: e.tensor_tensor(out=W2, in0=W, in1=W, op=ALU.mult), reads=[kS], writes=[kS])
        P.op("dve", lambda e: e.tensor_scalar(out=ACC, in0=W2, scalar1=1.0 / 11, scalar2=1.0 / 9, op0=ALU.mult, op1=ALU.add), reads=[kS], writes=[kS])
        for cf in (1.0 / 7, 1.0 / 5, 1.0 / 3, 1.0):
            P.op("dve", lambda e: e.tensor_tensor(out=ACC, in0=ACC, in1=W2, op=ALU.mult), reads=[kS], writes=[kS])
            P.op("dve", (lambda e, cf=cf: e.tensor_scalar(out=ACC, in0=ACC, scalar1=float(cf), scalar2=None, op0=ALU.add)), reads=[kS], writes=[kS])
        P.op("dve", lambda e: e.tensor_tensor(out=ACC, in0=ACC, in1=W, op=ALU.mult), reads=[kS], writes=[kS])
        P.op("dve", lambda e: e.tensor_scalar(out=Z, in0=lam, scalar1=-1.0, scalar2=0.0, op0=ALU.mult, op1=ALU.max), reads=[("pvec",), kS], writes=[kS])
        P.op("dve", lambda e: e.scalar_tensor_tensor(out=ACC, in0=ACC, scalar=2.0, in1=Z, op0=ALU.mult, op1=ALU.add), reads=[kS], writes=[kS])
        P.op("dve", lambda e: e.tensor_scalar(out=self.NSP[:], in0=ACC, scalar1=-8.0, scalar2=None, op0=ALU.mult), reads=[kS], writes=[("NSP",)])
        P.op("dve", lambda e: e.tensor_scalar(out=self.NSP2[:], in0=ACC, scalar1=-16.0, scalar2=None, op0=ALU.mult), reads=[kS], writes=[("NSP",)])
        for c in range(8):
            stg = self.Ubw(c % 2)[:, 0:31 * 128].rearrange("p (j m) -> p j m", j=31)
            kst = ("U", c % 2)
            for jj in range(31):
                P.op("dve" if jj % 2 == 0 else "pool", (lambda e, stg=stg, jj=jj, c=c: e.tensor_scalar(out=stg[:, jj, :], in0=self.ident, scalar1=self.pv("cf_dw_w", jj * 8 + c),
                                                                  scalar2=None, op0=ALU.mult)), reads=[("cst",), ("pvec",)], writes=[kst + (0, "dg", jj)])
            P.op("sp", (lambda e, stg=stg, c=c: e.dma_start(out=self.dg_d[c], in_=stg.rearrange("p j m -> p (j m)"))), reads=[kst], writes=[("dg", c)], dma=True)
        for name, K, N, cnt in WEIGHTS:
            for i in range(cnt):
                for r0 in range(0, K, 128):
                    r1 = min(K, r0 + 128)
                    src = self.w_in[name][i, r0:r1, :]
                    dst = self.w_bf[name][i, r0:r1, :]
                    P.op("pool", (lambda e, s=src, d=dst: e.dma_start(out=d, in_=s)),
                         writes=[("wbf", name, i, r0 // 128)], dma=True)
        seqs = [("p", 0, self.Lp)] + [("s", s, self.Ls) for s in range(self.nsamp)]
        tiles = []
        for (kind, sid, L) in seqs:
            ntile = (L + 511) // 512
            for ti in range(ntile):
                t0 = ti * 512
                tiles.append((kind, sid, t0, min(512, L - t0), ti == 0, ti == ntile - 1))
        for i, tile in enumerate(tiles):
            self.do_tile(tile, tiles[i + 1] if i + 1 < len(tiles) else None, i == 0)

    def load_weight(self, name, idx, k0, nk, n0, ncols):
        slot = self._wslot_rr % 3
        self._wslot_rr += 1
        ws = self.WS[slot]
        view = ws[:, 0:nk * ncols].rearrange("p (k n) -> p k n", k=nk)
        src = self.w_bf[name][idx].rearrange("(k p) n -> p k n", p=128)[:, k0:k0 + nk, n0:n0 + ncols]
        reads = [("wbf", name, idx, k) for k in range(k0, k0 + nk)]
        self.P.op("sp", (lambda e, v=view, s=src: e.dma_start(out=v, in_=s)),
                  reads=reads, writes=[("WS", slot)], dma=True)
        return slot, view

    def mm(self, bank_i, lhsT, rhs, start, stop, reads, out, rows=None):
        self.P.op("pe", (lambda e, o=out, l=lhsT, r=rhs, st=start, sp=stop: e.matmul(o, lhsT=l, rhs=r, start=st, stop=sp)),
                  reads=list(reads), writes=[("bank", bank_i)], rows=rows)

    def proj_fm(self, wname, widx, rhs3, rhs_keyf, T, evac, n0=0, ncols=D, nk=8, k0=0):
        slot, v = self.load_weight(wname, widx, k0, nk, n0, ncols)
        for jj in range(ncols // 128):
            b = self.bank()
            for k in range(nk):
                self.mm(b, v[:, k, jj * 128:(jj + 1) * 128], rhs3[:, k, :], k == 0, k == nk - 1,
                        [("WS", slot), rhs_keyf(k)], self.banks[b][:, 0:T])
            evac(n0 // 128 + jj, b)
        return slot, v

    def rmsnorm(self, gname, gidx, T, dst3, dst_keyf, sq_i=0, sq_half=1):
        P = self.P
        X3 = self.x3(T)
        SQf = self.bf(sq_i, sq_half, T)
        SQ3 = self.b3(sq_i, sq_half, T)
        sqk = ("U", sq_i, sq_half)
        P.op("act", lambda e: e.activation(out=SQf, in_=self.X[:, 0:8 * T], func=AF.Square), reads=[("X",)], writes=[sqk])
        b = self.bank()
        for c in range(NC8):
            self.mm(b, self.ones_bf[:], SQ3[:, c, :], c == 0, c == NC8 - 1, [("ones",), sqk], self.banks[b][:, 0:T])
        t, tk = self.tmp()
        P.op("act", lambda e: e.activation(out=t[:, 0:T], in_=self.banks[b][:, 0:T], func=AF.Sqrt, bias=self.eps_t[:, 0:1], scale=1.0),
             reads=[("bank", b), ("eps",)], writes=[tk])
        P.op("dve", lambda e: e.reciprocal(out=self.RSTD[:, 0:T], in_=t[:, 0:T]), reads=[tk], writes=[("RSTD",)])
        for c in range(NC8):
            P.op("dve", (lambda e, c=c: e.scalar_tensor_tensor(out=dst3[:, c, :], in0=X3[:, c, :], scalar=self.pv(gname, gidx * 8 + c),
                                                               in1=self.RSTD[:, 0:T], op0=ALU.mult, op1=ALU.mult)),
                 reads=[self.kx(c, T), ("RSTD",), ("pvec",)], writes=[dst_keyf(c)])

    def add_to_x(self, T):
        X3 = self.x3(T)

        def evac(j, b):
            self.P.op("dve", (lambda e: e.tensor_tensor(out=X3[:, j, :], in0=X3[:, j, :], in1=self.banks[b][:, 0:T], op=ALU.add)),
                      reads=[("bank", b), self.kx(j, T)], writes=[self.kx(j, T)])
        return evac

    def ffn(self, li, T):
        P = self.P
        H3 = self.b3(0, 0, T)
        hk = lambda c: self.kb(0, 0, c, T)
        self.rmsnorm("norm_ffn", li, T, H3, hk)
        abase = self.UALL[:, 4096:4096 + 2 * 4096].bitcast(BF16)
        A3 = abase[:, 0:NFF * T].rearrange("p (c t) -> p c t", c=NFF)

        def ka(j):
            f0 = j * T // 2
            return ("U", 1 + f0 // 4096, (f0 % 4096) // 2048, f"a{T}", j)
        for (n0, ncols) in [(0, 1024), (1024, 1024), (2048, 768)]:
            s1, v1 = self.load_weight("ffn_w1", li, 0, 8, n0, ncols)
            s3, v3 = self.load_weight("ffn_w3", li, 0, 8, n0, ncols)
            for jj in range(ncols // 128):
                j = n0 // 128 + jj
                b1, b3 = self.bank(), self.bank()
                for k in range(8):
                    self.mm(b1, v1[:, k, jj * 128:(jj + 1) * 128], H3[:, k, :], k == 0, k == 7, [("WS", s1), hk(k)], self.banks[b1][:, 0:T])
                for k in range(8):
                    self.mm(b3, v3[:, k, jj * 128:(jj + 1) * 128], H3[:, k, :], k == 0, k == 7, [("WS", s3), hk(k)], self.banks[b3][:, 0:T])
                t, tk = self.tmp()
                P.op("act", (lambda e, t=t, b1=b1: e.activation(out=t[:, 0:T], in_=self.banks[b1][:, 0:T], func=AF.Silu)),
                     reads=[("bank", b1)], writes=[tk])
                P.op("dve", (lambda e, t=t, b3=b3, j=j: e.tensor_tensor(out=A3[:, j, :], in0=t[:, 0:T], in1=self.banks[b3][:, 0:T], op=ALU.mult)),
                     reads=[tk, ("bank", b3)], writes=[ka(j)])
        ev = self.add_to_x(T)
        for q in range(4):
            self.proj_fm("ffn_w2", li, A3, ka, T, ev, n0=q * 256, ncols=256, nk=NFF)

    def load_x_dma(self, src_rows, T):
        P = self.P
        nb = (T + 127) // 128
        bs = min(128, T)
        XIN = self.Uf(4).rearrange("p (a b) -> p a b", a=4)
        kxin = lambda tb: ("U", 4, tb // 2, "x", tb)
        for tb in range(nb):
            P.op("sp", (lambda e, tb=tb: e.dma_start(out=XIN[0:bs, tb, :], in_=src_rows[tb * bs:(tb + 1) * bs, :])),
                 writes=[kxin(tb)], dma=True)

    def load_x_tr(self, T):
        P = self.P
        nb = (T + 127) // 128
        bs = min(128, T)
        XIN = self.Uf(4).rearrange("p (a b) -> p a b", a=4)
        kxin = lambda tb: ("U", 4, tb // 2, "x", tb)
        X3 = self.x3(T)
        for c in range(NC8):
            b = self.bank()
            for tb in range(nb):
                P.op("pe", (lambda e, b=b, tb=tb, c=c: e.transpose(self.banks[b][:, tb * bs:(tb + 1) * bs],
                                                                  XIN[0:bs, tb, c * 128:(c + 1) * 128], self.ident[0:bs, 0:bs])),
                     reads=[kxin(tb), ("cst",)], writes=[("bank", b)])
            P.op("act", (lambda e, b=b, c=c: e.copy(out=X3[:, c, :], in_=self.banks[b][:, 0:T])),
                 reads=[("bank", b)], writes=[self.kx(c, T)])

    def final_norm(self, T):
        self.rmsnorm("norm_final", 0, T, self.f3(5, T), lambda c: self.kf(5, c, T))

    def store_out(self, dst_rows, T):
        P = self.P
        nb = (T + 127) // 128
        bs = min(128, T)
        Y3 = self.f3(5, T)
        ky = lambda c: self.kf(5, c, T)
        XIN = self.Uf(6).rearrange("p (a b) -> p a b", a=4)
        kxin = lambda tb: ("U", 6, tb // 2, "x", tb)
        for tb in range(nb):
            for half in range(2):
                b = self.bank()
                for cc in range(4):
                    c = half * 4 + cc
                    P.op("pe", (lambda e, b=b, tb=tb, c=c, cc=cc: e.transpose(self.banks[b][0:bs, cc * 128:(cc + 1) * 128],
                                                                             Y3[:, c, tb * bs:(tb + 1) * bs], self.ident[:, :])),
                         reads=[ky(c), ("cst",)], writes=[("bank", b)])
                P.op("act", (lambda e, b=b, tb=tb, half=half: e.copy(out=XIN[0:bs, tb, half * 512:(half + 1) * 512], in_=self.banks[b][0:bs, :])),
                     reads=[("bank", b)], writes=[kxin(tb)])
            P.op("sp", (lambda e, tb=tb: e.dma_start(out=dst_rows[tb * bs:(tb + 1) * bs, :], in_=XIN[0:bs, tb, :])),
                 reads=[kxin(tb)], writes=[("yout", str(dst_rows.name if hasattr(dst_rows, 'name') else 0), tb)], dma=True)

    def tile_io(self, tile):
        kind, sid, t0, T = tile[0], tile[1], tile[2], tile[3]
        if kind == "p":
            return self.xp[t0:t0 + T, :], self.yp[t0:t0 + T, :]
        return self.xs[sid * self.Ls:(sid + 1) * self.Ls, :], self.ys[sid * self.Ls:(sid + 1) * self.Ls, :]

    def do_tile(self, tile, nxt, is_first_tile):
        kind, sid, t0, T, first, last = tile
        src, dst = self.tile_io(tile)
        if is_first_tile:
            self.load_x_dma(src, T)
            self.load_x_tr(T)
        for li in range(self.depth):
            m = li % 4
            if self.mixers[li]:
                if m == 0:
                    self.hgrn2(li // 4, li, T, kind, sid, first, last)
                elif m == 1:
                    self.rwkv7(li // 4, li, T, kind, sid, first, last)
                elif m == 2:
                    self.rglru(li // 4, li, T, kind, sid, first, last)
                elif m == 3:
                    self.conformer(li // 4, li, T, kind, sid, first, last)
            if li == self.depth - 1 and nxt is not None:
                self.load_x_dma(self.tile_io(nxt)[0], nxt[3])
            self.ffn(li, T)
        self.final_norm(T)
        if nxt is not None:
            self.load_x_tr(nxt[3])
        self.store_out(dst, T)

    def hgrn2(self, j, li, T, kind, sid, first, last):
        P = self.P
        nch = T // 64
        H3 = self.b3(0, 0, T)
        hk = lambda c: self.kb(0, 0, c, T)
        self.rmsnorm("norm_mix", li, T, H3, hk)
        if first:
            if kind == "p":
                P.op("pool", lambda e: e.memset(self.SH[:], 0.0), writes=[("SH",)])
            else:
                src = self.st_hg[sid].rearrange("h d v -> d h v")
                P.op("sp", lambda e: e.dma_start(out=self.SH[:], in_=src), writes=[("SH",)], dma=True)
        Q3 = self.b3(0, 1, T)
        kq = lambda c: self.kb(0, 1, c, T)
        self.proj_fm("hg_wq", j, H3, hk, T,
                     lambda jc, b: P.op("act", (lambda e: e.activation(out=Q3[:, jc, :], in_=self.banks[b][:, 0:T], func=AF.Silu)),
                                        reads=[("bank", b)], writes=[kq(jc)]))
        F3 = self.f3(1, T)
        kfF = lambda c: self.kf(1, c, T)

        def evac_f(jc, b):
            P.op("act", (lambda e: e.activation(out=F3[:, jc, :], in_=self.banks[b][:, 0:T], func=AF.Sigmoid)),
                 reads=[("bank", b)], writes=[kfF(jc)])
            P.op("dve", (lambda e: e.tensor_scalar(out=F3[:, jc, :], in0=F3[:, jc, :], scalar1=self.OML[:, jc:jc + 1], scalar2=self.LB[:, jc:jc + 1],
                                                   op0=ALU.mult, op1=ALU.add)),
                 reads=[kfF(jc), ("OML",), ("LB",)], writes=[kfF(jc)])
        self.proj_fm("hg_wf", j, H3, hk, T, evac_f)
        G3 = self.b3(2, 0, T)
        kg = lambda c: self.kb(2, 0, c, T)
        self.proj_fm("hg_wg", j, H3, hk, T,
                     lambda jc, b: P.op("act", (lambda e: e.activation(out=G3[:, jc, :], in_=self.banks[b][:, 0:T], func=AF.Silu)),
                                        reads=[("bank", b)], writes=[kg(jc)]))
        Ff = self.ff(1, T)
        K1f = self.bf(2, 1, T)
        Bf = self.ff(3, T)
        B3 = self.f3(3, T)
        Ef = self.ff(4, T)
        Qf = self.bf(0, 1, T)
        QTf = self.bf(5, 0, T)
        QT3 = self.b3(5, 0, T)
        KTf = self.bf(5, 1, T)
        KT3 = self.b3(5, 1, T)
        P.op("dve", lambda e: e.tensor_scalar(out=K1f, in0=Ff, scalar1=-1.0, scalar2=1.0, op0=ALU.mult, op1=ALU.add),
             reads=[("U", 1)], writes=[("U", 2, 1)])
        P.op("act", lambda e: e.activation(out=Ff, in_=Ff, func=AF.Ln), reads=[("U", 1)], writes=[("U", 1)])
        P.op("dve", lambda e: e.tensor_tensor_scan(out=Bf, data0=self.smask[:, 0:8 * T], data1=Ff, initial=0.0, op0=ALU.mult, op1=ALU.add),
             reads=[("U", 1), ("smask",)], writes=[("U", 3)])
        P.op("act", lambda e: e.activation(out=Ef, in_=Bf, func=AF.Exp), reads=[("U", 3)], writes=[("U", 4)])
        P.op("dve", lambda e: e.scalar_tensor_tensor(out=QTf, in0=Qf, scalar=float(128 ** -0.5), in1=Ef, op0=ALU.mult, op1=ALU.mult),
             reads=[("U", 0, 1), ("U", 4)], writes=[("U", 5, 0)])
        B4 = self.Uf(3)[:, 0:8 * T].rearrange("p (c n t) -> p c n t", c=8, t=64)
        P.op("act", lambda e: e.activation(out=self.GAM[:, :, 0:nch], in_=B4[:, :, :, 63], func=AF.Exp), reads=[("U", 3)], writes=[("GAM",)])
        P.op("dve", lambda e: e.tensor_scalar(out=Ef, in0=Bf, scalar1=-1.0, scalar2=80.0, op0=ALU.mult, op1=ALU.min),
             reads=[("U", 3)], writes=[("U", 4)])
        P.op("act", lambda e: e.activation(out=Ef, in_=Ef, func=AF.Exp), reads=[("U", 4)], writes=[("U", 4)])
        P.op("dve", lambda e: e.tensor_tensor(out=Ef, in0=K1f, in1=Ef, op=ALU.mult), reads=[("U", 2, 1), ("U", 4)], writes=[("U", 4)])
        P.op("act", lambda e: e.copy(out=KTf, in_=Ef), reads=[("U", 4)], writes=[("U", 5, 1)])
        E4 = self.Uf(4)[:, 0:8 * T].rearrange("p (c n t) -> p c n t", c=8, t=64)
        gb4 = self.GAM[:, :, 0:nch].unsqueeze(3).to_broadcast([128, 8, nch, 64])
        P.op("dve", lambda e: e.tensor_tensor(out=E4, in0=E4, in1=gb4, op=ALU.mult), reads=[("U", 4), ("GAM",)], writes=[("U", 4)])
        E3 = self.f3(4, T)
        s_i, v_i = self.load_weight("hg_wi", j, 0, 8, 0, D)
        VALL = self.Ubw(1)[0:64, 0:nch * D].rearrange("p (c n) -> p c n", c=nch)
        KHALL = self.Ubw(6)[0:64, 0:nch * D].rearrange("p (c n) -> p c n", c=nch)
        kv = lambda c: ("U", 1, c // 4, "tm", c)
        kkh = lambda c: ("U", 6, c // 4, "tm", c)
        for c in range(nch):
            for half in range(2):
                b = self.bank()
                for k in range(8):
                    self.mm(b, H3[:, k, c * 64:(c + 1) * 64], v_i[:, k, half * 512:(half + 1) * 512], k == 0, k == 7,
                            [("WS", s_i), hk(k)], self.banks[b][0:64, :])
                P.op("act", (lambda e, b=b, c=c, half=half: e.copy(out=VALL[:, c, half * 512:(half + 1) * 512], in_=self.banks[b][0:64, :])),
                     reads=[("bank", b)], writes=[kv(c)])
            for hh in range(2):
                b = self.bank()
                for h4 in range(4):
                    h = hh * 4 + h4
                    P.op("pe", (lambda e, b=b, h=h, h4=h4, c=c: e.transpose(self.banks[b][0:64, h4 * 128:(h4 + 1) * 128],
                                                                           E3[:, h, c * 64:(c + 1) * 64], self.ident[:, :])),
                         reads=[("U", 4), ("cst",)], writes=[("bank", b)])
                P.op("act", (lambda e, b=b, c=c, hh=hh: e.copy(out=KHALL[:, c, hh * 512:(hh + 1) * 512], in_=self.banks[b][0:64, :])),
                     reads=[("bank", b)], writes=[kkh(c)])
        SBF = self.Ubw(3)[:, 0:nch * 1024].rearrange("p (c h v) -> p c h v", c=nch, h=8)
        ksb = lambda c: ("U", 3, c // 4, "sb", c)
        for c in range(nch):
            P.op("act", (lambda e, c=c: e.copy(out=SBF[:, c, :, :], in_=self.SH[:])), reads=[("SH",)], writes=[ksb(c)])
            bb = []
            for hh in range(2):
                b = self.bank()
                bb.append(b)
                for h4 in range(4):
                    h = hh * 4 + h4
                    self.mm(b, KHALL[:, c, h * 128:(h + 1) * 128], VALL[:, c, h * 128:(h + 1) * 128], True, True,
                            [kkh(c), kv(c)], self.banks[b][:, h4 * 128:(h4 + 1) * 128])
            gam_bc = self.GAM[:, :, c:c + 1].to_broadcast([128, 8, 128])
            P.op("dve", (lambda e, g=gam_bc: e.tensor_tensor(out=self.SH[:], in0=self.SH[:], in1=g, op=ALU.mult)),
                 reads=[("SH",), ("GAM",)], writes=[("SH",)])
            for hh in range(2):
                b = bb[hh]
                shv = self.SH[:, hh * 4:(hh + 1) * 4, :]
                P.op("dve", (lambda e, b=b, shv=shv: e.tensor_tensor(out=shv, in0=shv, in1=self.banks[b][:, :].rearrange("p (h v) -> p h v", h=4), op=ALU.add)),
                     reads=[("SH",), ("bank", b)], writes=[("SH",)])
        if last:
            dsto = (self.o_hg_p[0] if kind == "p" else self.o_hg_s[sid]).rearrange("h d v -> d h v")
            P.op("sp", lambda e, dsto=dsto: e.dma_start(out=dsto, in_=self.SH[:]), reads=[("SH",)], writes=[("o_hg", kind, sid)], dma=True)
        AM = self.Ub(2, 1)[0:64, 0:nch * 512].rearrange("p (c h t) -> p c h t", c=nch, h=8)
        kam = lambda c: ("U", 2, 1, "am", c)
        m64 = self.cs("m64", 64)
        O3 = self.f3(6, T)
        O4 = self.Uf(6)[:, 0:8 * T].rearrange("p (h c t) -> p h c t", h=8, t=64)
        for c in range(nch):
            bs_ = self.bank()
            for h in range(8):
                self.mm(bs_, KT3[:, h, c * 64:(c + 1) * 64], QT3[:, h, c * 64:(c + 1) * 64], True, True,
                        [("U", 5, 1), ("U", 5, 0)], self.banks[bs_][0:64, h * 64:(h + 1) * 64])
            mb = m64.unsqueeze(1).to_broadcast([64, 8, 64])
            P.op("dve", (lambda e, c=c, b=bs_, mb=mb: e.tensor_tensor(out=AM[:, c, :, :], in0=self.banks[b][0:64, :].rearrange("p (h t) -> p h t", h=8),
                                                                     in1=mb, op=ALU.mult)),
                 reads=[("bank", bs_), ("cst",)], writes=[kam(c)])
            bo = self.bank()
            for h in range(8):
                o_ap = self.banks[bo][:, h * 64:(h + 1) * 64]
                self.mm(bo, VALL[:, c, h * 128:(h + 1) * 128], AM[:, c, h, :], True, False, [kv(c), kam(c)], o_ap)
                self.mm(bo, SBF[:, c, h, :], QT3[:, h, c * 64:(c + 1) * 64], False, True, [ksb(c), ("U", 5, 0)], o_ap)
            P.op("act", (lambda e, c=c, bo=bo: e.copy(out=O4[:, :, c, :], in_=self.banks[bo][:, :].rearrange("p (h t) -> p h t", h=8))),
                 reads=[("bank", bo)], writes=[("U", 6)])
        Of = self.ff(6, T)
        SQf = self.bf(0, 1, T)
        SQ3 = self.b3(0, 1, T)
        RS3 = self.f3(1, T)
        RSf = self.ff(1, T)
        P.op("act", lambda e: e.activation(out=SQf, in_=Of, func=AF.Square), reads=[("U", 6)], writes=[("U", 0, 1)])
        for h in range(8):
            b = self.bank()
            self.mm(b, self.ones128[:], SQ3[:, h, :], True, True, [("ones128",), ("U", 0, 1)], self.banks[b][:, 0:T])
            P.op("act", (lambda e, b=b, h=h: e.activation(out=RS3[:, h, :], in_=self.banks[b][:, 0:T], func=AF.Sqrt, bias=self.eps_t[:, 0:1], scale=1.0)),
                 reads=[("bank", b), ("eps",)], writes=[self.kf(1, h, T)])
        P.op("dve", lambda e: e.reciprocal(out=RSf, in_=RSf), reads=[("U", 1)], writes=[("U", 1)])
        P.op("dve", lambda e: e.scalar_tensor_tensor(out=Of, in0=Of, scalar=self.pv("hg_gn", j), in1=RSf, op0=ALU.mult, op1=ALU.mult),
             reads=[("U", 6), ("U", 1), ("pvec",)], writes=[("U", 6)])
        ONf = self.bf(5, 0, T)
        ON3 = self.b3(5, 0, T)
        P.op("dve", lambda e: e.tensor_tensor(out=ONf, in0=Of, in1=self.bf(2, 0, T), op=ALU.mult),
             reads=[("U", 6), ("U", 2, 0)], writes=[("U", 5, 0)])
        self.proj_fm("hg_wo", j, ON3, lambda k: ("U", 5, 0), T, self.add_to_x(T))


    def load_weight_rows(self, name, idx, rows, ncols):
        slot = self._wslot_rr % 3
        self._wslot_rr += 1
        view = self.WS[slot][0:rows, 0:ncols]
        src = self.w_bf[name][idx][0:rows, 0:ncols]
        self.P.op("sp", (lambda e, v=view, s=src: e.dma_start(out=v, in_=s)), reads=[("wbf", name, idx, 0)], writes=[("WS", slot)], dma=True)
        return slot, view

    def rwkv7(self, j, li, T, kind, sid, first, last):
        P = self.P
        nch = T // 64
        bs = min(128, T)
        nblk = T // bs
        nchb = bs // 64
        H3 = self.b3(0, 0, T)
        hk = lambda c: self.kb(0, 0, c, T)
        self.rmsnorm("norm_mix", li, T, H3, hk)
        if first:
            if kind == "p":
                P.op("pool", lambda e: e.memset(self.HST[:], 0.0), writes=[("HST",)])
                P.op("pool", lambda e: e.memset(self.SHIFT[:], 0.0), writes=[("SHIFT",)])
            else:
                P.op("sp", lambda e: e.dma_start(out=self.HST[:], in_=self.st_rw[sid]), writes=[("HST",)], dma=True)
                P.op("sp", lambda e: e.dma_start(out=self.SHIFT[:], in_=self.st_sh[sid]), writes=[("SHIFT",)], dma=True)
        XX3 = self.b3(0, 1, T)
        kxx = ("U", 0, 1)
        P.op("dve", lambda e: e.tensor_tensor(out=XX3[:, :, 1:T], in0=H3[:, :, 0:T - 1], in1=H3[:, :, 1:T], op=ALU.subtract),
             reads=[("U", 0, 0)], writes=[kxx])
        P.op("dve", lambda e: e.tensor_tensor(out=XX3[:, :, 0:1], in0=self.SHIFT[:].unsqueeze(2), in1=H3[:, :, 0:1], op=ALU.subtract),
             reads=[("U", 0, 0), ("SHIFT",)], writes=[kxx])
        P.op("act", lambda e: e.copy(out=self.SHIFT[:].unsqueeze(2), in_=H3[:, :, T - 1:T]), reads=[("U", 0, 0)], writes=[("SHIFT",)])
        if last:
            dsto = self.o_sh_p[0] if kind == "p" else self.o_sh_s[sid]
            P.op("sp", lambda e, dsto=dsto: e.dma_start(out=dsto, in_=self.SHIFT[:]), reads=[("SHIFT",)], writes=[("o_sh", kind, sid)], dma=True)
        self._var_rr = 0

        def variant(n):
            half = self._var_rr % 2
            self._var_rr += 1
            V3 = self.b3(1, half, T)
            for c in range(8):
                P.op("dve", (lambda e, c=c: e.scalar_tensor_tensor(out=V3[:, c, :], in0=XX3[:, c, :], scalar=self.pv("rw_mu", n * 8 + c), in1=H3[:, c, :],
                                                                 op0=ALU.mult, op1=ALU.add)),
                     reads=[kxx, hk(c), ("pvec",)], writes=[self.kb(1, half, c, T)])
            return V3, (lambda c, half=half: self.kb(1, half, c, T))

        def evac_copy(dst3, keyf):
            return lambda jc, b: P.op("act", (lambda e: e.copy(out=dst3[:, jc, :], in_=self.banks[b][:, 0:T])), reads=[("bank", b)], writes=[keyf(jc)])
        R3, K3, V3_, A3, G3 = self.b3(2, 0, T), self.b3(2, 1, T), self.b3(3, 0, T), self.b3(3, 1, T), self.b3(5, 0, T)
        LW3 = self.f3(4, T)
        xv, kxv = variant(0)
        self.proj_fm("rw_wr", j, xv, kxv, T, evac_copy(R3, lambda c: self.kb(2, 0, c, T)))
        xv, kxv = variant(2)
        self.proj_fm("rw_wk", j, xv, kxv, T, evac_copy(K3, lambda c: self.kb(2, 1, c, T)))
        xv, kxv = variant(3)
        self.proj_fm("rw_wv", j, xv, kxv, T, evac_copy(V3_, lambda c: self.kb(3, 0, c, T)))

        def lora(xn, w1name, w2name, hid, hid_func, evac2):
            xv, kxv = variant(xn)
            s1, v1 = self.load_weight(w1name, j, 0, 8, 0, hid)
            b = self.bank()
            for k in range(8):
                self.mm(b, v1[:, k, :], xv[:, k, :], k == 0, k == 7, [("WS", s1), kxv(k)], self.banks[b][0:hid, 0:T])
            P.op("act", (lambda e, b=b: e.activation(out=self.LOR[0:hid, 0:T], in_=self.banks[b][0:hid, 0:T], func=hid_func)),
                 reads=[("bank", b)], writes=[("LOR",)])
            s2, v2 = self.load_weight_rows(w2name, j, hid, D)
            for jc in range(8):
                b2 = self.bank()
                self.mm(b2, v2[:, jc * 128:(jc + 1) * 128], self.LOR[0:hid, 0:T], True, True, [("WS", s2), ("LOR",)], self.banks[b2][:, 0:T])
                evac2(jc, b2)
        lora(1, "rw_w1", "rw_w2", 64, AF.Tanh,
             lambda jc, b: P.op("act", (lambda e: e.activation(out=LW3[:, jc, :], in_=self.banks[b][:, 0:T], func=AF.Sigmoid, bias=self.pv("rw_w0", jc), scale=1.0)),
                                reads=[("bank", b), ("pvec",)], writes=[self.kf(4, jc, T)]))
        lora(4, "rw_a1", "rw_a2", 64, AF.Identity,
             lambda jc, b: P.op("act", (lambda e: e.activation(out=A3[:, jc, :], in_=self.banks[b][:, 0:T], func=AF.Sigmoid, bias=self.pv("rw_a0", jc), scale=1.0)),
                                reads=[("bank", b), ("pvec",)], writes=[self.kb(3, 1, jc, T)]))
        lora(5, "rw_g1", "rw_g2", 128, AF.Sigmoid, evac_copy(G3, lambda c: self.kb(5, 0, c, T)))
        LWf, Bf, E0f, N6f = self.ff(4, T), self.ff(6, T), self.ff(0, T), self.ff(6, T)
        N63 = self.f3(6, T)
        KK3, KKf = self.b3(5, 1, T), self.bf(5, 1, T)
        SQ3, SQf = self.b3(1, 0, T), self.bf(1, 0, T)
        KAPf = self.bf(1, 1, T)
        Rf, Kf, Af = self.bf(2, 0, T), self.bf(2, 1, T), self.bf(3, 1, T)
        P.op("dve", lambda e: e.tensor_scalar(out=LWf, in0=LWf, scalar1=-0.6065306597126334, scalar2=None, op0=ALU.mult), reads=[("U", 4)], writes=[("U", 4)])
        for c in range(8):
            P.op("dve", (lambda e, c=c: e.tensor_scalar(out=KK3[:, c, :], in0=K3[:, c, :], scalar1=self.pv("rw_kk", c), scalar2=None, op0=ALU.mult)),
                 reads=[self.kb(2, 1, c, T), ("pvec",)], writes=[self.kb(5, 1, c, T)])
        P.op("act", lambda e: e.activation(out=SQf, in_=KKf, func=AF.Square), reads=[("U", 5, 1)], writes=[("U", 1, 0)])
        for c in range(8):
            b = self.bank()
            self.mm(b, self.BD1[:], SQ3[:, c, :], True, True, [("BD1",), ("U", 1, 0)], self.banks[b][:, 0:T])
            P.op("act", (lambda e, c=c, b=b: e.activation(out=N63[:, c, :], in_=self.banks[b][:, 0:T], func=AF.Sqrt)), reads=[("bank", b)], writes=[self.kf(6, c, T)])
        P.op("dve", lambda e: e.tensor_scalar(out=N6f, in0=N6f, scalar1=1e-12, scalar2=None, op0=ALU.max), reads=[("U", 6)], writes=[("U", 6)])
        P.op("dve", lambda e: e.reciprocal(out=N6f, in_=N6f), reads=[("U", 6)], writes=[("U", 6)])
        P.op("dve", lambda e: e.tensor_tensor(out=KAPf, in0=KKf, in1=N6f, op=ALU.mult), reads=[("U", 5, 1), ("U", 6)], writes=[("U", 1, 1)])
        for c in range(8):
            P.op("dve", (lambda e, c=c: e.tensor_scalar(out=N63[:, c, :], in0=A3[:, c, :], scalar1=-1.0, scalar2=self.pv("rw_ka", c), op0=ALU.add, op1=ALU.mult)),
                 reads=[self.kb(3, 1, c, T), ("pvec",), ("U", 1, 1)], writes=[self.kf(6, c, T)])
        P.op("dve", lambda e: e.scalar_tensor_tensor(out=Kf, in0=N6f, scalar=1.0, in1=Kf, op0=ALU.add, op1=ALU.mult), reads=[("U", 6), ("U", 2, 1)], writes=[("U", 2, 1)])
        P.op("dve", lambda e: e.tensor_tensor(out=Af, in0=KAPf, in1=Af, op=ALU.mult), reads=[("U", 1, 1), ("U", 3, 1), ("U", 6)], writes=[("U", 3, 1)])
        BON3 = self.b3(5, 1, T)
        for c in range(8):
            P.op("dve", (lambda e, c=c: e.scalar_tensor_tensor(out=BON3[:, c, :], in0=R3[:, c, :], scalar=self.pv("rw_rk", c), in1=K3[:, c, :], op0=ALU.mult, op1=ALU.mult)),
                 reads=[self.kb(2, 0, c, T), ("U", 2, 1), ("pvec",), ("U", 1, 1)], writes=[self.kb(5, 1, c, T)])
            b = self.bank()
            self.mm(b, self.BD1[:], BON3[:, c, :], True, True, [("BD1",), self.kb(5, 1, c, T)], self.banks[b][:, 0:T])
            P.op("dve", (lambda e, c=c, b=b: e.tensor_tensor(out=BON3[:, c, :], in0=self.banks[b][:, 0:T], in1=V3_[:, c, :], op=ALU.mult)),
                 reads=[("bank", b), self.kb(3, 0, c, T)], writes=[self.kb(5, 1, c, T)])
        P.op("dve", lambda e: e.tensor_tensor_scan(out=Bf, data0=self.smask[:, 0:8 * T], data1=LWf, initial=0.0, op0=ALU.mult, op1=ALU.add),
             reads=[("U", 4), ("smask",)], writes=[("U", 6)])
        P.op("dve", lambda e: e.tensor_tensor(out=LWf, in0=Bf, in1=LWf, op=ALU.subtract), reads=[("U", 6), ("U", 4)], writes=[("U", 4)])
        B4 = self.Uf(6)[:, 0:8 * T].rearrange("p (c n t) -> p c n t", c=8, t=64)
        P.op("act", lambda e: e.activation(out=self.GAM[:, :, 0:nch], in_=B4[:, :, :, 63], func=AF.Exp), reads=[("U", 6)], writes=[("GAM",)])
        PAIR1 = self.Ubw(1).rearrange("p (a x) -> p a x", a=2)[:, :, 0:8 * T].rearrange("p a (c t) -> p a c t", c=8)
        PAIR2 = self.Ubw(2).rearrange("p (a x) -> p a x", a=2)[:, :, 0:8 * T].rearrange("p a (c t) -> p a c t", c=8)
        P.op("act", lambda e: e.activation(out=LWf, in_=LWf, func=AF.Exp), reads=[("U", 4)], writes=[("U", 4)])
        P.op("dve", lambda e: e.tensor_tensor(out=self.bf(1, 0, T), in0=KAPf, in1=LWf, op=ALU.mult), reads=[("U", 1, 1), ("U", 4)], writes=[("U", 1, 0)])
        P.op("act", lambda e: e.activation(out=E0f, in_=Bf, func=AF.Exp), reads=[("U", 6)], writes=[("U", 0)])
        P.op("dve", lambda e: e.tensor_tensor(out=self.bf(1, 1, T), in0=Rf, in1=E0f, op=ALU.mult), reads=[("U", 2, 0), ("U", 0), ("U", 1, 0), ("U", 5, 1)], writes=[("U", 1, 1)])
        P.op("act", lambda e: e.activation(out=E0f, in_=Bf, func=AF.Exp, scale=-1.0), reads=[("U", 6), ("U", 1, 1)], writes=[("U", 0)])
        P.op("dve", lambda e: e.tensor_tensor(out=self.bf(2, 0, T), in0=Af, in1=E0f, op=ALU.mult), reads=[("U", 3, 1), ("U", 0), ("U", 1, 1)], writes=[("U", 2, 0)])
        P.op("dve", lambda e: e.tensor_tensor(out=Kf, in0=Kf, in1=E0f, op=ALU.mult), reads=[("U", 2, 1), ("U", 0)], writes=[("U", 2, 1)])
        O3 = self.f3(6, T)
        maska = self.cs("maska")[0:bs, :].rearrange("p (a t) -> p a t", a=2)[:, :, 0:bs]
        maskb = self.cs("maskb")[0:bs, :].rearrange("p (a t) -> p a t", a=2)[:, :, 0:bs]
        mui = self.cs("maskb")[0:bs, 0:bs]
        idp = self.cs("idp")
        identb = self.ident[0:bs, 0:bs]
        TMall = self.Ubw(4)

        def gset(g):
            if g == 0:
                bb_ = self.UALL[:, 0:4096].bitcast(BF16)
                bf_ = self.UALL[:, 0:4096]
                kp = ("U", 0)
            else:
                bb_ = self.WS[g - 1][:, :]
                bf_ = self.WS[g - 1][:, :].bitcast(F32)
                kp = ("WS", g - 1)
            S = {}
            kk_ = lambda i: kp + ("rw", i)
            if INV_BF16:
                S["N2"] = [bb_[0:bs, 0:1024].rearrange("p (u a t) -> p u a t", u=4, a=2)[:, :, :, 0:bs],
                           bb_[0:bs, 2048:3072].rearrange("p (u a t) -> p u a t", u=4, a=2)[:, :, :, 0:bs]]
                S["PT"] = bb_[0:bs, 4096:4608].rearrange("p (u t) -> p u t", u=4)[:, :, 0:bs]
                S["PTb"] = S["PT"]
                S["kPTb"] = kk_(2)
            else:
                S["N2"] = [bf_[0:bs, 0:1024].rearrange("p (u a t) -> p u a t", u=4, a=2)[:, :, :, 0:bs],
                           bf_[0:bs, 1024:2048].rearrange("p (u a t) -> p u a t", u=4, a=2)[:, :, :, 0:bs]]
                S["PT"] = bf_[0:bs, 2048:2560].rearrange("p (u t) -> p u t", u=4)[:, :, 0:bs]
                S["PTb"] = bb_[0:bs, 5120:5632].rearrange("p (u t) -> p u t", u=4)[:, :, 0:bs]
                S["kPTb"] = kk_(3)
            S["kN2"] = [kk_(0), kk_(1)]
            S["kPT"] = kk_(2)
            S["DB"] = bb_[0:bs, 5632:6656].rearrange("p (u a t) -> p u a t", u=4, a=2)[:, :, :, 0:bs]
            S["kDB"] = kk_(4)
            S["CMT"] = bb_[0:bs, 6656:7168].rearrange("p (u t) -> p u t", u=4)[:, :, 0:bs]
            S["kCMT"] = kk_(5)
            S["X1b"] = bb_[0:bs, 7168:7680].rearrange("p (u t) -> p u t", u=4)[:, :, 0:bs]
            S["kX1"] = kk_(6)
            S["X2b"] = bb_[0:bs, 7680:7936].rearrange("p (u k) -> p u k", u=4)
            S["kX2"] = kk_(7)
            S["WT"] = bb_[0:bs, 0:512].rearrange("p (u t) -> p u t", u=4)[:, :, 0:bs]
            S["KS"] = bb_[0:bs, 512:768].rearrange("p (u k) -> p u k", u=4)
            S["MT0"] = bb_[:, 768:1280].rearrange("p (a k) -> p a k", k=64)
            S["HSB"] = bb_[:, 1280:1536].rearrange("p (n c v) -> p n c v", n=2, c=2)
            S["kAL"] = kk_(0)
            return S
        sets = [gset(g) for g in range(4)]
        for bl in range(nblk):
            tb = bl * bs
            par = bl % 2
            TM = [TMall[0:bs, par * 4096 + a * 1024:par * 4096 + (a + 1) * 1024].rearrange("p (c n) -> p c n", c=8) for a in range(4)]
            kTM = [("U", 4, par, "rw", a) for a in range(4)]
            srcs = [(PAIR1[:, 0], ("U", 1, 0)), (PAIR2[:, 0], ("U", 2, 0)), (PAIR2[:, 1], ("U", 2, 1)), (V3_, ("U", 3, 0))]
            for a in range(4):
                b = self.bank()
                pb = self.banks[b][0:bs, :].bitcast(BF16)
                for c in range(8):
                    P.op("pe", (lambda e, pb=pb, c=c, sv=srcs[a][0][:, c, tb:tb + bs]: e.transpose(pb[:, c * 128:(c + 1) * 128], sv, self.IDB[:, :])),
                         reads=[srcs[a][1], ("IDB",)], writes=[("bank", b)])
                P.op("act", (lambda e, pb=pb, tmv=TM[a].rearrange("p c n -> p (c n)"): e.copy(out=tmv, in_=pb[:, 0:1024])), reads=[("bank", b)], writes=[kTM[a]])
            KBT, ABT, KKT, VT = TM
            U4 = lambda bk: self.banks[bk][0:bs, 0:4 * bs].rearrange("p (u t) -> p u t", u=4)
            units_of = lambda g: [(2 * g + cl, e_, cl * 2 + e_) for cl in range(2) for e_ in range(2)]
            st = [dict() for _ in range(4)]

            def s_scores(g):
                S = sets[g]
                bC = [self.bank(), self.bank()]
                for (c, e_, i4) in units_of(g):
                    cl = c - 2 * g
                    ps = slice(64 * e_, 64 * e_ + 64)
                    rw_ = (64 * e_, 64 * e_ + 64)
                    b = self.bank()
                    self.mm(b, PAIR2[ps, 0, c, tb:tb + bs], PAIR1[ps, :, c, tb:tb + bs], True, True, [("U", 2, 0), ("U", 1)],
                            self.banks[b][0:bs, 0:2 * bs].rearrange("p (a t) -> p a t", a=2), rows=rw_)
                    self.mm(b, PAIR1[ps, 0, c, tb:tb + bs], PAIR2[ps, :, c, tb:tb + bs], True, True, [("U", 1, 0), ("U", 2)],
                            self.banks[b][0:bs, 2 * bs:4 * bs].rearrange("p (a t) -> p a t", a=2), rows=rw_)
                    self.mm(bC[e_], PAIR2[ps, 1, c, tb:tb + bs], PAIR1[ps, 1, c, tb:tb + bs], True, True, [("U", 2, 1), ("U", 1, 1)],
                            self.banks[bC[e_]][0:bs, cl * bs:(cl + 1) * bs], rows=rw_)
                    bk4 = self.banks[b][0:bs, 0:4 * bs].rearrange("p (x y t) -> p x y t", x=2, y=2)
                    P.op("dve", (lambda e, bk4=bk4, o_=S["N2"][0][:, i4]: e.tensor_tensor(out=o_, in0=bk4[:, :, 0, :], in1=maska, op=ALU.mult)),
                         reads=[("bank", b), ("cst",)], writes=[S["kN2"][0]])
                    P.op("dve", (lambda e, bk4=bk4, o_=S["DB"][:, i4]: e.tensor_tensor(out=o_, in0=bk4[:, :, 1, :], in1=maskb, op=ALU.mult)),
                         reads=[("bank", b), ("cst",)], writes=[S["kDB"]])
                cm4 = S["CMT"].rearrange("p (c e) t -> p c e t", e=2)
                for e_ in range(2):
                    P.op("dve", (lambda e, bk=bC[e_], o_=cm4[:, :, e_, :]: e.tensor_tensor(out=o_, in0=self.banks[bk][0:bs, 0:2 * bs].rearrange("p (c t) -> p c t", c=2),
                                                                                        in1=mui.unsqueeze(1).to_broadcast([bs, 2, bs]), op=ALU.mult)),
                         reads=[("bank", bC[e_]), ("cst",)], writes=[S["kCMT"]])
                P.op("dve", (lambda e, S=S: e.tensor_tensor(out=S["PT"], in0=S["N2"][0][:, :, 0, :], in1=identb.unsqueeze(1).to_broadcast([bs, 4, bs]), op=ALU.add)),
                     reads=[S["kN2"][0], ("cst",)], writes=[S["kPT"]])
                st[g]["cur"] = 0

            def s_level(g, lvl):
                S = sets[g]
                N2, kN2 = S["N2"], S["kN2"]
                cur = st[g]["cur"]
                nxt = 1 - cur
                bL = self.bank()
                bLT = self.bank() if lvl < 5 else None
                for (c, e_, i4) in units_of(g):
                    self.mm(bL, N2[cur][:, i4, 0, :], N2[cur][:, i4, 1, :], True, True, [kN2[cur]], self.banks[bL][0:bs, i4 * bs:(i4 + 1) * bs], rows=(0, bs))
                    if lvl < 5:
                        self.mm(bLT, N2[cur][:, i4, 1, :], N2[cur][:, i4, 0, :], True, True, [kN2[cur]], self.banks[bLT][0:bs, i4 * bs:(i4 + 1) * bs], rows=(0, bs))
                P.op("act", (lambda e, bL=bL, o_=N2[nxt][:, :, 1, :]: e.copy(out=o_, in_=U4(bL))), reads=[("bank", bL)], writes=[kN2[nxt]])
                if lvl < 5:
                    P.op("act", (lambda e, bLT=bLT, o_=N2[nxt][:, :, 0, :]: e.copy(out=o_, in_=U4(bLT))), reads=[("bank", bLT)], writes=[kN2[nxt]])
                bP = self.bank()
                for (c, e_, i4) in units_of(g):
                    self.mm(bP, N2[nxt][:, i4, 1, :], S["PT"][:, i4, :], True, True, [kN2[nxt], S["kPT"]], self.banks[bP][0:bs, i4 * bs:(i4 + 1) * bs], rows=(0, bs))
                P.op("dve", (lambda e, bP=bP, S=S: e.tensor_tensor(out=S["PT"], in0=S["PT"], in1=U4(bP), op=ALU.add)), reads=[("bank", bP), S["kPT"]], writes=[S["kPT"]])
                st[g]["cur"] = nxt

            def s_x12(g):
                S = sets[g]
                if not INV_BF16:
                    P.op("act", (lambda e, S=S: e.copy(out=S["PTb"], in_=S["PT"])), reads=[S["kPT"]], writes=[S["kPTb"]])
                b1, b2 = self.bank(), self.bank()
                for (c, e_, i4) in units_of(g):
                    self.mm(b1, S["PTb"][:, i4, :], S["DB"][:, i4, 1, :], True, True, [S["kPTb"], S["kDB"]], self.banks[b1][0:bs, i4 * bs:(i4 + 1) * bs], rows=(0, bs))
                    self.mm(b2, S["PTb"][:, i4, :], KBT[:, c, 64 * e_:64 * e_ + 64], True, True, [S["kPTb"], kTM[0]], self.banks[b2][0:bs, i4 * 64:(i4 + 1) * 64], rows=(0, bs))
                P.op("act", (lambda e, b1=b1, S=S: e.copy(out=S["X1b"], in_=U4(b1))), reads=[("bank", b1)], writes=[S["kX1"]])
                P.op("act", (lambda e, b2=b2, S=S: e.copy(out=S["X2b"], in_=self.banks[b2][0:bs, 0:256].rearrange("p (u k) -> p u k", u=4))), reads=[("bank", b2)], writes=[S["kX2"]])

            def s_wkr(g):
                S = sets[g]
                bW, bK, bR = self.bank(), self.bank(), self.bank()
                for (c, e_, i4) in units_of(g):
                    cl = c - 2 * g
                    self.mm(bW, S["X1b"][:, i4, :], S["DB"][:, i4, 0, :], True, True, [S["kX1"], S["kDB"]], self.banks[bW][0:bs, i4 * bs:(i4 + 1) * bs], rows=(0, bs))
                    self.mm(bK, S["X1b"][:, i4, :], ABT[:, c, 64 * e_:64 * e_ + 64], True, True, [S["kX1"], kTM[1]], self.banks[bK][0:bs, i4 * 64:(i4 + 1) * 64], rows=(0, bs))
                    self.mm(bR, S["X2b"][:, i4, :], S["DB"][:, i4, 0, :], True, True, [S["kX2"], S["kDB"]], self.banks[bR][64 * e_:64 * e_ + 64, cl * bs:(cl + 1) * bs], rows=(0, bs))
                P.op("dve", (lambda e, bW=bW, S=S: e.tensor_tensor(out=S["WT"], in0=S["CMT"], in1=U4(bW), op=ALU.subtract)),
                     reads=[("bank", bW), S["kCMT"], S["kN2"][1]], writes=[S["kAL"]])
                P.op("dve", (lambda e, bK=bK, S=S, kkv=KKT[:, 2 * g:2 * g + 2, :].rearrange("p c (e k) -> p (c e) k", e=2): e.tensor_tensor(
                    out=S["KS"], in0=kkv, in1=self.banks[bK][0:bs, 0:256].rearrange("p (u k) -> p u k", u=4), op=ALU.subtract)),
                    reads=[("bank", bK), kTM[2]], writes=[S["kAL"]])
                rv = PAIR1[:, 1, 2 * g:2 * g + 2, tb:tb + bs]
                P.op("dve", (lambda e, bR=bR, rv=rv: e.tensor_tensor(out=rv, in0=rv, in1=self.banks[bR][:, 0:2 * bs].rearrange("p (c t) -> p c t", c=2), op=ALU.subtract)),
                     reads=[("bank", bR), ("U", 1, 1)], writes=[("U", 1, 1)])

            def s_mt0(g):
                S = sets[g]
                nsl = nchb * 2
                for jj in range(nchb):
                    bM = self.bank()
                    tp = slice(64 * jj, 64 * jj + 64)
                    for (c, e_, i4) in units_of(g):
                        cl = c - 2 * g
                        self.mm(bM, S["X2b"][tp, i4, :], ABT[tp, c, 64 * e_:64 * e_ + 64], True, True, [S["kX2"], kTM[1]],
                                self.banks[bM][64 * e_:64 * e_ + 64, cl * 64:(cl + 1) * 64], rows=(64 * jj, 64 * jj + 64))
                    P.op("dve", (lambda e, bM=bM, o_=S["MT0"][:, jj * 2:jj * 2 + 2, :]: e.tensor_tensor(out=o_, in0=idp.unsqueeze(1).to_broadcast([128, 2, 64]),
                                                                                                   in1=self.banks[bM][:, 0:128].rearrange("p (a k) -> p a k", k=64), op=ALU.subtract)),
                         reads=[("bank", bM), ("cst",)], writes=[S["kAL"]])

            def s_chain(g, jj):
                S = sets[g]
                ch = bl * nchb + jj
                tp = slice(64 * jj, 64 * jj + 64)
                P.op("act", (lambda e, o_=S["HSB"][:, jj, :, :]: e.copy(out=o_, in_=self.HST[:, 2 * g:2 * g + 2, :])), reads=[("HST", g)], writes=[S["kAL"]])
                bG = [self.bank(), self.bank()]
                for (c, e_, i4) in units_of(g):
                    cl = c - 2 * g
                    ps = slice(64 * e_, 64 * e_ + 64)
                    oap = self.banks[bG[e_]][ps, cl * 64:(cl + 1) * 64]
                    self.mm(bG[e_], S["KS"][tp, i4, :], VT[tp, c, 64 * e_:64 * e_ + 64], True, False, [S["kAL"], kTM[3]], oap, rows=(64 * jj, 64 * jj + 64))
                    self.mm(bG[e_], S["MT0"][ps, jj * 2 + cl, :], S["HSB"][ps, jj, cl, :], False, True, [S["kAL"]], oap, rows=(64 * e_, 64 * e_ + 64))
                for e_ in range(2):
                    ps = slice(64 * e_, 64 * e_ + 64)
                    gm = self.GAM[ps, 2 * g:2 * g + 2, ch:ch + 1].to_broadcast([64, 2, 64])
                    P.op("dve", (lambda e, bk=bG[e_], ps=ps, gm=gm: e.tensor_tensor(out=self.HST[ps, 2 * g:2 * g + 2, :], in0=self.banks[bk][ps, 0:128].rearrange("p (c v) -> p c v", c=2),
                                                                                  in1=gm, op=ALU.mult)),
                         reads=[("bank", bG[e_]), ("GAM",)], writes=[("HST", g, e_)])

            def s_out(g):
                S = sets[g]
                bO = [self.bank(), self.bank()]
                for (c, e_, i4) in units_of(g):
                    cl = c - 2 * g
                    ps = slice(64 * e_, 64 * e_ + 64)
                    self.mm(bO[e_], VT[:, c, 64 * e_:64 * e_ + 64], S["WT"][:, i4, :], True, False, [kTM[3], S["kAL"]], self.banks[bO[e_]][ps, cl * bs:(cl + 1) * bs], rows=(0, bs))
                    for jj in range(nchb):
                        self.mm(bO[e_], S["HSB"][ps, jj, cl, :], PAIR1[ps, 1, c, tb + 64 * jj:tb + 64 * jj + 64], False, jj == nchb - 1, [S["kAL"], ("U", 1, 1)],
                                self.banks[bO[e_]][ps, cl * bs + 64 * jj:cl * bs + 64 * jj + 64], rows=(64 * e_, 64 * e_ + 64))
                for e_ in range(2):
                    ps = slice(64 * e_, 64 * e_ + 64)
                    P.op("act", (lambda e, bk=bO[e_], ps=ps, ov=O3[ps, 2 * g:2 * g + 2, tb:tb + bs]: e.copy(out=ov, in_=self.banks[bk][ps, 0:2 * bs].rearrange("p (c t) -> p c t", c=2))),
                         reads=[("bank", bO[e_])], writes=[("U", 6, "o", g, e_)])
            stages = [s_scores] + [(lambda g, l=l: s_level(g, l)) for l in range(1, 6)] + [s_x12, s_wkr, s_mt0] + \
                     [(lambda g, jj=jj: s_chain(g, jj)) for jj in range(nchb)] + [s_out]
            for sf in stages:
                for g in range(4):
                    sf(g)
        if last:
            dsto = self.o_rw_p[0] if kind == "p" else self.o_rw_s[sid]
            P.op("sp", lambda e, dsto=dsto: e.dma_start(out=dsto, in_=self.HST[:]), reads=[("HST",)], writes=[("o_rw", kind, sid)], dma=True)
        bdm = self.cs("bdm")
        OUT3 = self.b3(3, 1, T)
        for c in range(8):
            b = self.bank()
            self.mm(b, bdm, O3[:, c, :], True, True, [("cst",), ("U", 6)], self.banks[b][:, 0:T])
            P.op("dve", (lambda e, c=c, b=b: e.tensor_tensor(out=O3[:, c, :], in0=O3[:, c, :], in1=self.banks[b][:, 0:T], op=ALU.subtract)),
                 reads=[("bank", b), ("U", 6)], writes=[self.kf(6, c, T)])
            t, tk = self.tmp()
            P.op("act", (lambda e, c=c, t=t: e.activation(out=t[:, 0:T], in_=O3[:, c, :], func=AF.Square)), reads=[self.kf(6, c, T)], writes=[tk])
            b2 = self.bank()
            self.mm(b2, bdm, t[:, 0:T], True, True, [("cst",), tk], self.banks[b2][:, 0:T])
            P.op("act", (lambda e, b2=b2, t=t: e.activation(out=t[:, 0:T], in_=self.banks[b2][:, 0:T], func=AF.Sqrt, bias=self.gn_eps[:, 0:1], scale=1.0)),
                 reads=[("bank", b2), ("eps",)], writes=[tk])
            P.op("dve", (lambda e, t=t: e.reciprocal(out=t[:, 0:T], in_=t[:, 0:T])), reads=[tk], writes=[tk])
            P.op("dve", (lambda e, c=c, t=t: e.scalar_tensor_tensor(out=O3[:, c, :], in0=O3[:, c, :], scalar=self.pv("rw_ln_g", c), in1=t[:, 0:T], op0=ALU.mult, op1=ALU.mult)),
                 reads=[tk, self.kf(6, c, T), ("pvec",)], writes=[self.kf(6, c, T)])
            P.op("dve", (lambda e, c=c: e.scalar_tensor_tensor(out=O3[:, c, :], in0=O3[:, c, :], scalar=self.pv("rw_ln_b", c), in1=BON3[:, c, :], op0=ALU.add, op1=ALU.add)),
                 reads=[self.kf(6, c, T), self.kb(5, 1, c, T), ("pvec",)], writes=[self.kf(6, c, T)])
            P.op("dve", (lambda e, c=c: e.tensor_tensor(out=OUT3[:, c, :], in0=O3[:, c, :], in1=G3[:, c, :], op=ALU.mult)),
                 reads=[self.kf(6, c, T), self.kb(5, 0, c, T)], writes=[self.kb(3, 1, c, T)])
        self.proj_fm("rw_wo", j, OUT3, lambda k: self.kb(3, 1, k, T), T, self.add_to_x(T))

    def halo_view(self, halo, T):
        n = 8 * (halo + T)
        return self.UALL[:, 4096:4096 + n].rearrange("p (c t) -> p c t", c=8)

    def rglru(self, j, li, T, kind, sid, first, last):
        P = self.P
        H3 = self.b3(0, 0, T)
        hk = lambda c: self.kb(0, 0, c, T)
        self.rmsnorm("norm_mix", li, T, H3, hk)
        if first:
            if kind == "p":
                P.op("pool", lambda e: e.memset(self.LH[:], 0.0), writes=[("LH",)])
                P.op("pool", lambda e: e.memset(self.LC[:], 0.0), writes=[("LC",)])
            else:
                P.op("sp", lambda e: e.dma_start(out=self.LH[:], in_=self.st_lh[sid]), writes=[("LH",)], dma=True)
                P.op("sp", lambda e: e.dma_start(out=self.LC[:], in_=self.st_lc[sid]), writes=[("LC",)], dma=True)
        XWH = self.halo_view(3, T)
        kxw = ("U", 1)
        kxw2 = ("U", 2, 0)
        Y3 = self.b3(0, 1, T)
        ky = lambda c: self.kb(0, 1, c, T)
        U3 = self.f3(3, T)
        UB3 = self.b3(2, 1, T)
        R3 = self.f3(4, T)
        I3 = self.f3(5, T)
        M3 = self.f3(6, T)
        P.op("pool", lambda e: e.tensor_copy(out=XWH[:, :, 0:3], in_=self.LC[:]), reads=[("LC",)], writes=[kxw, kxw2])
        def evac_y(jc, b):
            ps = self.banks[b][:, 0:T]
            t, tk = self.tmp()
            P.op("act", (lambda e: e.activation(out=t[:, 0:T], in_=ps, func=AF.Square)), reads=[("bank", b)], writes=[tk])
            P.op("dve", (lambda e: e.tensor_scalar(out=t[:, 0:T], in0=t[:, 0:T], scalar1=0.044715, scalar2=1.0, op0=ALU.mult, op1=ALU.add)), reads=[tk], writes=[tk])
            P.op("dve", (lambda e: e.tensor_tensor(out=t[:, 0:T], in0=t[:, 0:T], in1=ps, op=ALU.mult)), reads=[tk, ("bank", b)], writes=[tk])
            P.op("act", (lambda e: e.activation(out=t[:, 0:T], in_=t[:, 0:T], func=AF.Sigmoid, scale=1.5957691216057308)), reads=[tk], writes=[tk])
            P.op("dve", (lambda e: e.tensor_tensor(out=Y3[:, jc, :], in0=t[:, 0:T], in1=ps, op=ALU.mult)), reads=[tk, ("bank", b)], writes=[ky(jc)])
        self.proj_fm("lru_wy", j, H3, hk, T, evac_y)
        self.proj_fm("lru_wx", j, H3, hk, T,
                     lambda jc, b: P.op("act", (lambda e: e.copy(out=XWH[:, jc, 3:3 + T], in_=self.banks[b][:, 0:T])),
                                        reads=[("bank", b)], writes=[kxw, kxw2]))
        P.op("pool", lambda e: e.tensor_copy(out=self.LC[:], in_=XWH[:, :, T:T + 3]), reads=[kxw, kxw2], writes=[("LC",)])
        if last:
            dsto = self.o_lc_p[0] if kind == "p" else self.o_lc_s[sid]
            P.op("sp", lambda e, dsto=dsto: e.dma_start(out=dsto, in_=self.LC[:]), reads=[("LC",)], writes=[("o_lc", kind, sid)], dma=True)
        for c in range(8):
            cw = lambda jj, c=c: self.pv("lru_conv_w", jj * 8 + c)
            P.op("dve", (lambda e, c=c, cw=cw: e.tensor_scalar(out=U3[:, c, :], in0=XWH[:, c, 3:3 + T], scalar1=cw(3), scalar2=self.pv("lru_conv_b", c),
                                                             op0=ALU.mult, op1=ALU.add)), reads=[kxw, kxw2, ("pvec",)], writes=[self.kf(3, c, T)])
            for jj in range(3):
                P.op("dve", (lambda e, c=c, cw=cw, jj=jj: e.scalar_tensor_tensor(out=U3[:, c, :], in0=XWH[:, c, jj:jj + T], scalar=cw(jj), in1=U3[:, c, :],
                                                                                 op0=ALU.mult, op1=ALU.add)),
                     reads=[kxw, kxw2, ("pvec",), self.kf(3, c, T)], writes=[self.kf(3, c, T)])
            P.op("act", (lambda e, c=c: e.copy(out=UB3[:, c, :], in_=U3[:, c, :])), reads=[self.kf(3, c, T)], writes=[self.kb(2, 1, c, T)])
        sga, vga = self.load_weight("lru_ga_w", j, 0, 8, 0, 128)
        sgx, vgx = self.load_weight("lru_gx_w", j, 0, 8, 0, 128)
        for c in range(8):
            b = self.bank()
            self.mm(b, vga[:, c, :], UB3[:, c, :], True, True, [("WS", sga), self.kb(2, 1, c, T)], self.banks[b][:, 0:T])
            P.op("act", (lambda e, c=c, b=b: e.activation(out=R3[:, c, :], in_=self.banks[b][:, 0:T], func=AF.Sigmoid, bias=self.pv("lru_ga_b", c), scale=1.0)),
                 reads=[("bank", b), ("pvec",)], writes=[self.kf(4, c, T)])
            b2 = self.bank()
            self.mm(b2, vgx[:, c, :], UB3[:, c, :], True, True, [("WS", sgx), self.kb(2, 1, c, T)], self.banks[b2][:, 0:T])
            P.op("act", (lambda e, c=c, b2=b2: e.activation(out=I3[:, c, :], in_=self.banks[b2][:, 0:T], func=AF.Sigmoid, bias=self.pv("lru_gx_b", c), scale=1.0)),
                 reads=[("bank", b2), ("pvec",)], writes=[self.kf(5, c, T)])
            P.op("act", (lambda e, c=c: e.activation(out=M3[:, c, :], in_=R3[:, c, :], func=AF.Exp, scale=self.NSP2[:, c:c + 1])),
                 reads=[self.kf(4, c, T), ("NSP",)], writes=[self.kf(6, c, T)])
            P.op("act", (lambda e, c=c: e.activation(out=M3[:, c, :], in_=M3[:, c, :], func=AF.Sqrt, scale=-1.0, bias=self.one_t[:, 0:1])),
                 reads=[self.kf(6, c, T), ("eps",)], writes=[self.kf(6, c, T)])
            P.op("act", (lambda e, c=c: e.activation(out=R3[:, c, :], in_=R3[:, c, :], func=AF.Exp, scale=self.NSP[:, c:c + 1])),
                 reads=[self.kf(4, c, T), ("NSP",)], writes=[self.kf(4, c, T)])
            P.op("dve", (lambda e, c=c: e.tensor_tensor(out=I3[:, c, :], in0=I3[:, c, :], in1=U3[:, c, :], op=ALU.mult)),
                 reads=[self.kf(5, c, T), self.kf(3, c, T)], writes=[self.kf(5, c, T)])
            P.op("dve", (lambda e, c=c: e.tensor_tensor(out=I3[:, c, :], in0=I3[:, c, :], in1=M3[:, c, :], op=ALU.mult)),
                 reads=[self.kf(5, c, T), self.kf(6, c, T)], writes=[self.kf(5, c, T)])
            P.op("dve", (lambda e, c=c: e.tensor_tensor_scan(out=U3[:, c, :], data0=R3[:, c, :], data1=I3[:, c, :], initial=self.LH[:, c:c + 1],
                                                            op0=ALU.mult, op1=ALU.add)),
                 reads=[self.kf(4, c, T), self.kf(5, c, T), ("LH",)], writes=[self.kf(3, c, T)])
        P.op("pool", lambda e: e.tensor_copy(out=self.LH[:], in_=U3[:, :, T - 1]), reads=[("U", 3)], writes=[("LH",)])
        if last:
            dsto = self.o_lh_p[0] if kind == "p" else self.o_lh_s[sid]
            P.op("sp", lambda e, dsto=dsto: e.dma_start(out=dsto, in_=self.LH[:]), reads=[("LH",)], writes=[("o_lh", kind, sid)], dma=True)
        OUTf = self.bf(2, 1, T)
        OUT3 = self.b3(2, 1, T)
        P.op("dve", lambda e: e.tensor_tensor(out=OUTf, in0=self.ff(3, T), in1=self.bf(0, 1, T), op=ALU.mult),
             reads=[("U", 3), ("U", 0, 1)], writes=[("U", 2, 1)])
        self.proj_fm("lru_wo", j, OUT3, lambda k: ("U", 2, 1), T, self.add_to_x(T))

    def conformer(self, j, li, T, kind, sid, first, last):
        P = self.P
        H3 = self.b3(0, 0, T)
        hk = lambda c: self.kb(0, 0, c, T)
        self.rmsnorm("norm_mix", li, T, H3, hk)
        if first:
            if kind == "p":
                P.op("pool", lambda e: e.memset(self.CH[:], 0.0), writes=[("CH",)])
            else:
                P.op("sp", lambda e: e.dma_start(out=self.CH[:], in_=self.st_cf[sid]), writes=[("CH",)], dma=True)
        UH = self.Ubw(1)[:, 0:8 * (30 + T)].rearrange("p (c t) -> p c t", c=8)
        kuh = ("U", 1)
        P.op("pool", lambda e: e.tensor_copy(out=UH[:, :, 0:30], in_=self.CH[:]), reads=[("CH",)], writes=[kuh])
        sa, va = self.load_weight("cf_w1", j, 0, 8, 0, D)
        sg, vg = self.load_weight("cf_w1", j, 0, 8, D, D)
        for c in range(8):
            ba, bg = self.bank(), self.bank()
            for k in range(8):
                self.mm(ba, va[:, k, c * 128:(c + 1) * 128], H3[:, k, :], k == 0, k == 7, [("WS", sa), hk(k)], self.banks[ba][:, 0:T])
            for k in range(8):
                self.mm(bg, vg[:, k, c * 128:(c + 1) * 128], H3[:, k, :], k == 0, k == 7, [("WS", sg), hk(k)], self.banks[bg][:, 0:T])
            t, tk = self.tmp()
            P.op("act", (lambda e, t=t, bg=bg, c=c: e.activation(out=t[:, 0:T], in_=self.banks[bg][:, 0:T], func=AF.Sigmoid, bias=self.pv("cf_b1", 8 + c), scale=1.0)),
                 reads=[("bank", bg), ("pvec",)], writes=[tk])
            P.op("dve", (lambda e, t=t, ba=ba, c=c: e.scalar_tensor_tensor(out=UH[:, c, 30:30 + T], in0=self.banks[ba][:, 0:T], scalar=self.pv("cf_b1", c),
                                                                          in1=t[:, 0:T], op0=ALU.add, op1=ALU.mult)),
                 reads=[("bank", ba), tk, ("pvec",)], writes=[kuh + (0, "uh", c)])
        P.op("pool", lambda e: e.tensor_copy(out=self.CH[:], in_=UH[:, :, T:T + 30]), reads=[kuh], writes=[("CH",)])
        if last:
            dsto = self.o_cf_p[0] if kind == "p" else self.o_cf_s[sid]
            P.op("sp", lambda e, dsto=dsto: e.dma_start(out=dsto, in_=self.CH[:]), reads=[("CH",)], writes=[("o_cf", kind, sid)], dma=True)
        C3 = self.f3(3, T)
        for c in range(8):
            slot = self._wslot_rr % 3
            self._wslot_rr += 1
            dgv = self.WS[slot][:, 0:31 * 128].rearrange("p (j m) -> p j m", j=31)
            P.op("sp", (lambda e, dgv=dgv, c=c: e.dma_start(out=dgv.rearrange("p j m -> p (j m)"), in_=self.dg_d[c])), reads=[("dg", c)], writes=[("WS", slot)], dma=True)
            b = self.bank()
            for jj in range(31):
                self.mm(b, dgv[:, jj, :], UH[:, c, jj:jj + T], jj == 0, jj == 30, [("WS", slot), kuh + (0, "uh", c)], self.banks[b][:, 0:T])
            P.op("act", (lambda e, b=b, c=c: e.activation(out=C3[:, c, :], in_=self.banks[b][:, 0:T], func=AF.Identity, bias=self.pv("cf_dw_b", c), scale=1.0)),
                 reads=[("bank", b), ("pvec",)], writes=[self.kf(3, c, T)])
        Cf = self.ff(3, T)
        CBf = self.bf(0, 1, T)
        CB3 = self.b3(0, 1, T)
        XC3 = self.f3(4, T)
        XCf = self.ff(4, T)
        P.op("act", lambda e: e.copy(out=CBf, in_=Cf), reads=[("U", 3)], writes=[("U", 0, 1)])
        bm = self.bank()
        for c in range(8):
            self.mm(bm, self.ones_bf[:], CB3[:, c, :], c == 0, c == 7, [("ones",), ("U", 0, 1)], self.banks[bm][:, 0:T])
        mb = self.banks[bm][:, 0:T].unsqueeze(1).to_broadcast([128, 8, T])
        P.op("dve", lambda e: e.tensor_tensor(out=XC3, in0=C3, in1=mb, op=ALU.subtract), reads=[("U", 3), ("bank", bm)], writes=[("U", 4)])
        P.op("act", lambda e: e.activation(out=CBf, in_=XCf, func=AF.Square), reads=[("U", 4)], writes=[("U", 0, 1)])
        bv = self.bank()
        for c in range(8):
            self.mm(bv, self.ones_bf[:], CB3[:, c, :], c == 0, c == 7, [("ones",), ("U", 0, 1)], self.banks[bv][:, 0:T])
        t, tk = self.tmp()
        P.op("act", lambda e: e.activation(out=t[:, 0:T], in_=self.banks[bv][:, 0:T], func=AF.Sqrt, bias=self.eps_t[:, 0:1], scale=1.0),
             reads=[("bank", bv), ("eps",)], writes=[tk])
        P.op("dve", lambda e: e.reciprocal(out=self.RSTD[:, 0:T], in_=t[:, 0:T]), reads=[tk], writes=[("RSTD",)])
        CS3 = self.b3(2, 1, T)
        for c in range(8):
            P.op("dve", (lambda e, c=c: e.scalar_tensor_tensor(out=XC3[:, c, :], in0=XC3[:, c, :], scalar=self.pv("cf_ln_g", c), in1=self.RSTD[:, 0:T],
                                                             op0=ALU.mult, op1=ALU.mult)),
                 reads=[self.kf(4, c, T), ("RSTD",), ("pvec",)], writes=[self.kf(4, c, T)])
            P.op("act", (lambda e, c=c: e.activation(out=CS3[:, c, :], in_=XC3[:, c, :], func=AF.Silu, bias=self.pv("cf_ln_b", c), scale=1.0)),
                 reads=[self.kf(4, c, T), ("pvec",)], writes=[self.kb(2, 1, c, T)])
        X3 = self.x3(T)

        def evac_o(jc, b):
            P.op("dve", (lambda e: e.scalar_tensor_tensor(out=X3[:, jc, :], in0=self.banks[b][:, 0:T], scalar=self.pv("cf_b2", jc), in1=X3[:, jc, :],
                                                          op0=ALU.add, op1=ALU.add)),
                 reads=[("bank", b), self.kx(jc, T), ("pvec",)], writes=[self.kx(jc, T)])
        self.proj_fm("cf_w2", j, CS3, lambda k: self.kb(2, 1, k, T), T, evac_o)


def run(inputs, Lp=SEQ, mixers=(True, True, True, True), depth=4, trace=False):
    bld = Builder(Lp, mixers=mixers, depth=depth)
    nc = bld.build()
    f = lambda k: np.asarray(inputs[k], np.float32)
    pvec = pack_pvec(inputs)
    cst = make_cst()
    xp, xs = f("x_prompt"), f("x_sample")
    nb = xp.shape[0]
    zeros_p = np.zeros((Lp, D), np.float32)
    in_maps = []
    for c in range(N_CORES):
        m = {"pvec": pvec, "cst": cst}
        m["xp"] = np.ascontiguousarray(xp[c, :Lp]) if c < nb else zeros_p
        m["xs"] = np.ascontiguousarray(xs[2 * c:2 * c + 2].reshape(2 * DEC_SEQ, D))
        m["st_hg"] = np.ascontiguousarray(f("state_hgrn")[0, 2 * c:2 * c + 2])
        m["st_rw"] = np.stack([_rw_in(f("state_rwkv")[0, q]) for q in (2 * c, 2 * c + 1)])
        m["st_sh"] = np.stack([_cols(f("state_rwkv_shift")[0, q]) for q in (2 * c, 2 * c + 1)])
        m["st_lh"] = np.stack([_cols(f("state_lru")[0, q]) for q in (2 * c, 2 * c + 1)])
        m["st_lc"] = np.stack([_rows_to_pcr(f("state_lru_conv")[0, q]) for q in (2 * c, 2 * c + 1)])
        m["st_cf"] = np.stack([_rows_to_pcr(f("state_conf_conv")[0, q]) for q in (2 * c, 2 * c + 1)])
        for name, K, N, cnt in WEIGHTS:
            m[name] = np.ascontiguousarray(f(name)[:cnt]).reshape(cnt, K, N)
        in_maps.append(m)
    res = run_bass_kernel_spmd(nc, in_maps, core_ids=list(range(N_CORES)), **({"trace": True} if trace else {}))
    rs = res.results
    y_prompt = np.stack([rs[c]["yp"] for c in range(nb)], axis=0)
    y_sample = np.concatenate([rs[c]["ys"].reshape(2, DEC_SEQ, D) for c in range(N_CORES)], axis=0)
    p_hg = np.stack([rs[c]["o_hg_p"][0] for c in range(nb)], axis=0)[None]
    s_hg = np.concatenate([rs[c]["o_hg_s"] for c in range(N_CORES)], axis=0)[None]
    p_lh = np.stack([_pc_to_vec(rs[c]["o_lh_p"][0]) for c in range(nb)], axis=0)[None]
    s_lh = np.stack([_pc_to_vec(rs[c]["o_lh_s"][q]) for c in range(N_CORES) for q in range(2)], axis=0)[None]
    p_lc = np.stack([_pcr_to_rows(rs[c]["o_lc_p"][0]) for c in range(nb)], axis=0)[None]
    s_lc = np.stack([_pcr_to_rows(rs[c]["o_lc_s"][q]) for c in range(N_CORES) for q in range(2)], axis=0)[None]
    p_cf = np.stack([_pcr_to_rows(rs[c]["o_cf_p"][0]) for c in range(nb)], axis=0)[None]
    s_cf = np.stack([_pcr_to_rows(rs[c]["o_cf_s"][q]) for c in range(N_CORES) for q in range(2)], axis=0)[None]
    p_rw = np.stack([_rw_out(rs[c]["o_rw_p"][0]) for c in range(nb)], axis=0)[None]
    s_rw = np.stack([_rw_out(rs[c]["o_rw_s"][q]) for c in range(N_CORES) for q in range(2)], axis=0)[None]
    p_sh = np.stack([_pc_to_vec(rs[c]["o_sh_p"][0]) for c in range(nb)], axis=0)[None]
    s_sh = np.stack([_pc_to_vec(rs[c]["o_sh_s"][q]) for c in range(N_CORES) for q in range(2)], axis=0)[None]
    outs = [y_prompt, y_sample, p_hg, p_rw, p_sh, p_lh, p_lc, p_cf, s_hg, s_rw, s_sh, s_lh, s_lc, s_cf]
    return tuple(outs), res


def kernel(**inputs):
    outs, _ = run(inputs)
    return outs
```
